# Optimizing a Trainium2 kernel written in Bass

```python
import math
import jax, jax.numpy as jnp
from jax import lax
import numpy as np

D_MODEL = 2048
BATCH = 4
SEQ = 2048
DEPTH = 1

HEAD_DIM = 128
MIX_WIDTH = D_MODEL
A_HEADS = MIX_WIDTH // (2 * HEAD_DIM)
B_HEADS = MIX_WIDTH // (2 * HEAD_DIM)
A_WIDTH = A_HEADS * HEAD_DIM
B_WIDTH = B_HEADS * HEAD_DIM
DIFF_DIM = HEAD_DIM // 2
IN_WIDTH = 3 * A_WIDTH + 3 * B_WIDTH
DILATED_CONFIGS = ((128, 1), (512, 4), (2048, 16))
ROPE_THETA = 500000.0
ROPE_FRACTION = 4
D_FF = ((-(-8 * D_MODEL // 3) + 255) // 256) * 256
Q_BLOCK = 128
RMS_EPS = 1e-6
SUBLN_EPS = 1e-5
NEG_INF = -1e30

kernel_name = 'hybrid_dilated_diff_attn_encoder_block'


def rmsnorm(x, g, eps=RMS_EPS):
    xf = x.astype(jnp.float32)
    y = xf * lax.rsqrt(jnp.mean(xf * xf, axis=-1, keepdims=True) + eps)
    return (y * g.astype(jnp.float32)).astype(x.dtype)


def rope_tables(seq, rot_dim):
    inv_freq = ROPE_THETA ** (-jnp.arange(0, rot_dim, 2, dtype=jnp.float32) / rot_dim)
    ang = jnp.arange(seq, dtype=jnp.float32)[:, None] * inv_freq[None, :]
    return jnp.cos(ang), jnp.sin(ang)


def rope(x, cos, sin):
    r2 = cos.shape[-1]
    shape = (x.shape[1],) + (1,) * (x.ndim - 3) + (r2,)
    c, s = cos.reshape(shape), sin.reshape(shape)
    x1, x2 = x[..., :r2], x[..., r2:2 * r2]
    return jnp.concatenate([x1 * c - x2 * s, x2 * c + x1 * s, x[..., 2 * r2:]], axis=-1).astype(x.dtype)


def dilated_branch(q, k, v, dil, half):
    B, S, H, Dh = q.shape
    L = S // dil
    nb = -(-L // half)
    Lp = nb * half

    def by_residue(a):
        return a.reshape(B, L, dil, H, Dh).transpose(0, 2, 3, 1, 4)

    qs = jnp.pad(by_residue(q), ((0, 0), (0, 0), (0, 0), (0, Lp - L), (0, 0))).reshape(B, dil, H, nb, half, Dh)
    kpad = ((0, 0), (0, 0), (0, 0), (half, Lp - L + half), (0, 0))
    kp = jnp.pad(by_residue(k), kpad).reshape(B, dil, H, nb + 2, half, Dh)
    vp = jnp.pad(by_residue(v), kpad).reshape(B, dil, H, nb + 2, half, Dh)

    def band(a):
        return jnp.concatenate([a[:, :, :, :-2], a[:, :, :, 1:-1], a[:, :, :, 2:]], axis=4)

    kb, vb = band(kp), band(vp)
    s = jnp.einsum('brhnqe,brhnke->brhnqk', qs, kb, preferred_element_type=jnp.float32) * (Dh ** -0.5)
    blk = jnp.arange(nb)[:, None, None]
    qa = jnp.arange(half)[None, :, None]
    kj = jnp.arange(3 * half)[None, None, :]
    kpos = blk * half - half + kj
    dist = kj - half - qa
    mask = (jnp.abs(dist) <= half) & (kpos >= 0) & (kpos < L)
    s = jnp.where(mask, s, NEG_INF)
    m = jnp.max(s, axis=-1, keepdims=True)
    p = jnp.exp(s - m)
    den = jnp.sum(p, axis=-1)
    o = jnp.einsum('brhnqk,brhnke->brhnqe', p, vb.astype(jnp.float32)) / den[..., None]
    lse = m[..., 0] + jnp.log(den)
    o = o.reshape(B, dil, H, Lp, Dh)[:, :, :, :L].transpose(0, 3, 1, 2, 4).reshape(B, S, H, Dh)
    lse = lse.reshape(B, dil, H, Lp)[:, :, :, :L].transpose(0, 3, 1, 2).reshape(B, S, H)
    return o, lse


def dilated_attention(q, k, v):
    outs, lses = [], []
    for window, dil in DILATED_CONFIGS:
        o, lse = dilated_branch(q, k, v, dil, window // (2 * dil))
        outs.append(o)
        lses.append(lse)
    w = jax.nn.softmax(jnp.stack(lses, axis=0), axis=0)
    return jnp.einsum('gbsh,gbshe->bshe', w, jnp.stack(outs, axis=0))


def diff_attention(q, k, v, lam):
    B, S, H, _, d = q.shape
    nqb = S // Q_BLOCK
    qb = q.reshape(B, nqb, Q_BLOCK, H, 2, d).transpose(1, 0, 3, 4, 2, 5)
    kt = k.transpose(0, 2, 3, 1, 4)
    vt = v.transpose(0, 2, 1, 3)

    def one_block(qblk):
        s = jnp.einsum('bhcqe,bhcke->bhcqk', qblk, kt, preferred_element_type=jnp.float32) * (d ** -0.5)
        p = jax.nn.softmax(s, axis=-1)
        a = p[:, :, 0] - lam * p[:, :, 1]
        return jnp.einsum('bhqk,bhke->bhqe', a, vt.astype(jnp.float32))

    o = lax.map(one_block, qb)
    return o.transpose(1, 0, 3, 2, 4).reshape(B, S, H, 2 * d)


def setup_inputs(seed: int = 0) -> dict:
    key = jax.random.key(seed)
    ks = jax.random.split(key, 11)
    f32 = jnp.float32

    def dense(k, shape):
        return jax.random.normal(k, shape, f32) * (shape[-2] ** -0.5)

    def gain(k, shape):
        return 1.0 + 0.02 * jax.random.normal(k, shape, f32)

    return {
        'x': jax.random.normal(ks[0], (BATCH, SEQ, D_MODEL), f32),
        'norm_attn': gain(ks[1], (DEPTH, D_MODEL)),
        'w_in': dense(ks[2], (DEPTH, D_MODEL, IN_WIDTH)),
        'lambda_qk': 0.1 * jax.random.normal(ks[3], (DEPTH, 4, DIFF_DIM), f32),
        'subln': gain(ks[4], (DEPTH, 2 * DIFF_DIM)),
        'w_out': dense(ks[5], (DEPTH, MIX_WIDTH, D_MODEL)),
        'norm_ffn': gain(ks[6], (DEPTH, D_MODEL)),
        'w_gate': dense(ks[7], (DEPTH, D_MODEL, D_FF)),
        'w_up': dense(ks[8], (DEPTH, D_MODEL, D_FF)),
        'w_down': dense(ks[9], (DEPTH, D_FF, D_MODEL)),
        'norm_final': gain(ks[10], (D_MODEL,)),
    }


def reference(x, norm_attn, w_in, lambda_qk, subln, w_out, norm_ffn, w_gate, w_up, w_down, norm_final):
    B, S, _ = x.shape
    cos_a, sin_a = rope_tables(S, HEAD_DIM // ROPE_FRACTION)
    cos_b, sin_b = rope_tables(S, DIFF_DIM // ROPE_FRACTION)
    splits = [A_WIDTH, 2 * A_WIDTH, 3 * A_WIDTH, 3 * A_WIDTH + B_WIDTH, 3 * A_WIDTH + 2 * B_WIDTH]
    for l in range(DEPTH):
        h = rmsnorm(x, norm_attn[l])
        proj = h @ w_in[l]
        qa, ka, va, qb, kb, vb = jnp.split(proj, splits, axis=-1)
        qa = rope(qa.reshape(B, S, A_HEADS, HEAD_DIM), cos_a, sin_a)
        ka = rope(ka.reshape(B, S, A_HEADS, HEAD_DIM), cos_a, sin_a)
        va = va.reshape(B, S, A_HEADS, HEAD_DIM)
        ya = dilated_attention(qa, ka, va).astype(x.dtype).reshape(B, S, A_WIDTH)
        lam_init = 0.8 - 0.6 * math.exp(-0.3 * l)
        lq = lambda_qk[l].astype(jnp.float32)
        lam = jnp.exp(jnp.sum(lq[0] * lq[1])) - jnp.exp(jnp.sum(lq[2] * lq[3])) + lam_init
        qb = rope(qb.reshape(B, S, B_HEADS, 2, DIFF_DIM), cos_b, sin_b)
        kb = rope(kb.reshape(B, S, B_HEADS, 2, DIFF_DIM), cos_b, sin_b)
        vb = vb.reshape(B, S, B_HEADS, 2 * DIFF_DIM)
        yb = diff_attention(qb, kb, vb, lam)
        yb = (rmsnorm(yb, subln[l], SUBLN_EPS) * (1.0 - lam_init)).astype(x.dtype).reshape(B, S, B_WIDTH)
        x = x + jnp.concatenate([ya, yb], axis=-1) @ w_out[l]
        h = rmsnorm(x, norm_ffn[l])
        x = x + (jax.nn.silu(h @ w_gate[l]) * (h @ w_up[l])) @ w_down[l]
    return rmsnorm(x, norm_final)
```

```python
from contextlib import ExitStack
import math
import numpy as np
import ml_dtypes
import concourse.bass as bass
import concourse.mybir as mybir
from concourse.bass_utils import run_bass_kernel_spmd

F32 = mybir.dt.float32
BF16 = mybir.dt.bfloat16
AF = mybir.ActivationFunctionType
ALU = mybir.AluOpType
AX = mybir.AxisListType

D = 2048
SEQ = 2048
NT = 16
NQ = 8
DFF = 5632
NF = 44
LAM_INIT = 0.8 - 0.6 * math.exp(0.0)
ROPE_THETA = 500000.0


class Tok:
    __slots__ = ("sem", "val", "eng")

    def __init__(self, sem, val, eng):
        self.sem = sem
        self.val = val
        self.eng = eng


class Buf:
    __slots__ = ("name", "w", "rs", "excl")

    def __init__(self, name, excl=False):
        self.name = name
        self.w = None
        self.rs = {}
        self.excl = excl


class Sched:
    def __init__(self, nc, stack):
        self.nc = nc
        self.stack = stack
        self.E = {}
        for name, eng in (("pe", nc.tensor), ("act", nc.scalar), ("dve", nc.vector),
                          ("pool", nc.gpsimd), ("sp", nc.sync)):
            sem = stack.enter_context(nc.semaphore("s_" + name))
            self.E[name] = dict(eng=eng, sem=sem, cnt=0, waited={}, name=name)

    def new_sem(self, name):
        return self.stack.enter_context(self.nc.semaphore(name))

    def slot(self, name):
        return dict(sem=self.new_sem(name), cnt=0)

    def _wait(self, E, reads, writes):
        need = {}

        def add(tok, raw):
            if tok is None:
                return
            if tok.eng == E["name"]:
                if E["name"] in ("pe", "sp", "pool"):
                    return
            k = tok.sem.num
            if k not in need or need[k][1] < tok.val:
                need[k] = (tok.sem, tok.val, tok.eng)

        for b in reads:
            add(b.w, True)
            if b.excl:
                for t in b.rs.values():
                    if t.eng != E["name"]:
                        add(t, False)
        for b in writes:
            add(b.w, False)
            for t in b.rs.values():
                add(t, False)
        for k, (sem, val, en) in need.items():
            if E["waited"].get(k, 0) >= val:
                continue
            if en in self.E:
                assert self.E[en]["cnt"] >= val, f"wait on unflagged {en} {val}>{self.E[en]['cnt']}"
            E["eng"].wait_ge(sem, val)
            E["waited"][k] = val

    def op(self, en, fn, reads=(), writes=(), sig=True):
        E = self.E[en]
        self._wait(E, reads, writes)
        ins = fn(E["eng"])
        if en == "pe" and not sig:
            tok = Tok(E["sem"], E["cnt"] + 1, en)
        else:
            ins.then_inc(E["sem"], 1)
            E["cnt"] += 1
            tok = Tok(E["sem"], E["cnt"], en)
        for b in reads:
            b.rs[en] = tok
        for b in writes:
            b.w = tok
            b.rs = {}
        return tok

    def dma(self, en, slot, out, in_, reads=(), writes=()):
        E = self.E[en]
        self._wait(E, reads, writes)
        ins = E["eng"].dma_start(out=out, in_=in_)
        ins.then_inc(slot["sem"], 16)
        slot["cnt"] += 16
        tok = Tok(slot["sem"], slot["cnt"], "dma")
        for b in reads:
            b.rs["dma%d" % slot["sem"].num] = tok
        for b in writes:
            b.w = tok
            b.rs = {}
        return tok

    def wait_tok(self, en, tok):
        E = self.E[en]
        if E["waited"].get(tok.sem.num, 0) < tok.val:
            E["eng"].wait_ge(tok.sem, tok.val)
            E["waited"][tok.sem.num] = tok.val

    def barrier(self, engines=("pe", "act", "dve", "sp")):
        for en in engines:
            for e2 in ("pe", "act", "dve"):
                if e2 == en:
                    continue
                E2 = self.E[e2]
                if E2["cnt"] > 0:
                    self.wait_tok(en, Tok(E2["sem"], E2["cnt"], e2))


def bcast(ap, n, pos):
    l = [list(x) for x in ap.ap]
    l.insert(pos, [0, n])
    return bass.AP(ap.tensor, ap.offset, l)


import os
STOP = int(os.environ.get("K_STOP", "99"))


def build_program():
    nc = bass.Bass("TRN2", target_bir_lowering=False)
    dt_in = lambda name, shape, dt=F32: nc.dram_tensor(name, shape, dt, kind="ExternalInput").ap()
    xb_d = dt_in("xb", [SEQ, D])
    g1_d = dt_in("g1", [128, D])
    g2_d = dt_in("g2", [128, D])
    g3_d = dt_in("g3", [128, D])
    w_in_d = dt_in("w_in", [D, 6144])
    w_out_d = dt_in("w_out", [D, D])
    w_gate_d = dt_in("w_gate", [D, DFF])
    w_up_d = dt_in("w_up", [D, DFF])
    w_down_d = dt_in("w_down", [DFF, D])
    cosA_d = dt_in("cosA", [128, NT, 16])
    sinA_d = dt_in("sinA", [128, NT, 16])
    cosB_d = dt_in("cosB", [128, NT, 8])
    sinB_d = dt_in("sinB", [128, NT, 8])
    mown_d = dt_in("mown", [128, 1920], BF16)
    moth_d = dt_in("moth", [128, 1920], BF16)
    lamq_d = dt_in("lamq", [128, 256])
    subl_d = dt_in("subl", [128, 128])
    ident_d = dt_in("ident", [128, 128])
    out_d = nc.dram_tensor("out", [NQ * 128, D], F32, kind="ExternalOutput").ap()

    xb_t = xb_d.rearrange("(t p) d -> t p d", p=128)
    out_t = out_d.rearrange("(t p) d -> t p d", p=128)
    w_in_v = w_in_d.rearrange("(c p) n -> p c n", p=128)
    w_out_v = w_out_d.rearrange("(c p) n -> p c n", p=128)
    w_gate_v = w_gate_d.rearrange("(c p) n -> p c n", p=128)
    w_up_v = w_up_d.rearrange("(c p) n -> p c n", p=128)
    w_down_v = w_down_d.rearrange("(f p) n -> p f n", p=128)

    with ExitStack() as st:
        S = Sched(nc, st)
        ARENA_BYTES = 206 * 1024
        arena = st.enter_context(nc.sbuf_tensor("arena", [128, ARENA_BYTES // 2], BF16))

        def view(off, shape, dt):
            n = 1
            for s_ in shape:
                n *= s_
            esz = 4 if dt == F32 else 2
            assert off % 4 == 0 and off + n * esz <= ARENA_BYTES, (off, shape)
            ap = arena[:, off // 2:(off + n * esz) // 2]
            if dt == F32:
                ap = ap.bitcast(F32)
            if len(shape) == 2:
                ap = ap.rearrange("p (a b) -> p a b", b=shape[1])
            elif len(shape) == 3:
                ap = ap.rearrange("p (a b c) -> p a b c", b=shape[1], c=shape[2])
            elif len(shape) == 4:
                ap = ap.rearrange("p (a b c d) -> p a b c d", b=shape[1], c=shape[2], d=shape[3])
            return ap

        class Bump:
            def __init__(self, base, size):
                self.base, self.size, self.off = base, size, 0

            def reset(self):
                self.off = 0

            def alloc(self, shape, dt):
                n = 1
                for s_ in shape:
                    n *= s_
                sz = (n * (4 if dt == F32 else 2) + 3) // 4 * 4
                assert self.off + sz <= self.size, ("region overflow", self.off, sz, self.size)
                v = view(self.base + self.off, shape, dt)
                self.off += sz
                return v

        o = 0
        R_P = Bump(o, 4096); o += 4096
        R_A = Bump(o, 65536); o += 65536
        R_W = Bump(o, 32768); o += 32768
        R_Y = Bump(o, 32768); o += 32768
        R_S = Bump(o, ARENA_BYTES - o)
        assert R_S.size >= 73000, R_S.size

        psb = [st.enter_context(nc.psum_tensor(f"psb{i}", [128, 512], F32)) for i in range(8)]
        Bps = [Buf(f"ps{i}", excl=True) for i in range(8)]
        ps3b = psb[3][:].bitcast(BF16)
        Bps3 = [Bps[3], Bps[3]]

        identf = R_P.alloc([128], F32)
        identb = R_P.alloc([128], BF16)
        epsA = R_P.alloc([1], F32)
        epsB = R_P.alloc([1], F32)
        lamq = R_P.alloc([256], F32)
        lprod = R_P.alloc([2, 64], F32)
        ldots = R_P.alloc([2], F32)
        lex = R_P.alloc([2], F32)
        neglam = R_P.alloc([1], F32)
        subl = R_P.alloc([128], F32)
        ss1 = R_P.alloc([NT], F32)
        rs1 = R_P.alloc([NT], F32)
        ss2 = R_P.alloc([NQ], F32)
        rs2 = R_P.alloc([NQ], F32)
        ss3 = R_P.alloc([NQ], F32)
        rs3 = R_P.alloc([NQ], F32)
        Bconst = Buf("const")
        Bss1 = [Buf(f"ss1_{t}") for t in range(NT)]
        Bss2 = [Buf(f"ss2_{t}") for t in range(NQ)]
        Bss3 = [Buf(f"ss3_{t}") for t in range(NQ)]

        wslot = [view(R_W.base + i * 16384, [16, 512], BF16) for i in range(2)]
        Bw = [Buf("w0"), Buf("w1")]
        sl_w = [S.slot("ldw0"), S.slot("ldw1")]
        wctr = [0]

        def load_unit(src_ap, nchunks=16):
            i = wctr[0] % 2
            wctr[0] += 1
            S.dma("pool", sl_w[i], wslot[i][:, 0:nchunks, :], src_ap, writes=[Bw[i]])
            return i

        sl_c = S.slot("ldc")
        sl_g = S.slot("ldg")
        sl_t = S.slot("ldt")
        sl_x1 = S.slot("ldxres")
        sl_x = [S.slot("ldx0"), S.slot("ldx1")]
        sl_o = S.slot("sto")

        def finish_early():
            S.barrier()
            tkk = S.dma("sp", sl_o, out_t[0], view(R_A.base, [D], F32))
            S.wait_tok("sp", tkk)
            return nc

        S.dma("sp", sl_c, identf, ident_d[:, :], writes=[Bconst])
        S.dma("sp", sl_c, lamq, lamq_d[:, :], writes=[Bconst])
        tk = S.dma("sp", sl_c, subl, subl_d[:, :], writes=[Bconst])
        S.op("dve", lambda e: e.tensor_copy(identb, identf), reads=[Bconst], writes=[Bconst])
        S.op("dve", lambda e: e.memset(epsA, 1e-6), writes=[Bconst])
        S.op("dve", lambda e: e.memset(epsB, 1e-5), writes=[Bconst])
        for tl in (ss1, ss2, ss3):
            S.op("dve", lambda e: e.memset(tl, 0.0), writes=[Bconst])
        lqv = lamq.rearrange("p (a b d) -> p a b d", a=2, b=2, d=64)
        S.op("dve", lambda e: e.tensor_tensor(lprod, lqv[:, :, 0, :], lqv[:, :, 1, :], ALU.mult), reads=[Bconst], writes=[Bconst])
        S.op("dve", lambda e: e.reduce_sum(ldots, lprod, axis=AX.X), reads=[Bconst], writes=[Bconst])
        S.op("act", lambda e: e.activation(lex, ldots, AF.Exp), reads=[Bconst], writes=[Bconst])
        S.op("dve", lambda e: e.tensor_tensor(neglam, lex[:, 1:2], lex[:, 0:1], ALU.subtract), reads=[Bconst], writes=[Bconst])
        S.op("dve", lambda e: e.tensor_scalar(neglam, neglam, -LAM_INIT, None, ALU.add), reads=[Bconst], writes=[Bconst])

        def norm_to_T(t, src, Bsrc, gfull, Bg, ss, rs, Bss, xs, Bxs, junk, Bjunk, dstT, BdstT, par):
            S.op("act", lambda e: e.activation(junk, src, AF.Square, accum_out=ss[:, t:t + 1]),
                 reads=[Bsrc, Bconst], writes=[Bjunk, Bss])
            S.op("act", lambda e: e.activation(rs[:, t:t + 1], ss[:, t:t + 1], AF.Sqrt, scale=1.0 / D, bias=epsA[:, 0:1]),
                 reads=[Bss, Bconst], writes=[Bss])
            S.op("dve", lambda e: e.reciprocal(rs[:, t:t + 1], rs[:, t:t + 1]), reads=[Bss], writes=[Bss])
            S.op("dve", lambda e: e.scalar_tensor_tensor(xs, src, rs[:, t:t + 1], gfull, ALU.mult, ALU.mult),
                 reads=[Bsrc, Bss, Bg], writes=[Bxs])
            for hb in range(2):
                bk = 2 * par + hb
                pv = psb[bk][:].bitcast(BF16)
                for c8 in range(8):
                    c = hb * 8 + c8
                    S.op("pe", lambda e: e.transpose(pv[:, c8 * 128:(c8 + 1) * 128], xs[:, c * 128:(c + 1) * 128], identb),
                         reads=[Bxs, Bconst], writes=[Bps[bk]], sig=(c8 == 7))
                dst = dstT[:, hb * 8:hb * 8 + 8, t * 128:(t + 1) * 128]
                srcv = pv.rearrange("p (a b) -> p a b", b=128)
                if hb == 0:
                    S.op("act", lambda e: e.copy(dst, srcv), reads=[Bps[bk]], writes=[BdstT])
                else:
                    S.op("dve", lambda e: e.tensor_copy(dst, srcv), reads=[Bps[bk]], writes=[BdstT])

        hT = R_A.alloc([16, SEQ], BF16)
        BhT = [Buf(f"hT{t}") for t in range(NT)]
        R_S.reset()
        xin = [R_S.alloc([D], F32) for _ in range(2)]
        xs_b = [R_S.alloc([D], BF16) for _ in range(2)]
        gfull = R_S.alloc([D], F32)
        junk = R_S.alloc([D], BF16)
        Bxin = [Buf("xin0"), Buf("xin1")]
        Bxs = [Buf("xs0"), Buf("xs1")]
        Bg = Buf("gfull")
        Bjunk = Buf("junk")
        S.dma("sp", sl_g, gfull, g1_d[:, :], writes=[Bg])
        GROUPS = [("A", 0), ("A", 1), ("B", 0), ("B", 1)]

        def in_units(G):
            typ, gi = GROUPS[G]
            base = 0 if typ == "A" else 6
            return dict(q=base + gi, k=base + 2 + gi, v=base + 4 + gi)

        def in_unit_ap(n):
            return w_in_v[:, :, n * 512:(n + 1) * 512]

        pending_units = {}
        pending_units[(0, "k")] = load_unit(in_unit_ap(in_units(0)["k"]))
        pending_units[(0, "v")] = load_unit(in_unit_ap(in_units(0)["v"]))

        for t in range(NT):
            S.dma("sp", sl_x[t % 2], xin[t % 2], xb_t[t], writes=[Bxin[t % 2]])
            norm_to_T(t, xin[t % 2], Bxin[t % 2], gfull, Bg, ss1, rs1, Bss1[t], xs_b[t % 2], Bxs[t % 2],
                      junk, Bjunk, hT, BhT[t], t % 2)
        S.barrier()
        if STOP == 1:
            return finish_early()

        R_S.reset()
        QT = R_S.alloc([2, 4, NQ * 128], BF16)
        KT = R_S.alloc([4, SEQ], BF16)
        Vaug = R_S.alloc([NT, 4, 130], BF16)
        mown = R_S.alloc([1920], BF16)
        moth = R_S.alloc([1920], BF16)
        cosA = R_S.alloc([NT, 16], F32)
        sinA = R_S.alloc([NT, 16], F32)
        cosB = R_S.alloc([NT, 8], F32)
        sinB = R_S.alloc([NT, 8], F32)
        tm = [R_S.alloc([512], BF16) for _ in range(2)]
        NPT = 3
        pt = [R_S.alloc([512], BF16) for _ in range(NPT)]
        rt1 = R_S.alloc([128], F32)
        rt2 = R_S.alloc([128], F32)
        rsrc = R_S.alloc([128], F32)
        rden = R_S.alloc([4], F32)
        o1n = R_S.alloc([4, 128], F32)
        yb = R_S.alloc([4, 128], F32)
        ysq = R_S.alloc([4, 128], F32)
        ssq = R_S.alloc([4], F32)
        ytm = [R_S.alloc([4, 128], BF16) for _ in range(2)]
        yT = R_Y.alloc([16, NQ * 128], BF16)
        BQT = [Buf(f"QT{t}") for t in range(NQ)]
        BKT = [Buf(f"KT{t}") for t in range(NT)]
        BV = [Buf(f"V{t}") for t in range(NT)]
        Btab = Buf("tables")
        Btm = [Buf("tm0"), Buf("tm1")]
        Bpt = [Buf(f"pt{i}") for i in range(NPT)]
        Brt = Buf("rt")
        Brs = Buf("rsrc")
        Bep = Buf("ep")
        Bo1n = Buf("o1n")
        Bytm = [Buf("ytm0"), Buf("ytm1")]
        ByT = [Buf(f"yT{q}") for q in range(2)]
        S.dma("sp", sl_t, mown, mown_d[:, :], writes=[Btab])
        S.dma("sp", sl_t, moth, moth_d[:, :], writes=[Btab])
        S.dma("sp", sl_t, cosA, cosA_d[:, :, :], writes=[Btab])
        S.dma("sp", sl_t, sinA, sinA_d[:, :, :], writes=[Btab])
        S.dma("sp", sl_t, cosB, cosB_d[:, :, :], writes=[Btab])
        S.dma("sp", sl_t, sinB, sinB_d[:, :, :], writes=[Btab])
        S.op("dve", lambda e: e.memset(QT, 0.0), writes=BQT)
        S.op("dve", lambda e: e.memset(Vaug[:, :, :, 128:130], 1.0), writes=BV)

        if STOP == 10:
            return finish_early()
        pbank = [0]

        def next_pbank():
            b = pbank[0] % 3
            pbank[0] += 1
            return b

        trh = [0]

        def proj(bk, t, wi):
            for c in range(16):
                S.op("pe", lambda e: e.matmul(psb[bk][:], hT[:, c, t * 128:(t + 1) * 128], wslot[wi][:, c, :],
                                              start=(c == 0), stop=(c == 15)),
                     reads=[BhT[t], Bw[wi]], writes=[Bps[bk]], sig=(c == 15))

        def rope_evac(bk, t, typ, dst, Bdst):
            if typ == "A":
                nh, hd, r2, ct, stb = 4, 128, 16, cosA, sinA
            else:
                nh, hd, r2, ct, stb = 8, 64, 8, cosB, sinB
            src = psb[bk][:].rearrange("p (h d) -> p h d", d=hd)
            dv = dst.rearrange("p (h d) -> p h d", d=hd)
            rs_ = rsrc[:, 0:nh * 2 * r2].rearrange("p (h d) -> p h d", d=2 * r2)
            t1 = rt1[:, 0:nh * 2 * r2].rearrange("p (h d) -> p h d", d=2 * r2)
            t2 = rt2[:, 0:nh * 2 * r2].rearrange("p (h d) -> p h d", d=2 * r2)
            cb = bcast(ct[:, t, :], nh, 1)
            sb_ = bcast(stb[:, t, :], nh, 1)
            S.op("act", lambda e: e.copy(dv[:, :, 2 * r2:hd], src[:, :, 2 * r2:hd]), reads=[Bps[bk]], writes=[Bdst])
            S.op("act", lambda e: e.copy(rs_, src[:, :, 0:2 * r2]), reads=[Bps[bk]], writes=[Brs])
            if STOP == 16:
                return
            S.op("dve", lambda e: e.tensor_tensor(t1[:, :, 0:r2], rs_[:, :, 0:r2], cb, ALU.mult), reads=[Brs, Btab], writes=[Brt])
            if STOP == 17:
                return
            S.op("dve", lambda e: e.tensor_tensor(t1[:, :, r2:2 * r2], rs_[:, :, r2:2 * r2], cb, ALU.mult), reads=[Brs, Btab], writes=[Brt])
            S.op("dve", lambda e: e.tensor_tensor(t2[:, :, 0:r2], rs_[:, :, r2:2 * r2], sb_, ALU.mult), reads=[Brs, Btab], writes=[Brt])
            S.op("dve", lambda e: e.tensor_tensor(t2[:, :, r2:2 * r2], rs_[:, :, 0:r2], sb_, ALU.mult), reads=[Brs, Btab], writes=[Brt])
            S.op("dve", lambda e: e.tensor_tensor(dv[:, :, 0:r2], t1[:, :, 0:r2], t2[:, :, 0:r2], ALU.subtract), reads=[Brt], writes=[Bdst])
            S.op("dve", lambda e: e.tensor_tensor(dv[:, :, r2:2 * r2], t1[:, :, r2:2 * r2], t2[:, :, r2:2 * r2], ALU.add), reads=[Brt], writes=[Bdst])

        def transpose4(srcap, Bsrc):
            hf = trh[0] % 2
            trh[0] += 1
            for h in range(4):
                S.op("pe", lambda e: e.transpose(ps3b[:, hf * 512 + h * 128: hf * 512 + (h + 1) * 128],
                                                 srcap[:, h * 128:(h + 1) * 128], identb),
                     reads=[Bsrc, Bconst], writes=[Bps3[hf]], sig=(h == 3))
            return hf

        scaleA = 128.0 ** -0.5
        scaleB = 64.0 ** -0.5
        oset_ctr = [0]

        for G in range(4):
            typ, gi = GROUPS[G]
            U = in_units(G)
            if G == 2:
                S.op("dve", lambda e: e.memset(QT, 0.0), writes=BQT)
            wi_k = pending_units.pop((G, "k"))
            wi_v = pending_units.pop((G, "v"))
            for t in range(NT):
                bk = next_pbank()
                proj(bk, t, wi_k)
                tmi = t % 2
                if STOP == 14:
                    continue
                rope_evac(bk, t, typ, tm[tmi], Btm[tmi])
                if STOP in (15, 16, 17):
                    continue
                hf = transpose4(tm[tmi], Btm[tmi])
                S.op("act", lambda e: e.copy(KT[:, :, t * 128:(t + 1) * 128],
                                             ps3b[:, hf * 512:(hf + 1) * 512].rearrange("p (h k) -> p h k", k=128)),
                     reads=[Bps3[hf]], writes=[BKT[t]])
            if STOP in (11, 14, 15, 16, 17):
                return finish_early()
            wi_q = load_unit(in_unit_ap(U["q"]))
            for t in range(NT):
                bk = next_pbank()
                proj(bk, t, wi_v)
                S.op("act", lambda e: e.copy(Vaug[:, t, :, 0:128], psb[bk][:].rearrange("p (h d) -> p h d", d=128)),
                     reads=[Bps[bk]], writes=[BV[t]])
            if G + 1 < 4:
                pending_units[(G + 1, "k")] = load_unit(in_unit_ap(in_units(G + 1)["k"]))
            if STOP == 12:
                return finish_early()
            for t in range(NQ):
                bk = next_pbank()
                proj(bk, t, wi_q)
                tmi = t % 2
                rope_evac(bk, t, typ, tm[tmi], Btm[tmi])
                hf = transpose4(tm[tmi], Btm[tmi])
                pv = ps3b[:, hf * 512:(hf + 1) * 512].rearrange("p (h k) -> p h k", k=128)
                if typ == "A":
                    S.op("act", lambda e: e.copy(QT[:, 0, :, t * 128:(t + 1) * 128], pv), reads=[Bps3[hf]], writes=[BQT[t]])
                else:
                    S.op("act", lambda e: e.copy(QT[0:64, 0, :, t * 128:(t + 1) * 128], pv[0:64]), reads=[Bps3[hf]], writes=[BQT[t]])
                    S.op("act", lambda e: e.copy(QT[64:128, 1, :, t * 128:(t + 1) * 128], pv[64:128]), reads=[Bps3[hf]], writes=[BQT[t]])
            if G + 1 < 4:
                pending_units[(G + 1, "v")] = load_unit(in_unit_ap(in_units(G + 1)["v"]))
            else:
                pending_units["o0"] = load_unit(w_out_v[:, :, 0:512])
                pending_units["o1"] = load_unit(w_out_v[:, :, 512:1024])

            if STOP == 13:
                return finish_early()
            maps = [0] if typ == "A" else [0, 1]
            its = [(hh, qb, m, kt) for hh in range(4) for qb in range(2) for m in maps for kt in range(NT)]
            scale = scaleA if typ == "A" else scaleB
            state = {}

            def issue_S(i):
                hh, qb, m, kt = its[i]
                sbk = i % 3
                pi = i % NPT
                S.op("pe", lambda e: e.matmul(psb[sbk][:], KT[:, hh, kt * 128:(kt + 1) * 128],
                                              QT[:, m, hh, qb * 512:(qb + 1) * 512], start=True, stop=True),
                     reads=[BKT[kt]] + BQT[qb * 4:qb * 4 + 4], writes=[Bps[sbk]], sig=True)
                S.op("act", lambda e: e.activation(pt[pi], psb[sbk][:], AF.Exp, scale=scale), reads=[Bps[sbk]], writes=[Bpt[pi]])
                if typ == "A":
                    u = qb * 512 - 128 * kt
                    if kt < 8:
                        msk = mown[:, u + 896:u + 896 + 512]
                    else:
                        msk = moth[:, u + 1920:u + 1920 + 512]
                    S.op("dve", lambda e: e.tensor_tensor(pt[pi], pt[pi], msk, ALU.mult), reads=[Bpt[pi], Btab], writes=[Bpt[pi]])

            def issue_PV(i):
                hh, qb, m, kt = its[i]
                pi = i % NPT
                if kt == 0:
                    state["os"] = oset_ctr[0] % 2
                    oset_ctr[0] += 1
                os_ = state["os"]
                banks = (4 + 2 * os_, 5 + 2 * os_)
                for j in range(4):
                    bk = banks[j // 2]
                    col = (j % 2) * 130
                    S.op("pe", lambda e: e.matmul(psb[bk][:, col:col + 129], pt[pi][:, j * 128:(j + 1) * 128],
                                                  Vaug[:, kt, hh, 0:129], start=(kt == 0 and j % 2 == 0),
                                                  stop=(kt == NT - 1 and j % 2 == 1)),
                         reads=[Bpt[pi], BV[kt]], writes=[Bps[bk]], sig=(j == 3))
                if kt == NT - 1:
                    epilogue(hh, qb, m, banks)

            def epilogue(hh, qb, m, banks):
                head = (0 if typ == "A" else 8) + gi * 4 + hh
                ov = [psb[b][:, 0:260].rearrange("p (j c) -> p j c", c=130) for b in banks]
                for bi in range(2):
                    S.op("dve", lambda e: e.reciprocal(rden[:, 2 * bi:2 * bi + 2].rearrange("p (j o) -> p j o", o=1), ov[bi][:, :, 128:129]),
                         reads=[Bps[banks[bi]]], writes=[Bep])
                yi = oset_ctr[0] % 2
                if typ == "A":
                    for bi in range(2):
                        S.op("dve", lambda e: e.tensor_tensor(ytm[yi][:, 2 * bi:2 * bi + 2, :], ov[bi][:, :, 0:128],
                                                              bcast(rden[:, 2 * bi:2 * bi + 2], 128, 2), ALU.mult),
                             reads=[Bps[banks[bi]], Bep], writes=[Bytm[yi]])
                elif m == 0:
                    for bi in range(2):
                        S.op("dve", lambda e: e.tensor_tensor(o1n[:, 2 * bi:2 * bi + 2, :], ov[bi][:, :, 0:128],
                                                              bcast(rden[:, 2 * bi:2 * bi + 2], 128, 2), ALU.mult),
                             reads=[Bps[banks[bi]], Bep], writes=[Bo1n])
                    return
                else:
                    for bi in range(2):
                        S.op("dve", lambda e: e.tensor_tensor(yb[:, 2 * bi:2 * bi + 2, :], ov[bi][:, :, 0:128],
                                                              bcast(rden[:, 2 * bi:2 * bi + 2], 128, 2), ALU.mult),
                             reads=[Bps[banks[bi]], Bep], writes=[Bep])
                    S.op("dve", lambda e: e.scalar_tensor_tensor(yb, yb, neglam[:, 0:1], o1n, ALU.mult, ALU.add),
                         reads=[Bep, Bo1n, Bconst], writes=[Bep])
                    S.op("dve", lambda e: e.tensor_tensor(ysq, yb, yb, ALU.mult), reads=[Bep], writes=[Bep])
                    S.op("dve", lambda e: e.reduce_sum(ssq, ysq, axis=AX.X), reads=[Bep], writes=[Bep])
                    S.op("act", lambda e: e.activation(ssq, ssq, AF.Sqrt, scale=1.0 / 128.0, bias=epsB[:, 0:1]), reads=[Bep, Bconst], writes=[Bep])
                    S.op("dve", lambda e: e.reciprocal(ssq, ssq), reads=[Bep], writes=[Bep])
                    S.op("dve", lambda e: e.tensor_scalar(ssq, ssq, 1.0 - LAM_INIT, None, ALU.mult), reads=[Bep], writes=[Bep])
                    S.op("dve", lambda e: e.tensor_tensor(yb, yb, bcast(ssq, 128, 2), ALU.mult), reads=[Bep], writes=[Bep])
                    S.op("dve", lambda e: e.tensor_tensor(ytm[yi], yb, bcast(subl, 4, 1), ALU.mult), reads=[Bep, Bconst], writes=[Bytm[yi]])
                hf = transpose4(ytm[yi].rearrange("p j d -> p (j d)"), Bytm[yi])
                S.op("act", lambda e: e.copy(yT[:, head, qb * 512:(qb + 1) * 512], ps3b[:, hf * 512:(hf + 1) * 512]),
                     reads=[Bps3[hf]], writes=[ByT[qb]])

            LA = 2
            for i in range(len(its) + LA):
                if i < len(its):
                    issue_S(i)
                if i - LA >= 0:
                    issue_PV(i - LA)
            if STOP == 20 + G:
                return finish_early()
        S.barrier()
        if STOP == 2:
            return finish_early()

        x1 = view(R_A.base, [NQ, D], F32)
        Bx1 = [Buf(f"x1_{t}") for t in range(NQ)]
        for t in range(NQ):
            tkx = S.dma("sp", sl_x1, x1[:, t, :], xb_t[t], writes=[Bx1[t]])
        for t in range(NQ):
            Bx1[t].w = tkx
        obank = [0]
        for n in range(4):
            wi = pending_units.pop(f"o{n}")
            for t in range(NQ):
                bk = obank[0] % 8
                obank[0] += 1
                for c in range(16):
                    S.op("pe", lambda e: e.matmul(psb[bk][:], yT[:, c, t * 128:(t + 1) * 128], wslot[wi][:, c, :],
                                                  start=(c == 0), stop=(c == 15)),
                         reads=[ByT[t // 4], Bw[wi]], writes=[Bps[bk]], sig=(c == 15))
                S.op("dve", lambda e: e.tensor_tensor(x1[:, t, n * 512:(n + 1) * 512], x1[:, t, n * 512:(n + 1) * 512], psb[bk][:], ALU.add),
                     reads=[Bps[bk], Bx1[t]], writes=[Bx1[t]])
            if n + 2 < 4:
                pending_units[f"o{n + 2}"] = load_unit(w_out_v[:, :, (n + 2) * 512:(n + 3) * 512])
            elif n == 2:
                pending_units[("g", 0, 0)] = load_unit(w_gate_v[:, :, 0:512])
            else:
                pending_units[("u", 0, 0)] = load_unit(w_up_v[:, :, 0:512])
        S.barrier()
        if STOP == 3:
            return finish_early()

        R_S.reset()
        h2T = R_Y.base
        h2T = view(R_Y.base, [16, NQ * 128], BF16)
        Bh2T = [Buf(f"h2T{t}") for t in range(NQ)]
        aT = R_S.alloc([NF, 512], BF16)
        sg = [R_S.alloc([512], F32) for _ in range(4)]
        gfull = R_S.alloc([D], F32)
        junk = R_S.alloc([D], BF16)
        xs_b = [R_S.alloc([D], BF16) for _ in range(2)]
        Bg = Buf("gfull2")
        Bjunk = Buf("junk2")
        Bxs = [Buf("xs2_0"), Buf("xs2_1")]
        BaT = [Buf(f"aT{f}") for f in range(NF)]
        Bsg = [Buf(f"sg{i}") for i in range(4)]
        S.dma("sp", sl_g, gfull, g2_d[:, :], writes=[Bg])
        for t in range(NQ):
            norm_to_T(t, x1[:, t, :], Bx1[t], gfull, Bg, ss2, rs2, Bss2[t], xs_b[t % 2], Bxs[t % 2],
                      junk, Bjunk, h2T, Bh2T[t], t % 2)
        S.barrier(engines=("sp",))
        Bg3 = Buf("gfull3")
        S.dma("sp", sl_g, gfull, g3_d[:, :], writes=[Bg3])

        for blk in range(2):
            tb = slice(blk * 512, (blk + 1) * 512)
            for uu in range(11):
                wg = pending_units.pop(("g", blk, uu))
                wu = pending_units.pop(("u", blk, uu))
                for i in range(4):
                    f = uu * 4 + i
                    bk = f % 2
                    for c in range(16):
                        S.op("pe", lambda e: e.matmul(psb[bk][:], wslot[wg][:, c, i * 128:(i + 1) * 128], h2T[:, c, tb],
                                                      start=(c == 0), stop=(c == 15)),
                             reads=[Bw[wg]] + Bh2T[blk * 4:blk * 4 + 4], writes=[Bps[bk]], sig=(c == 15))
                    S.op("act", lambda e: e.activation(sg[i], psb[bk][:], AF.Silu), reads=[Bps[bk]], writes=[Bsg[i]])
                if uu + 1 < 11:
                    pending_units[("g", blk, uu + 1)] = load_unit(w_gate_v[:, :, (uu + 1) * 512:(uu + 2) * 512])
                else:
                    pending_units[("d", blk, 0)] = load_unit(w_down_v[:, 0:16, 0:512])
                for i in range(4):
                    f = uu * 4 + i
                    bk = 2 + f % 2
                    for c in range(16):
                        S.op("pe", lambda e: e.matmul(psb[bk][:], wslot[wu][:, c, i * 128:(i + 1) * 128], h2T[:, c, tb],
                                                      start=(c == 0), stop=(c == 15)),
                             reads=[Bw[wu]] + Bh2T[blk * 4:blk * 4 + 4], writes=[Bps[bk]], sig=(c == 15))
                    S.op("dve", lambda e: e.tensor_tensor(aT[:, f, :], sg[i], psb[bk][:], ALU.mult),
                         reads=[Bps[bk], Bsg[i]], writes=[BaT[f]])
                if uu + 1 < 11:
                    pending_units[("u", blk, uu + 1)] = load_unit(w_up_v[:, :, (uu + 1) * 512:(uu + 2) * 512])
                else:
                    pending_units[("d", blk, 1)] = load_unit(w_down_v[:, 16:32, 0:512])
            dunits = [(n, fu) for n in range(4) for fu in range(3)]
            frange = [(0, 16), (16, 32), (32, 44)]
            for di, (n, fu) in enumerate(dunits):
                wi = pending_units.pop(("d", blk, di))
                f0, f1 = frange[fu]
                for j in range(4):
                    bk = 4 + j
                    for f in range(f0, f1):
                        S.op("pe", lambda e: e.matmul(psb[bk][:], aT[:, f, j * 128:(j + 1) * 128], wslot[wi][:, f - f0, :],
                                                      start=(f == 0), stop=(f == NF - 1)),
                             reads=[BaT[f], Bw[wi]], writes=[Bps[bk]], sig=(f == f1 - 1))
                    if fu == 2:
                        T = blk * 4 + j
                        S.op("dve", lambda e: e.tensor_tensor(x1[:, T, n * 512:(n + 1) * 512], x1[:, T, n * 512:(n + 1) * 512], psb[bk][:], ALU.add),
                             reads=[Bps[bk], Bx1[T]], writes=[Bx1[T]])
                nx = di + 2
                if nx < len(dunits):
                    n2, fu2 = dunits[nx]
                    a0, a1 = frange[fu2]
                    pending_units[("d", blk, nx)] = load_unit(w_down_v[:, a0:a1, n2 * 512:(n2 + 1) * 512], nchunks=a1 - a0)
                elif blk == 0:
                    if nx == len(dunits):
                        pending_units[("g", 1, 0)] = load_unit(w_gate_v[:, :, 0:512])
                    else:
                        pending_units[("u", 1, 0)] = load_unit(w_up_v[:, :, 0:512])
            for j in range(4):
                T = blk * 4 + j
                src = x1[:, T, :]
                S.op("act", lambda e: e.activation(junk, src, AF.Square, accum_out=ss3[:, T:T + 1]),
                     reads=[Bx1[T], Bconst], writes=[Bjunk, Bss3[T]])
                S.op("act", lambda e: e.activation(rs3[:, T:T + 1], ss3[:, T:T + 1], AF.Sqrt, scale=1.0 / D, bias=epsA[:, 0:1]),
                     reads=[Bss3[T], Bconst], writes=[Bss3[T]])
                S.op("dve", lambda e: e.reciprocal(rs3[:, T:T + 1], rs3[:, T:T + 1]), reads=[Bss3[T]], writes=[Bss3[T]])
                S.op("dve", lambda e: e.scalar_tensor_tensor(src, src, rs3[:, T:T + 1], gfull, ALU.mult, ALU.mult),
                     reads=[Bx1[T], Bss3[T], Bg3], writes=[Bx1[T]])
                last = S.dma("sp", sl_o, out_t[T], src, reads=[Bx1[T]])
        S.wait_tok("sp", last)
        assert not pending_units, pending_units
    return nc


def _mult(d):
    d = np.asarray(d)
    m = (np.abs(d) <= 64).astype(np.float32)
    m += ((d % 4 == 0) & (np.abs(d) <= 256)).astype(np.float32)
    m += ((d % 16 == 0) & (np.abs(d) <= 1024)).astype(np.float32)
    return m


def _rope_tab(pos, rot_dim):
    inv = (np.float32(ROPE_THETA) ** (-np.arange(0, rot_dim, 2, dtype=np.float32) / np.float32(rot_dim))).astype(np.float32)
    ang = pos.astype(np.float32)[:, None] * inv[None, :]
    return np.cos(ang).astype(np.float32), np.sin(ang).astype(np.float32)


_NC_CACHE = {}


def kernel(x, norm_attn, w_in, lambda_qk, subln, w_out, norm_ffn, w_gate, w_up, w_down, norm_final):
    x = np.asarray(x, dtype=np.float32)
    f32c = lambda a: np.ascontiguousarray(np.asarray(a, dtype=np.float32))
    rep = lambda v: np.ascontiguousarray(np.broadcast_to(np.asarray(v, dtype=np.float32).reshape(1, -1), (128, np.asarray(v).size)))
    if "nc" not in _NC_CACHE:
        _NC_CACHE["nc"] = build_program()
    nc = _NC_CACHE["nc"]
    shared = dict(
        g1=rep(norm_attn[0]), g2=rep(norm_ffn[0]), g3=rep(norm_final),
        w_in=f32c(w_in[0]), w_out=f32c(w_out[0]), w_gate=f32c(w_gate[0]), w_up=f32c(w_up[0]), w_down=f32c(w_down[0]),
        lamq=rep(np.asarray(lambda_qk[0]).reshape(-1)), subl=rep(subln[0]),
        ident=np.eye(128, dtype=np.float32),
    )
    p = np.arange(128)[:, None]
    c = np.arange(1920)[None, :]
    mown = _mult(p - (c - 896)).astype(ml_dtypes.bfloat16)
    in_maps = []
    for core in range(8):
        b, hf = core // 2, core % 2
        own = np.arange(hf * 1024, (hf + 1) * 1024)
        oth = np.arange((1 - hf) * 1024, (2 - hf) * 1024)
        pos = np.concatenate([own, oth])
        xb = np.ascontiguousarray(x[b][pos])
        ca, sa = _rope_tab(pos, 32)
        cb, sb = _rope_tab(pos, 16)
        lay = lambda a: np.ascontiguousarray(a.reshape(NT, 128, -1).transpose(1, 0, 2))
        moth = _mult(p - (c - 1920) - 2048 * hf).astype(ml_dtypes.bfloat16)
        m = dict(shared)
        m.update(xb=xb, cosA=lay(ca), sinA=lay(sa), cosB=lay(cb), sinB=lay(sb), mown=mown, moth=moth)
        in_maps.append(m)
    if _NC_CACHE.get("prep_only"):
        return nc, in_maps
    res = run_bass_kernel_spmd(nc, in_maps, core_ids=list(range(8)))
    out = np.empty((4, SEQ, D), dtype=np.float32)
    for core in range(8):
        b, hf = core // 2, core % 2
        out[b, hf * 1024:(hf + 1) * 1024] = res.results[core]["out"]
    return out
```

```python
from contextlib import ExitStack
import math
import numpy as np
import ml_dtypes
import concourse.bass as bass
import concourse.mybir as mybir
from concourse.bass_utils import run_bass_kernel_spmd

F32 = mybir.dt.float32
BF16 = mybir.dt.bfloat16
AF = mybir.ActivationFunctionType
ALU = mybir.AluOpType
AX = mybir.AxisListType

D = 2048
SEQ = 2048
NT = 16
NQ = 8
DFF = 5632
NF = 44
LAM_INIT = 0.8 - 0.6 * math.exp(0.0)
ROPE_THETA = 500000.0


class Tok:
    __slots__ = ("sem", "val", "eng")

    def __init__(self, sem, val, eng):
        self.sem = sem
        self.val = val
        self.eng = eng


class Buf:
    __slots__ = ("name", "w", "rs", "excl")

    def __init__(self, name, excl=False):
        self.name = name
        self.w = None
        self.rs = {}
        self.excl = excl


class Sched:
    def __init__(self, nc, stack):
        self.nc = nc
        self.stack = stack
        self.E = {}
        for name, eng in (("pe", nc.tensor), ("act", nc.scalar), ("dve", nc.vector),
                          ("pool", nc.gpsimd), ("sp", nc.sync)):
            sem = stack.enter_context(nc.semaphore("s_" + name))
            self.E[name] = dict(eng=eng, sem=sem, cnt=0, waited={}, name=name)

    def new_sem(self, name):
        return self.stack.enter_context(self.nc.semaphore(name))

    def slot(self, name):
        return dict(sem=self.new_sem(name), cnt=0)

    def _wait(self, E, reads, writes):
        need = {}

        def add(tok, raw):
            if tok is None:
                return
            if tok.eng == E["name"]:
                if E["name"] in ("pe", "sp", "pool"):
                    return
            k = tok.sem.num
            if k not in need or need[k][1] < tok.val:
                need[k] = (tok.sem, tok.val, tok.eng)

        for b in reads:
            add(b.w, True)
            if b.excl:
                for t in b.rs.values():
                    if t.eng != E["name"]:
                        add(t, False)
        for b in writes:
            add(b.w, False)
            for t in b.rs.values():
                add(t, False)
        for k, (sem, val, en) in need.items():
            if E["waited"].get(k, 0) >= val:
                continue
            if en in self.E:
                assert self.E[en]["cnt"] >= val, f"wait on unflagged {en} {val}>{self.E[en]['cnt']}"
            E["eng"].wait_ge(sem, val)
            E["waited"][k] = val

    def op(self, en, fn, reads=(), writes=(), sig=True):
        E = self.E[en]
        self._wait(E, reads, writes)
        ins = fn(E["eng"])
        if en == "pe" and not sig:
            tok = Tok(E["sem"], E["cnt"] + 1, en)
        else:
            ins.then_inc(E["sem"], 1)
            E["cnt"] += 1
            tok = Tok(E["sem"], E["cnt"], en)
        for b in reads:
            b.rs[en] = tok
        for b in writes:
            b.w = tok
            b.rs = {}
        return tok

    def dma(self, en, slot, out, in_, reads=(), writes=()):
        E = self.E[en]
        self._wait(E, reads, writes)
        ins = E["eng"].dma_start(out=out, in_=in_)
        ins.then_inc(slot["sem"], 16)
        slot["cnt"] += 16
        tok = Tok(slot["sem"], slot["cnt"], "dma")
        for b in reads:
            b.rs["dma%d" % slot["sem"].num] = tok
        for b in writes:
            b.w = tok
            b.rs = {}
        return tok

    def wait_tok(self, en, tok):
        E = self.E[en]
        if E["waited"].get(tok.sem.num, 0) < tok.val:
            E["eng"].wait_ge(tok.sem, tok.val)
            E["waited"][tok.sem.num] = tok.val

    def barrier(self, engines=("pe", "act", "dve", "sp")):
        for en in engines:
            for e2 in ("pe", "act", "dve"):
                if e2 == en:
                    continue
                E2 = self.E[e2]
                if E2["cnt"] > 0:
                    self.wait_tok(en, Tok(E2["sem"], E2["cnt"], e2))


def bcast(ap, n, pos):
    l = [list(x) for x in ap.ap]
    l.insert(pos, [0, n])
    return bass.AP(ap.tensor, ap.offset, l)


import os
STOP = int(os.environ.get("K_STOP", "99"))


def build_program():
    nc = bass.Bass("TRN2", target_bir_lowering=False)
    dt_in = lambda name, shape, dt=F32: nc.dram_tensor(name, shape, dt, kind="ExternalInput").ap()
    xb_d = dt_in("xb", [SEQ, D])
    g1_d = dt_in("g1", [128, D])
    g2_d = dt_in("g2", [128, D])
    g3_d = dt_in("g3", [128, D])
    w_in_d = dt_in("w_in", [D, 6144])
    w_out_d = dt_in("w_out", [D, D])
    w_gate_d = dt_in("w_gate", [D, DFF])
    w_up_d = dt_in("w_up", [D, DFF])
    w_down_d = dt_in("w_down", [DFF, D])
    cosA_d = dt_in("cosA", [128, NT, 16])
    sinA_d = dt_in("sinA", [128, NT, 16])
    cosB_d = dt_in("cosB", [128, NT, 8])
    sinB_d = dt_in("sinB", [128, NT, 8])
    mown_d = dt_in("mown", [128, 1920], BF16)
    moth_d = dt_in("moth", [128, 1920], BF16)
    lamq_d = dt_in("lamq", [128, 256])
    subl_d = dt_in("subl", [128, 128])
    ident_d = dt_in("ident", [128, 128])
    out_d = nc.dram_tensor("out", [NQ * 128, D], F32, kind="ExternalOutput").ap()

    xb_t = xb_d.rearrange("(t p) d -> t p d", p=128)
    out_t = out_d.rearrange("(t p) d -> t p d", p=128)
    w_in_v = w_in_d.rearrange("(c p) n -> p c n", p=128)
    w_out_v = w_out_d.rearrange("(c p) n -> p c n", p=128)
    w_gate_v = w_gate_d.rearrange("(c p) n -> p c n", p=128)
    w_up_v = w_up_d.rearrange("(c p) n -> p c n", p=128)
    w_down_v = w_down_d.rearrange("(f p) n -> p f n", p=128)

    with ExitStack() as st:
        S = Sched(nc, st)
        ARENA_BYTES = 206 * 1024
        arena = st.enter_context(nc.sbuf_tensor("arena", [128, ARENA_BYTES // 2], BF16))

        def view(off, shape, dt):
            n = 1
            for s_ in shape:
                n *= s_
            esz = 4 if dt == F32 else 2
            assert off % 4 == 0 and off + n * esz <= ARENA_BYTES, (off, shape)
            ap = arena[:, off // 2:(off + n * esz) // 2]
            if dt == F32:
                ap = ap.bitcast(F32)
            if len(shape) == 2:
                ap = ap.rearrange("p (a b) -> p a b", b=shape[1])
            elif len(shape) == 3:
                ap = ap.rearrange("p (a b c) -> p a b c", b=shape[1], c=shape[2])
            elif len(shape) == 4:
                ap = ap.rearrange("p (a b c d) -> p a b c d", b=shape[1], c=shape[2], d=shape[3])
            return ap

        class Bump:
            def __init__(self, base, size):
                self.base, self.size, self.off = base, size, 0

            def reset(self):
                self.off = 0

            def alloc(self, shape, dt):
                n = 1
                for s_ in shape:
                    n *= s_
                sz = (n * (4 if dt == F32 else 2) + 3) // 4 * 4
                assert self.off + sz <= self.size, ("region overflow", self.off, sz, self.size)
                v = view(self.base + self.off, shape, dt)
                self.off += sz
                return v

        o = 0
        R_P = Bump(o, 4096); o += 4096
        R_A = Bump(o, 65536); o += 65536
        R_W = Bump(o, 32768); o += 32768
        R_Y = Bump(o, 32768); o += 32768
        R_S = Bump(o, ARENA_BYTES - o)
        assert R_S.size >= 73000, R_S.size

        psb = [st.enter_context(nc.psum_tensor(f"psb{i}", [128, 512], F32)) for i in range(8)]
        Bps = [Buf(f"ps{i}", excl=True) for i in range(8)]
        ps3b = psb[3][:].bitcast(BF16)
        Bps3 = [Bps[3], Bps[3]]

        identf = R_P.alloc([128], F32)
        identb = R_P.alloc([128], BF16)
        epsA = R_P.alloc([1], F32)
        epsB = R_P.alloc([1], F32)
        lamq = R_P.alloc([256], F32)
        lprod = R_P.alloc([2, 64], F32)
        ldots = R_P.alloc([2], F32)
        lex = R_P.alloc([2], F32)
        neglam = R_P.alloc([1], F32)
        subl = R_P.alloc([128], F32)
        ss1 = R_P.alloc([NT], F32)
        rs1 = R_P.alloc([NT], F32)
        ss2 = R_P.alloc([NQ], F32)
        rs2 = R_P.alloc([NQ], F32)
        ss3 = R_P.alloc([NQ], F32)
        rs3 = R_P.alloc([NQ], F32)
        Bconst = Buf("const")
        Bss1 = [Buf(f"ss1_{t}") for t in range(NT)]
        Bss2 = [Buf(f"ss2_{t}") for t in range(NQ)]
        Bss3 = [Buf(f"ss3_{t}") for t in range(NQ)]

        wslot = [view(R_W.base + i * 16384, [16, 512], BF16) for i in range(2)]
        Bw = [Buf("w0"), Buf("w1")]
        sl_w = [S.slot("ldw0"), S.slot("ldw1")]
        wctr = [0]

        def load_unit(src_ap, nchunks=16):
            i = wctr[0] % 2
            wctr[0] += 1
            S.dma("pool", sl_w[i], wslot[i][:, 0:nchunks, :], src_ap, writes=[Bw[i]])
            return i

        sl_c = S.slot("ldc")
        sl_g = S.slot("ldg")
        sl_t = S.slot("ldt")
        sl_x1 = S.slot("ldxres")
        sl_x = [S.slot("ldx0"), S.slot("ldx1")]
        sl_o = S.slot("sto")

        def finish_early():
            S.barrier()
            tkk = S.dma("sp", sl_o, out_t[0], view(R_A.base, [D], F32))
            S.wait_tok("sp", tkk)
            return nc

        S.dma("sp", sl_c, identf, ident_d[:, :], writes=[Bconst])
        S.dma("sp", sl_c, lamq, lamq_d[:, :], writes=[Bconst])
        tk = S.dma("sp", sl_c, subl, subl_d[:, :], writes=[Bconst])
        S.op("dve", lambda e: e.tensor_copy(identb, identf), reads=[Bconst], writes=[Bconst])
        S.op("dve", lambda e: e.memset(epsA, 1e-6), writes=[Bconst])
        S.op("dve", lambda e: e.memset(epsB, 1e-5), writes=[Bconst])
        for tl in (ss1, ss2, ss3):
            S.op("dve", lambda e: e.memset(tl, 0.0), writes=[Bconst])
        lqv = lamq.rearrange("p (a b d) -> p a b d", a=2, b=2, d=64)
        S.op("dve", lambda e: e.tensor_tensor(lprod, lqv[:, :, 0, :], lqv[:, :, 1, :], ALU.mult), reads=[Bconst], writes=[Bconst])
        S.op("dve", lambda e: e.reduce_sum(ldots, lprod, axis=AX.X), reads=[Bconst], writes=[Bconst])
        S.op("act", lambda e: e.activation(lex, ldots, AF.Exp), reads=[Bconst], writes=[Bconst])
        S.op("dve", lambda e: e.tensor_tensor(neglam, lex[:, 1:2], lex[:, 0:1], ALU.subtract), reads=[Bconst], writes=[Bconst])
        S.op("dve", lambda e: e.tensor_scalar(neglam, neglam, -LAM_INIT, None, ALU.add), reads=[Bconst], writes=[Bconst])

        def norm_to_T(t, src, Bsrc, gfull, Bg, ss, rs, Bss, xs, Bxs, junk, Bjunk, dstT, BdstT, par):
            S.op("act", lambda e: e.activation(junk, src, AF.Square, accum_out=ss[:, t:t + 1]),
                 reads=[Bsrc, Bconst], writes=[Bjunk, Bss])
            S.op("act", lambda e: e.activation(rs[:, t:t + 1], ss[:, t:t + 1], AF.Sqrt, scale=1.0 / D, bias=epsA[:, 0:1]),
                 reads=[Bss, Bconst], writes=[Bss])
            S.op("dve", lambda e: e.reciprocal(rs[:, t:t + 1], rs[:, t:t + 1]), reads=[Bss], writes=[Bss])
            S.op("dve", lambda e: e.scalar_tensor_tensor(xs, src, rs[:, t:t + 1], gfull, ALU.mult, ALU.mult),
                 reads=[Bsrc, Bss, Bg], writes=[Bxs])
            for hb in range(2):
                bk = 2 * par + hb
                pv = psb[bk][:].bitcast(BF16)
                for c8 in range(8):
                    c = hb * 8 + c8
                    S.op("pe", lambda e: e.transpose(pv[:, c8 * 128:(c8 + 1) * 128], xs[:, c * 128:(c + 1) * 128], identb),
                         reads=[Bxs, Bconst], writes=[Bps[bk]], sig=(c8 == 7))
                dst = dstT[:, hb * 8:hb * 8 + 8, t * 128:(t + 1) * 128]
                srcv = pv.rearrange("p (a b) -> p a b", b=128)
                if hb == 0:
                    S.op("act", lambda e: e.copy(dst, srcv), reads=[Bps[bk]], writes=[BdstT])
                else:
                    S.op("dve", lambda e: e.tensor_copy(dst, srcv), reads=[Bps[bk]], writes=[BdstT])

        hT = R_A.alloc([16, SEQ], BF16)
        BhT = [Buf(f"hT{t}") for t in range(NT)]
        R_S.reset()
        xin = [R_S.alloc([D], F32) for _ in range(2)]
        xs_b = [R_S.alloc([D], BF16) for _ in range(2)]
        gfull = R_S.alloc([D], F32)
        junk = R_S.alloc([D], BF16)
        Bxin = [Buf("xin0"), Buf("xin1")]
        Bxs = [Buf("xs0"), Buf("xs1")]
        Bg = Buf("gfull")
        Bjunk = Buf("junk")
        S.dma("sp", sl_g, gfull, g1_d[:, :], writes=[Bg])
        GROUPS = [("A", 0), ("A", 1), ("B", 0), ("B", 1)]

        def in_units(G):
            typ, gi = GROUPS[G]
            base = 0 if typ == "A" else 6
            return dict(q=base + gi, k=base + 2 + gi, v=base + 4 + gi)

        def in_unit_ap(n):
            return w_in_v[:, :, n * 512:(n + 1) * 512]

        pending_units = {}
        pending_units[(0, "k")] = load_unit(in_unit_ap(in_units(0)["k"]))
        pending_units[(0, "v")] = load_unit(in_unit_ap(in_units(0)["v"]))

        for t in range(NT):
            S.dma("sp", sl_x[t % 2], xin[t % 2], xb_t[t], writes=[Bxin[t % 2]])
            norm_to_T(t, xin[t % 2], Bxin[t % 2], gfull, Bg, ss1, rs1, Bss1[t], xs_b[t % 2], Bxs[t % 2],
                      junk, Bjunk, hT, BhT[t], t % 2)
        S.barrier()
        if STOP == 1:
            return finish_early()

        R_S.reset()
        QT = R_S.alloc([2, 4, NQ * 128], BF16)
        KT = R_S.alloc([4, SEQ], BF16)
        Vaug = R_S.alloc([NT, 4, 130], BF16)
        mown = R_S.alloc([1920], BF16)
        moth = R_S.alloc([1920], BF16)
        cosA = R_S.alloc([NT, 16], F32)
        sinA = R_S.alloc([NT, 16], F32)
        cosB = R_S.alloc([NT, 8], F32)
        sinB = R_S.alloc([NT, 8], F32)
        tm = [R_S.alloc([512], BF16) for _ in range(2)]
        NPT = 4
        pt = [R_S.alloc([512], BF16) for _ in range(NPT)]
        rt1 = R_S.alloc([128], F32)
        rt2 = R_S.alloc([128], F32)
        rsrc = R_S.alloc([128], F32)
        rden = R_S.alloc([4], F32)
        o1n = R_S.alloc([4, 128], F32)
        yb = R_S.alloc([4, 128], F32)
        ssq = R_S.alloc([4], F32)
        ytm = [R_S.alloc([4, 128], BF16) for _ in range(2)]
        yT = R_Y.alloc([16, NQ * 128], BF16)
        BQT = [Buf(f"QT{t}") for t in range(NQ)]
        BKT = [Buf(f"KT{t}") for t in range(NT)]
        BV = [Buf(f"V{t}") for t in range(NT)]
        Btab = Buf("tables")
        Btm = [Buf("tm0"), Buf("tm1")]
        Bpt = [Buf(f"pt{i}") for i in range(NPT)]
        Brt = Buf("rt")
        Brs = Buf("rsrc")
        Bep = Buf("ep")
        Bo1n = Buf("o1n")
        Bytm = [Buf("ytm0"), Buf("ytm1")]
        ByT = [Buf(f"yT{q}") for q in range(2)]
        S.dma("sp", sl_t, mown, mown_d[:, :], writes=[Btab])
        S.dma("sp", sl_t, moth, moth_d[:, :], writes=[Btab])
        S.dma("sp", sl_t, cosA, cosA_d[:, :, :], writes=[Btab])
        S.dma("sp", sl_t, sinA, sinA_d[:, :, :], writes=[Btab])
        S.dma("sp", sl_t, cosB, cosB_d[:, :, :], writes=[Btab])
        S.dma("sp", sl_t, sinB, sinB_d[:, :, :], writes=[Btab])
        S.op("dve", lambda e: e.memset(QT, 0.0), writes=BQT)
        S.op("dve", lambda e: e.memset(Vaug[:, :, :, 128:130], 1.0), writes=BV)

        if STOP == 10:
            return finish_early()
        pbank = [0]

        def next_pbank():
            b = (0, 1, 2, 4)[pbank[0] % 4]
            pbank[0] += 1
            return b

        trh = [0]

        def proj(bk, t, wi):
            for c in range(16):
                S.op("pe", lambda e: e.matmul(psb[bk][:], hT[:, c, t * 128:(t + 1) * 128], wslot[wi][:, c, :],
                                              start=(c == 0), stop=(c == 15)),
                     reads=[BhT[t], Bw[wi]], writes=[Bps[bk]], sig=(c == 15))

        def rope_evac(bk, t, typ, dst, Bdst):
            if typ == "A":
                nh, hd, r2, ct, stb = 4, 128, 16, cosA, sinA
            else:
                nh, hd, r2, ct, stb = 8, 64, 8, cosB, sinB
            src = psb[bk][:].rearrange("p (h d) -> p h d", d=hd)
            dv = dst.rearrange("p (h d) -> p h d", d=hd)
            rs_ = rsrc[:, 0:nh * 2 * r2].rearrange("p (h d) -> p h d", d=2 * r2)
            t1 = rt1[:, 0:nh * 2 * r2].rearrange("p (h d) -> p h d", d=2 * r2)
            t2 = rt2[:, 0:nh * 2 * r2].rearrange("p (h d) -> p h d", d=2 * r2)
            cb = bcast(ct[:, t, :], nh, 1)
            sb_ = bcast(stb[:, t, :], nh, 1)
            S.op("act", lambda e: e.copy(dv[:, :, 2 * r2:hd], src[:, :, 2 * r2:hd]), reads=[Bps[bk]], writes=[Bdst])
            S.op("act", lambda e: e.copy(rs_, src[:, :, 0:2 * r2]), reads=[Bps[bk]], writes=[Brs])
            if STOP == 16:
                return
            S.op("dve", lambda e: e.tensor_tensor(t1[:, :, 0:r2], rs_[:, :, 0:r2], cb, ALU.mult), reads=[Brs, Btab], writes=[Brt])
            if STOP == 17:
                return
            S.op("dve", lambda e: e.tensor_tensor(t1[:, :, r2:2 * r2], rs_[:, :, r2:2 * r2], cb, ALU.mult), reads=[Brs, Btab], writes=[Brt])
            S.op("dve", lambda e: e.tensor_tensor(t2[:, :, 0:r2], rs_[:, :, r2:2 * r2], sb_, ALU.mult), reads=[Brs, Btab], writes=[Brt])
            S.op("dve", lambda e: e.tensor_tensor(t2[:, :, r2:2 * r2], rs_[:, :, 0:r2], sb_, ALU.mult), reads=[Brs, Btab], writes=[Brt])
            S.op("dve", lambda e: e.tensor_tensor(dv[:, :, 0:r2], t1[:, :, 0:r2], t2[:, :, 0:r2], ALU.subtract), reads=[Brt], writes=[Bdst])
            S.op("dve", lambda e: e.tensor_tensor(dv[:, :, r2:2 * r2], t1[:, :, r2:2 * r2], t2[:, :, r2:2 * r2], ALU.add), reads=[Brt], writes=[Bdst])

        psbf = [psb[i][:].bitcast(BF16) for i in range(8)]

        def transpose4(srcap, Bsrc, bank=3):
            for h in range(4):
                S.op("pe", lambda e: e.transpose(psbf[bank][:, h * 128:(h + 1) * 128],
                                                 srcap[:, h * 128:(h + 1) * 128], identb),
                     reads=[Bsrc, Bconst], writes=[Bps[bank]], sig=(h == 3))
            return psbf[bank][:, 0:512]

        scaleA = 128.0 ** -0.5
        scaleB = 64.0 ** -0.5
        oset_ctr = [0]
        sb_ctr = [0]

        for G in range(4):
            typ, gi = GROUPS[G]
            U = in_units(G)
            if G == 2:
                S.op("dve", lambda e: e.memset(QT, 0.0), writes=BQT)
            wi_k = pending_units.pop((G, "k"))
            wi_v = pending_units.pop((G, "v"))
            def k_tail(t):
                tmi = t % 2
                pv = transpose4(tm[tmi], Btm[tmi], 3)
                S.op("act", lambda e: e.copy(KT[:, :, t * 128:(t + 1) * 128], pv.rearrange("p (h k) -> p h k", k=128)),
                     reads=[Bps[3]], writes=[BKT[t]])

            for t in range(NT + 1):
                if t < NT:
                    bk = next_pbank()
                    proj(bk, t, wi_k)
                    rope_evac(bk, t, typ, tm[t % 2], Btm[t % 2])
                if t >= 1:
                    k_tail(t - 1)
            if STOP in (11, 14, 15, 16, 17):
                return finish_early()
            wi_q = load_unit(in_unit_ap(U["q"]))
            for t in range(NT):
                bk = next_pbank()
                proj(bk, t, wi_v)
                S.op("act", lambda e: e.copy(Vaug[:, t, :, 0:128], psb[bk][:].rearrange("p (h d) -> p h d", d=128)),
                     reads=[Bps[bk]], writes=[BV[t]])
            if G + 1 < 4:
                pending_units[(G + 1, "k")] = load_unit(in_unit_ap(in_units(G + 1)["k"]))
            def q_tail(t):
                tmi = t % 2
                pvf = transpose4(tm[tmi], Btm[tmi], 3)
                pv = pvf.rearrange("p (h k) -> p h k", k=128)
                if typ == "A":
                    S.op("act", lambda e: e.copy(QT[:, 0, :, t * 128:(t + 1) * 128], pv), reads=[Bps[3]], writes=[BQT[t]])
                else:
                    S.op("act", lambda e: e.copy(QT[0:64, 0, :, t * 128:(t + 1) * 128], pv[0:64]), reads=[Bps[3]], writes=[BQT[t]])
                    S.op("act", lambda e: e.copy(QT[64:128, 1, :, t * 128:(t + 1) * 128], pv[64:128]), reads=[Bps[3]], writes=[BQT[t]])

            for t in range(NQ + 1):
                if t < NQ:
                    bk = next_pbank()
                    proj(bk, t, wi_q)
                    rope_evac(bk, t, typ, tm[t % 2], Btm[t % 2])
                if t >= 1:
                    q_tail(t - 1)
            if G + 1 < 4:
                pending_units[(G + 1, "v")] = load_unit(in_unit_ap(in_units(G + 1)["v"]))
            else:
                pending_units["o0"] = load_unit(w_out_v[:, :, 0:512])
                pending_units["o1"] = load_unit(w_out_v[:, :, 512:1024])

            if STOP == 13:
                return finish_early()
            maps = [0] if typ == "A" else [0, 1]
            its = [(hh, qb, m, kt) for hh in range(4) for qb in range(2) for m in maps for kt in range(NT)]
            scale = scaleA if typ == "A" else scaleB
            state = {}

            def issue_S(i):
                hh, qb, m, kt = its[i]
                sbk = sb_ctr[0] % 4
                sb_ctr[0] += 1
                pi = i % NPT
                S.op("pe", lambda e: e.matmul(psb[sbk][:], KT[:, hh, kt * 128:(kt + 1) * 128],
                                              QT[:, m, hh, qb * 512:(qb + 1) * 512], start=True, stop=True),
                     reads=[BKT[kt]] + BQT[qb * 4:qb * 4 + 4], writes=[Bps[sbk]], sig=True)
                S.op("act", lambda e: e.activation(pt[pi], psb[sbk][:], AF.Exp, scale=scale), reads=[Bps[sbk]], writes=[Bpt[pi]])
                if typ == "A":
                    u = qb * 512 - 128 * kt
                    if kt < 8:
                        msk = mown[:, u + 896:u + 896 + 512]
                    else:
                        msk = moth[:, u + 1920:u + 1920 + 512]
                    S.op("dve", lambda e: e.tensor_tensor(pt[pi], pt[pi], msk, ALU.mult), reads=[Bpt[pi], Btab], writes=[Bpt[pi]])

            def issue_PV(i):
                hh, qb, m, kt = its[i]
                pi = i % NPT
                if kt == 0:
                    state["os"] = oset_ctr[0] % 2
                    oset_ctr[0] += 1
                os_ = state["os"]
                banks = (4 + 2 * os_, 5 + 2 * os_)
                for j in range(4):
                    bk = banks[j // 2]
                    col = (j % 2) * 130
                    S.op("pe", lambda e: e.matmul(psb[bk][:, col:col + 129], pt[pi][:, j * 128:(j + 1) * 128],
                                                  Vaug[:, kt, hh, 0:129], start=(kt == 0 and j % 2 == 0),
                                                  stop=(kt == NT - 1 and j % 2 == 1)),
                         reads=[Bpt[pi], BV[kt]], writes=[Bps[bk]], sig=(j == 3))
                if kt == NT - 1:
                    epilogue(hh, qb, m, banks)

            def epilogue(hh, qb, m, banks):
                head = (0 if typ == "A" else 8) + gi * 4 + hh
                ov = [psb[b][:, 0:260].rearrange("p (j c) -> p j c", c=130) for b in banks]
                for bi in range(2):
                    S.op("dve", lambda e: e.reciprocal(rden[:, 2 * bi:2 * bi + 2].rearrange("p (j o) -> p j o", o=1), ov[bi][:, :, 128:129]),
                         reads=[Bps[banks[bi]]], writes=[Bep])
                yi = oset_ctr[0] % 2
                if typ == "A":
                    for bi in range(2):
                        S.op("dve", lambda e: e.tensor_tensor(ytm[yi][:, 2 * bi:2 * bi + 2, :], ov[bi][:, :, 0:128],
                                                              bcast(rden[:, 2 * bi:2 * bi + 2], 128, 2), ALU.mult),
                             reads=[Bps[banks[bi]], Bep], writes=[Bytm[yi]])
                elif m == 0:
                    for bi in range(2):
                        S.op("dve", lambda e: e.tensor_tensor(o1n[:, 2 * bi:2 * bi + 2, :], ov[bi][:, :, 0:128],
                                                              bcast(rden[:, 2 * bi:2 * bi + 2], 128, 2), ALU.mult),
                             reads=[Bps[banks[bi]], Bep], writes=[Bo1n])
                    return
                else:
                    for bi in range(2):
                        S.op("dve", lambda e: e.tensor_tensor(yb[:, 2 * bi:2 * bi + 2, :], ov[bi][:, :, 0:128],
                                                              bcast(rden[:, 2 * bi:2 * bi + 2], 128, 2), ALU.mult),
                             reads=[Bps[banks[bi]], Bep], writes=[Bep])
                    S.op("dve", lambda e: e.scalar_tensor_tensor(yb, yb, neglam[:, 0:1], o1n, ALU.mult, ALU.add),
                         reads=[Bep, Bo1n, Bconst], writes=[Bep])
                    S.op("dve", lambda e: e.tensor_tensor(o1n, yb, yb, ALU.mult), reads=[Bep, Bo1n], writes=[Bo1n])
                    S.op("dve", lambda e: e.reduce_sum(ssq, o1n, axis=AX.X), reads=[Bo1n], writes=[Bep])
                    S.op("act", lambda e: e.activation(ssq, ssq, AF.Sqrt, scale=1.0 / 128.0, bias=epsB[:, 0:1]), reads=[Bep, Bconst], writes=[Bep])
                    S.op("dve", lambda e: e.reciprocal(ssq, ssq), reads=[Bep], writes=[Bep])
                    S.op("dve", lambda e: e.tensor_scalar(ssq, ssq, 1.0 - LAM_INIT, None, ALU.mult), reads=[Bep], writes=[Bep])
                    S.op("dve", lambda e: e.tensor_tensor(yb, yb, bcast(ssq, 128, 2), ALU.mult), reads=[Bep], writes=[Bep])
                    S.op("dve", lambda e: e.tensor_tensor(ytm[yi], yb, bcast(subl, 4, 1), ALU.mult), reads=[Bep, Bconst], writes=[Bytm[yi]])
                tb_ = sb_ctr[0] % 4
                sb_ctr[0] += 1
                pvf = transpose4(ytm[yi].rearrange("p j d -> p (j d)"), Bytm[yi], tb_)
                S.op("dve", lambda e: e.tensor_copy(yT[:, head, qb * 512:(qb + 1) * 512], pvf),
                     reads=[Bps[tb_]], writes=[ByT[qb]])

            LA = 3
            for i in range(len(its) + LA):
                if i < len(its):
                    issue_S(i)
                if i - LA >= 0:
                    issue_PV(i - LA)
            if STOP == 20 + G:
                return finish_early()
        S.barrier()
        if STOP == 2:
            return finish_early()

        x1 = view(R_A.base, [NQ, D], F32)
        Bx1 = [Buf(f"x1_{t}") for t in range(NQ)]
        for t in range(NQ):
            tkx = S.dma("sp", sl_x1, x1[:, t, :], xb_t[t], writes=[Bx1[t]])
        for t in range(NQ):
            Bx1[t].w = tkx
        obank = [0]
        for n in range(4):
            wi = pending_units.pop(f"o{n}")
            for t in range(NQ):
                bk = obank[0] % 8
                obank[0] += 1
                for c in range(16):
                    S.op("pe", lambda e: e.matmul(psb[bk][:], yT[:, c, t * 128:(t + 1) * 128], wslot[wi][:, c, :],
                                                  start=(c == 0), stop=(c == 15)),
                         reads=[ByT[t // 4], Bw[wi]], writes=[Bps[bk]], sig=(c == 15))
                S.op("dve", lambda e: e.tensor_tensor(x1[:, t, n * 512:(n + 1) * 512], x1[:, t, n * 512:(n + 1) * 512], psb[bk][:], ALU.add),
                     reads=[Bps[bk], Bx1[t]], writes=[Bx1[t]])
            if n + 2 < 4:
                pending_units[f"o{n + 2}"] = load_unit(w_out_v[:, :, (n + 2) * 512:(n + 3) * 512])
            elif n == 2:
                pending_units[("g", 0, 0)] = load_unit(w_gate_v[:, :, 0:512])
            else:
                pending_units[("u", 0, 0)] = load_unit(w_up_v[:, :, 0:512])
        S.barrier()
        if STOP == 3:
            return finish_early()

        R_S.reset()
        h2T = R_Y.base
        h2T = view(R_Y.base, [16, NQ * 128], BF16)
        Bh2T = [Buf(f"h2T{t}") for t in range(NQ)]
        aT = R_S.alloc([NF, 512], BF16)
        sg = [R_S.alloc([512], F32) for _ in range(4)]
        gfull = R_S.alloc([D], F32)
        junk = R_S.alloc([D], BF16)
        xs_b = [R_S.alloc([D], BF16) for _ in range(2)]
        Bg = Buf("gfull2")
        Bjunk = Buf("junk2")
        Bxs = [Buf("xs2_0"), Buf("xs2_1")]
        BaT = [Buf(f"aT{f}") for f in range(NF)]
        Bsg = [Buf(f"sg{i}") for i in range(4)]
        S.dma("sp", sl_g, gfull, g2_d[:, :], writes=[Bg])
        for t in range(NQ):
            norm_to_T(t, x1[:, t, :], Bx1[t], gfull, Bg, ss2, rs2, Bss2[t], xs_b[t % 2], Bxs[t % 2],
                      junk, Bjunk, h2T, Bh2T[t], t % 2)
        S.barrier(engines=("sp",))
        Bg3 = Buf("gfull3")
        S.dma("sp", sl_g, gfull, g3_d[:, :], writes=[Bg3])

        for blk in range(2):
            tb = slice(blk * 512, (blk + 1) * 512)
            for uu in range(11):
                wg = pending_units.pop(("g", blk, uu))
                wu = pending_units.pop(("u", blk, uu))
                for i in range(4):
                    f = uu * 4 + i
                    bk = f % 2
                    for c in range(16):
                        S.op("pe", lambda e: e.matmul(psb[bk][:], wslot[wg][:, c, i * 128:(i + 1) * 128], h2T[:, c, tb],
                                                      start=(c == 0), stop=(c == 15)),
                             reads=[Bw[wg]] + Bh2T[blk * 4:blk * 4 + 4], writes=[Bps[bk]], sig=(c == 15))
                    S.op("act", lambda e: e.activation(sg[i], psb[bk][:], AF.Silu), reads=[Bps[bk]], writes=[Bsg[i]])
                if uu + 1 < 11:
                    pending_units[("g", blk, uu + 1)] = load_unit(w_gate_v[:, :, (uu + 1) * 512:(uu + 2) * 512])
                else:
                    pending_units[("d", blk, 0)] = load_unit(w_down_v[:, 0:16, 0:512])
                for i in range(4):
                    f = uu * 4 + i
                    bk = 2 + f % 2
                    for c in range(16):
                        S.op("pe", lambda e: e.matmul(psb[bk][:], wslot[wu][:, c, i * 128:(i + 1) * 128], h2T[:, c, tb],
                                                      start=(c == 0), stop=(c == 15)),
                             reads=[Bw[wu]] + Bh2T[blk * 4:blk * 4 + 4], writes=[Bps[bk]], sig=(c == 15))
                    S.op("dve", lambda e: e.tensor_tensor(aT[:, f, :], sg[i], psb[bk][:], ALU.mult),
                         reads=[Bps[bk], Bsg[i]], writes=[BaT[f]])
                if uu + 1 < 11:
                    pending_units[("u", blk, uu + 1)] = load_unit(w_up_v[:, :, (uu + 1) * 512:(uu + 2) * 512])
                else:
                    pending_units[("d", blk, 1)] = load_unit(w_down_v[:, 16:32, 0:512])
            dunits = [(n, fu) for n in range(4) for fu in range(3)]
            frange = [(0, 16), (16, 32), (32, 44)]
            for di, (n, fu) in enumerate(dunits):
                wi = pending_units.pop(("d", blk, di))
                f0, f1 = frange[fu]
                for j in range(4):
                    bk = 4 + j
                    for f in range(f0, f1):
                        S.op("pe", lambda e: e.matmul(psb[bk][:], aT[:, f, j * 128:(j + 1) * 128], wslot[wi][:, f - f0, :],
                                                      start=(f == 0), stop=(f == NF - 1)),
                             reads=[BaT[f], Bw[wi]], writes=[Bps[bk]], sig=(f == f1 - 1))
                    if fu == 2:
                        T = blk * 4 + j
                        S.op("dve", lambda e: e.tensor_tensor(x1[:, T, n * 512:(n + 1) * 512], x1[:, T, n * 512:(n + 1) * 512], psb[bk][:], ALU.add),
                             reads=[Bps[bk], Bx1[T]], writes=[Bx1[T]])
                nx = di + 2
                if nx < len(dunits):
                    n2, fu2 = dunits[nx]
                    a0, a1 = frange[fu2]
                    pending_units[("d", blk, nx)] = load_unit(w_down_v[:, a0:a1, n2 * 512:(n2 + 1) * 512], nchunks=a1 - a0)
                elif blk == 0:
                    if nx == len(dunits):
                        pending_units[("g", 1, 0)] = load_unit(w_gate_v[:, :, 0:512])
                    else:
                        pending_units[("u", 1, 0)] = load_unit(w_up_v[:, :, 0:512])
            for j in range(4):
                T = blk * 4 + j
                src = x1[:, T, :]
                S.op("act", lambda e: e.activation(junk, src, AF.Square, accum_out=ss3[:, T:T + 1]),
                     reads=[Bx1[T], Bconst], writes=[Bjunk, Bss3[T]])
                S.op("act", lambda e: e.activation(rs3[:, T:T + 1], ss3[:, T:T + 1], AF.Sqrt, scale=1.0 / D, bias=epsA[:, 0:1]),
                     reads=[Bss3[T], Bconst], writes=[Bss3[T]])
                S.op("dve", lambda e: e.reciprocal(rs3[:, T:T + 1], rs3[:, T:T + 1]), reads=[Bss3[T]], writes=[Bss3[T]])
                S.op("dve", lambda e: e.scalar_tensor_tensor(src, src, rs3[:, T:T + 1], gfull, ALU.mult, ALU.mult),
                     reads=[Bx1[T], Bss3[T], Bg3], writes=[Bx1[T]])
                last = S.dma("sp", sl_o, out_t[T], src, reads=[Bx1[T]])
        S.wait_tok("sp", last)
        assert not pending_units, pending_units
    return nc


def _mult(d):
    d = np.asarray(d)
    m = (np.abs(d) <= 64).astype(np.float32)
    m += ((d % 4 == 0) & (np.abs(d) <= 256)).astype(np.float32)
    m += ((d % 16 == 0) & (np.abs(d) <= 1024)).astype(np.float32)
    return m


def _rope_tab(pos, rot_dim):
    inv = (np.float32(ROPE_THETA) ** (-np.arange(0, rot_dim, 2, dtype=np.float32) / np.float32(rot_dim))).astype(np.float32)
    ang = pos.astype(np.float32)[:, None] * inv[None, :]
    return np.cos(ang).astype(np.float32), np.sin(ang).astype(np.float32)


_NC_CACHE = {}


def kernel(x, norm_attn, w_in, lambda_qk, subln, w_out, norm_ffn, w_gate, w_up, w_down, norm_final):
    x = np.asarray(x, dtype=np.float32)
    f32c = lambda a: np.ascontiguousarray(np.asarray(a, dtype=np.float32))
    rep = lambda v: np.ascontiguousarray(np.broadcast_to(np.asarray(v, dtype=np.float32).reshape(1, -1), (128, np.asarray(v).size)))
    if "nc" not in _NC_CACHE:
        _NC_CACHE["nc"] = build_program()
    nc = _NC_CACHE["nc"]
    shared = dict(
        g1=rep(norm_attn[0]), g2=rep(norm_ffn[0]), g3=rep(norm_final),
        w_in=f32c(w_in[0]), w_out=f32c(w_out[0]), w_gate=f32c(w_gate[0]), w_up=f32c(w_up[0]), w_down=f32c(w_down[0]),
        lamq=rep(np.asarray(lambda_qk[0]).reshape(-1)), subl=rep(subln[0]),
        ident=np.eye(128, dtype=np.float32),
    )
    p = np.arange(128)[:, None]
    c = np.arange(1920)[None, :]
    mown = _mult(p - (c - 896)).astype(ml_dtypes.bfloat16)
    in_maps = []
    for core in range(8):
        b, hf = core // 2, core % 2
        own = np.arange(hf * 1024, (hf + 1) * 1024)
        oth = np.arange((1 - hf) * 1024, (2 - hf) * 1024)
        pos = np.concatenate([own, oth])
        xb = np.ascontiguousarray(x[b][pos])
        ca, sa = _rope_tab(pos, 32)
        cb, sb = _rope_tab(pos, 16)
        lay = lambda a: np.ascontiguousarray(a.reshape(NT, 128, -1).transpose(1, 0, 2))
        moth = _mult(p - (c - 1920) - 2048 * hf).astype(ml_dtypes.bfloat16)
        m = dict(shared)
        m.update(xb=xb, cosA=lay(ca), sinA=lay(sa), cosB=lay(cb), sinB=lay(sb), mown=mown, moth=moth)
        in_maps.append(m)
    if _NC_CACHE.get("prep_only"):
        return nc, in_maps
    res = run_bass_kernel_spmd(nc, in_maps, core_ids=list(range(8)))
    out = np.empty((4, SEQ, D), dtype=np.float32)
    for core in range(8):
        b, hf = core // 2, core % 2
        out[b, hf * 1024:(hf + 1) * 1024] = res.results[core]["out"]
    return out
```

```python
from contextlib import ExitStack
import math
import numpy as np
import ml_dtypes
import concourse.bass as bass
import concourse.mybir as mybir
from concourse.bass_utils import run_bass_kernel_spmd

F32 = mybir.dt.float32
BF16 = mybir.dt.bfloat16
AF = mybir.ActivationFunctionType
ALU = mybir.AluOpType
AX = mybir.AxisListType

D = 2048
SEQ = 2048
NT = 16
NQ = 8
DFF = 5632
NF = 44
LAM_INIT = 0.8 - 0.6 * math.exp(0.0)
ROPE_THETA = 500000.0


class Tok:
    __slots__ = ("sem", "val", "eng")

    def __init__(self, sem, val, eng):
        self.sem = sem
        self.val = val
        self.eng = eng


class Buf:
    __slots__ = ("name", "w", "rs", "excl")

    def __init__(self, name, excl=False):
        self.name = name
        self.w = None
        self.rs = {}
        self.excl = excl


class Sched:
    def __init__(self, nc, stack):
        self.nc = nc
        self.stack = stack
        self.E = {}
        for name, eng in (("pe", nc.tensor), ("act", nc.scalar), ("dve", nc.vector),
                          ("pool", nc.gpsimd), ("sp", nc.sync)):
            sem = stack.enter_context(nc.semaphore("s_" + name))
            self.E[name] = dict(eng=eng, sem=sem, cnt=0, waited={}, name=name)

    def new_sem(self, name):
        return self.stack.enter_context(self.nc.semaphore(name))

    def slot(self, name):
        return dict(sem=self.new_sem(name), cnt=0)

    def _wait(self, E, reads, writes):
        need = {}

        def add(tok, raw):
            if tok is None:
                return
            if tok.eng == E["name"]:
                if E["name"] in ("pe", "sp", "pool"):
                    return
            k = tok.sem.num
            if k not in need or need[k][1] < tok.val:
                need[k] = (tok.sem, tok.val, tok.eng)

        for b in reads:
            add(b.w, True)
            if b.excl:
                for t in b.rs.values():
                    if t.eng != E["name"]:
                        add(t, False)
        for b in writes:
            add(b.w, False)
            for t in b.rs.values():
                add(t, False)
        for k, (sem, val, en) in need.items():
            if E["waited"].get(k, 0) >= val:
                continue
            if en in self.E:
                assert self.E[en]["cnt"] >= val, f"wait on unflagged {en} {val}>{self.E[en]['cnt']}"
            E["eng"].wait_ge(sem, val)
            E["waited"][k] = val

    def op(self, en, fn, reads=(), writes=(), sig=True):
        E = self.E[en]
        self._wait(E, reads, writes)
        ins = fn(E["eng"])
        if en == "pe" and not sig:
            tok = Tok(E["sem"], E["cnt"] + 1, en)
        else:
            ins.then_inc(E["sem"], 1)
            E["cnt"] += 1
            tok = Tok(E["sem"], E["cnt"], en)
        for b in reads:
            b.rs[en] = tok
        for b in writes:
            b.w = tok
            b.rs = {}
        return tok

    def dma(self, en, slot, out, in_, reads=(), writes=()):
        E = self.E[en]
        self._wait(E, reads, writes)
        ins = E["eng"].dma_start(out=out, in_=in_)
        ins.then_inc(slot["sem"], 16)
        slot["cnt"] += 16
        tok = Tok(slot["sem"], slot["cnt"], "dma")
        for b in reads:
            b.rs["dma%d" % slot["sem"].num] = tok
        for b in writes:
            b.w = tok
            b.rs = {}
        return tok

    def wait_tok(self, en, tok):
        E = self.E[en]
        if E["waited"].get(tok.sem.num, 0) < tok.val:
            E["eng"].wait_ge(tok.sem, tok.val)
            E["waited"][tok.sem.num] = tok.val

    def barrier(self, engines=("pe", "act", "dve", "sp")):
        for en in engines:
            for e2 in ("pe", "act", "dve"):
                if e2 == en:
                    continue
                E2 = self.E[e2]
                if E2["cnt"] > 0:
                    self.wait_tok(en, Tok(E2["sem"], E2["cnt"], e2))


def bcast(ap, n, pos):
    l = [list(x) for x in ap.ap]
    l.insert(pos, [0, n])
    return bass.AP(ap.tensor, ap.offset, l)


import os
STOP = int(os.environ.get("K_STOP", "99"))


def build_program():
    nc = bass.Bass("TRN2", target_bir_lowering=False)
    dt_in = lambda name, shape, dt=F32: nc.dram_tensor(name, shape, dt, kind="ExternalInput").ap()
    xb_d = dt_in("xb", [SEQ, D])
    g1_d = dt_in("g1", [128, D])
    g2_d = dt_in("g2", [128, D])
    g3_d = dt_in("g3", [128, D])
    w_in_d = dt_in("w_in", [D, 6144])
    w_out_d = dt_in("w_out", [D, D])
    w_gate_d = dt_in("w_gate", [D, DFF])
    w_up_d = dt_in("w_up", [D, DFF])
    w_down_d = dt_in("w_down", [DFF, D])
    cosA_d = dt_in("cosA", [128, NT, 16])
    sinA_d = dt_in("sinA", [128, NT, 16])
    cosB_d = dt_in("cosB", [128, NT, 8])
    sinB_d = dt_in("sinB", [128, NT, 8])
    mown_d = dt_in("mown", [128, 1920], BF16)
    moth_d = dt_in("moth", [128, 1920], BF16)
    lamq_d = dt_in("lamq", [128, 256])
    subl_d = dt_in("subl", [128, 128])
    ident_d = dt_in("ident", [128, 128])
    out_d = nc.dram_tensor("out", [NQ * 128, D], F32, kind="ExternalOutput").ap()

    xb_t = xb_d.rearrange("(t p) d -> t p d", p=128)
    out_t = out_d.rearrange("(t p) d -> t p d", p=128)
    w_in_v = w_in_d.rearrange("(c p) n -> p c n", p=128)
    w_out_v = w_out_d.rearrange("(c p) n -> p c n", p=128)
    w_gate_v = w_gate_d.rearrange("(c p) n -> p c n", p=128)
    w_up_v = w_up_d.rearrange("(c p) n -> p c n", p=128)
    w_down_v = w_down_d.rearrange("(f p) n -> p f n", p=128)

    with ExitStack() as st:
        S = Sched(nc, st)
        ARENA_BYTES = 206 * 1024
        arena = st.enter_context(nc.sbuf_tensor("arena", [128, ARENA_BYTES // 2], BF16))

        def view(off, shape, dt):
            n = 1
            for s_ in shape:
                n *= s_
            esz = 4 if dt == F32 else 2
            assert off % 4 == 0 and off + n * esz <= ARENA_BYTES, (off, shape)
            ap = arena[:, off // 2:(off + n * esz) // 2]
            if dt == F32:
                ap = ap.bitcast(F32)
            if len(shape) == 2:
                ap = ap.rearrange("p (a b) -> p a b", b=shape[1])
            elif len(shape) == 3:
                ap = ap.rearrange("p (a b c) -> p a b c", b=shape[1], c=shape[2])
            elif len(shape) == 4:
                ap = ap.rearrange("p (a b c d) -> p a b c d", b=shape[1], c=shape[2], d=shape[3])
            return ap

        class Bump:
            def __init__(self, base, size):
                self.base, self.size, self.off = base, size, 0

            def reset(self):
                self.off = 0

            def alloc(self, shape, dt):
                n = 1
                for s_ in shape:
                    n *= s_
                sz = (n * (4 if dt == F32 else 2) + 3) // 4 * 4
                assert self.off + sz <= self.size, ("region overflow", self.off, sz, self.size)
                v = view(self.base + self.off, shape, dt)
                self.off += sz
                return v

        o = 0
        R_P = Bump(o, 4096); o += 4096
        R_A = Bump(o, 65536); o += 65536
        R_W = Bump(o, 32768); o += 32768
        R_Y = Bump(o, 32768); o += 32768
        R_S = Bump(o, ARENA_BYTES - o)
        assert R_S.size >= 73000, R_S.size

        psb = [st.enter_context(nc.psum_tensor(f"psb{i}", [128, 512], F32)) for i in range(8)]
        Bps = [Buf(f"ps{i}", excl=True) for i in range(8)]
        ps3b = psb[3][:].bitcast(BF16)
        Bps3 = [Bps[3], Bps[3]]

        identf = R_P.alloc([128], F32)
        identb = R_P.alloc([128], BF16)
        epsA = R_P.alloc([1], F32)
        epsB = R_P.alloc([1], F32)
        lamq = R_P.alloc([256], F32)
        lprod = R_P.alloc([2, 64], F32)
        ldots = R_P.alloc([2], F32)
        lex = R_P.alloc([2], F32)
        neglam = R_P.alloc([1], F32)
        subl = R_P.alloc([128], F32)
        ss1 = R_P.alloc([NT], F32)
        rs1 = R_P.alloc([NT], F32)
        ss2 = R_P.alloc([NQ], F32)
        rs2 = R_P.alloc([NQ], F32)
        ss3 = R_P.alloc([NQ], F32)
        rs3 = R_P.alloc([NQ], F32)
        Bconst = Buf("const")
        Bss1 = [Buf(f"ss1_{t}") for t in range(NT)]
        Bss2 = [Buf(f"ss2_{t}") for t in range(NQ)]
        Bss3 = [Buf(f"ss3_{t}") for t in range(NQ)]

        wslot = [view(R_W.base + i * 16384, [16, 512], BF16) for i in range(2)]
        Bw = [Buf("w0"), Buf("w1")]
        sl_w = [S.slot("ldw0"), S.slot("ldw1")]
        wctr = [0]

        def load_unit(src_ap, nchunks=16):
            i = wctr[0] % 2
            wctr[0] += 1
            S.dma("pool", sl_w[i], wslot[i][:, 0:nchunks, :], src_ap, writes=[Bw[i]])
            return i

        sl_c = S.slot("ldc")
        sl_g = S.slot("ldg")
        sl_t = S.slot("ldt")
        sl_x1 = S.slot("ldxres")
        sl_x = [S.slot("ldx0"), S.slot("ldx1")]
        sl_o = S.slot("sto")

        def finish_early():
            S.barrier()
            tkk = S.dma("sp", sl_o, out_t[0], view(R_A.base, [D], F32))
            S.wait_tok("sp", tkk)
            return nc

        S.dma("sp", sl_c, identf, ident_d[:, :], writes=[Bconst])
        S.dma("sp", sl_c, lamq, lamq_d[:, :], writes=[Bconst])
        tk = S.dma("sp", sl_c, subl, subl_d[:, :], writes=[Bconst])
        S.op("dve", lambda e: e.tensor_copy(identb, identf), reads=[Bconst], writes=[Bconst])
        S.op("dve", lambda e: e.memset(epsA, 1e-6), writes=[Bconst])
        S.op("dve", lambda e: e.memset(epsB, 1e-5), writes=[Bconst])
        for tl in (ss1, ss2, ss3):
            S.op("dve", lambda e: e.memset(tl, 0.0), writes=[Bconst])
        lqv = lamq.rearrange("p (a b d) -> p a b d", a=2, b=2, d=64)
        S.op("dve", lambda e: e.tensor_tensor(lprod, lqv[:, :, 0, :], lqv[:, :, 1, :], ALU.mult), reads=[Bconst], writes=[Bconst])
        S.op("dve", lambda e: e.reduce_sum(ldots, lprod, axis=AX.X), reads=[Bconst], writes=[Bconst])
        S.op("act", lambda e: e.activation(lex, ldots, AF.Exp), reads=[Bconst], writes=[Bconst])
        S.op("dve", lambda e: e.tensor_tensor(neglam, lex[:, 1:2], lex[:, 0:1], ALU.subtract), reads=[Bconst], writes=[Bconst])
        S.op("dve", lambda e: e.tensor_scalar(neglam, neglam, -LAM_INIT, None, ALU.add), reads=[Bconst], writes=[Bconst])

        def norm_to_T(t, src, Bsrc, gfull, Bg, ss, rs, Bss, xs, Bxs, junk, Bjunk, dstT, BdstT, par):
            S.op("act", lambda e: e.activation(junk, src, AF.Square, accum_out=ss[:, t:t + 1]),
                 reads=[Bsrc, Bconst], writes=[Bjunk, Bss])
            S.op("act", lambda e: e.activation(rs[:, t:t + 1], ss[:, t:t + 1], AF.Sqrt, scale=1.0 / D, bias=epsA[:, 0:1]),
                 reads=[Bss, Bconst], writes=[Bss])
            S.op("dve", lambda e: e.reciprocal(rs[:, t:t + 1], rs[:, t:t + 1]), reads=[Bss], writes=[Bss])
            S.op("dve", lambda e: e.scalar_tensor_tensor(xs, src, rs[:, t:t + 1], gfull, ALU.mult, ALU.mult),
                 reads=[Bsrc, Bss, Bg], writes=[Bxs])
            for hb in range(2):
                bk = 2 * par + hb
                pv = psb[bk][:].bitcast(BF16)
                for c8 in range(8):
                    c = hb * 8 + c8
                    S.op("pe", lambda e: e.transpose(pv[:, c8 * 128:(c8 + 1) * 128], xs[:, c * 128:(c + 1) * 128], identb),
                         reads=[Bxs, Bconst], writes=[Bps[bk]], sig=(c8 == 7))
                dst = dstT[:, hb * 8:hb * 8 + 8, t * 128:(t + 1) * 128]
                srcv = pv.rearrange("p (a b) -> p a b", b=128)
                if hb == 0:
                    S.op("act", lambda e: e.copy(dst, srcv), reads=[Bps[bk]], writes=[BdstT])
                else:
                    S.op("dve", lambda e: e.tensor_copy(dst, srcv), reads=[Bps[bk]], writes=[BdstT])

        hT = R_A.alloc([16, SEQ], BF16)
        BhT = [Buf(f"hT{t}") for t in range(NT)]
        R_S.reset()
        xin = [R_S.alloc([D], F32) for _ in range(2)]
        xs_b = [R_S.alloc([D], BF16) for _ in range(2)]
        gfull = R_S.alloc([D], F32)
        junk = R_S.alloc([D], BF16)
        Bxin = [Buf("xin0"), Buf("xin1")]
        Bxs = [Buf("xs0"), Buf("xs1")]
        Bg = Buf("gfull")
        Bjunk = Buf("junk")
        S.dma("sp", sl_g, gfull, g1_d[:, :], writes=[Bg])
        GROUPS = [("A", 0), ("A", 1), ("B", 0), ("B", 1)]

        def in_units(G):
            typ, gi = GROUPS[G]
            base = 0 if typ == "A" else 6
            return dict(q=base + gi, k=base + 2 + gi, v=base + 4 + gi)

        def in_unit_ap(n):
            return w_in_v[:, :, n * 512:(n + 1) * 512]

        pending_units = {}
        pending_units[(0, "k")] = load_unit(in_unit_ap(in_units(0)["k"]))
        pending_units[(0, "v")] = load_unit(in_unit_ap(in_units(0)["v"]))

        for t in range(NT):
            S.dma("sp", sl_x[t % 2], xin[t % 2], xb_t[t], writes=[Bxin[t % 2]])
            norm_to_T(t, xin[t % 2], Bxin[t % 2], gfull, Bg, ss1, rs1, Bss1[t], xs_b[t % 2], Bxs[t % 2],
                      junk, Bjunk, hT, BhT[t], t % 2)
        S.barrier()
        if STOP == 1:
            return finish_early()

        R_S.reset()
        QT = R_S.alloc([2, 4, NQ * 128], BF16)
        KT = R_S.alloc([4, SEQ], BF16)
        Vaug = R_S.alloc([NT, 4, 130], BF16)
        mown = R_S.alloc([1920], BF16)
        moth = R_S.alloc([1920], BF16)
        cosA = R_S.alloc([NT, 16], F32)
        sinA = R_S.alloc([NT, 16], F32)
        cosB = R_S.alloc([NT, 8], F32)
        sinB = R_S.alloc([NT, 8], F32)
        tm = [R_S.alloc([512], BF16) for _ in range(2)]
        NPT = 4
        pt = [R_S.alloc([512], BF16) for _ in range(NPT)]
        rt1 = R_S.alloc([128], F32)
        rt2 = R_S.alloc([128], F32)
        rsrc = R_S.alloc([128], F32)
        rden = R_S.alloc([4], F32)
        o1n = R_S.alloc([4, 128], F32)
        yb = R_S.alloc([4, 128], F32)
        ssq = R_S.alloc([4], F32)
        ytm = [R_S.alloc([4, 128], BF16) for _ in range(2)]
        yT = R_Y.alloc([16, NQ * 128], BF16)
        BQT = [Buf(f"QT{t}") for t in range(NQ)]
        BKT = [Buf(f"KT{t}") for t in range(NT)]
        BV = [Buf(f"V{t}") for t in range(NT)]
        Btab = Buf("tables")
        Btm = [Buf("tm0"), Buf("tm1")]
        Bpt = [Buf(f"pt{i}") for i in range(NPT)]
        Brt = Buf("rt")
        Brs = Buf("rsrc")
        Bep = Buf("ep")
        Bo1n = Buf("o1n")
        Bytm = [Buf("ytm0"), Buf("ytm1")]
        ByT = [Buf(f"yT{q}") for q in range(2)]
        S.dma("sp", sl_t, mown, mown_d[:, :], writes=[Btab])
        S.dma("sp", sl_t, moth, moth_d[:, :], writes=[Btab])
        S.dma("sp", sl_t, cosA, cosA_d[:, :, :], writes=[Btab])
        S.dma("sp", sl_t, sinA, sinA_d[:, :, :], writes=[Btab])
        S.dma("sp", sl_t, cosB, cosB_d[:, :, :], writes=[Btab])
        S.dma("sp", sl_t, sinB, sinB_d[:, :, :], writes=[Btab])
        S.op("dve", lambda e: e.memset(QT, 0.0), writes=BQT)
        S.op("dve", lambda e: e.memset(Vaug[:, :, :, 128:130], 1.0), writes=BV)

        if STOP == 10:
            return finish_early()
        pbank = [0]

        def next_pbank():
            b = (0, 1, 2, 4)[pbank[0] % 4]
            pbank[0] += 1
            return b

        trh = [0]

        def proj(bk, t, wi):
            for c in range(16):
                S.op("pe", lambda e: e.matmul(psb[bk][:], hT[:, c, t * 128:(t + 1) * 128], wslot[wi][:, c, :],
                                              start=(c == 0), stop=(c == 15)),
                     reads=[BhT[t], Bw[wi]], writes=[Bps[bk]], sig=(c == 15))

        def rope_evac(bk, t, typ, dst, Bdst):
            if typ == "A":
                nh, hd, r2, ct, stb = 4, 128, 16, cosA, sinA
            else:
                nh, hd, r2, ct, stb = 8, 64, 8, cosB, sinB
            src = psb[bk][:].rearrange("p (h d) -> p h d", d=hd)
            dv = dst.rearrange("p (h d) -> p h d", d=hd)
            rs_ = rsrc[:, 0:nh * 2 * r2].rearrange("p (h d) -> p h d", d=2 * r2)
            t1 = rt1[:, 0:nh * 2 * r2].rearrange("p (h d) -> p h d", d=2 * r2)
            t2 = rt2[:, 0:nh * 2 * r2].rearrange("p (h d) -> p h d", d=2 * r2)
            cb = bcast(ct[:, t, :], nh, 1)
            sb_ = bcast(stb[:, t, :], nh, 1)
            S.op("act", lambda e: e.copy(dv[:, :, 2 * r2:hd], src[:, :, 2 * r2:hd]), reads=[Bps[bk]], writes=[Bdst])
            S.op("act", lambda e: e.copy(rs_, src[:, :, 0:2 * r2]), reads=[Bps[bk]], writes=[Brs])
            if STOP == 16:
                return
            S.op("dve", lambda e: e.tensor_tensor(t1[:, :, 0:r2], rs_[:, :, 0:r2], cb, ALU.mult), reads=[Brs, Btab], writes=[Brt])
            if STOP == 17:
                return
            S.op("dve", lambda e: e.tensor_tensor(t1[:, :, r2:2 * r2], rs_[:, :, r2:2 * r2], cb, ALU.mult), reads=[Brs, Btab], writes=[Brt])
            S.op("dve", lambda e: e.tensor_tensor(t2[:, :, 0:r2], rs_[:, :, r2:2 * r2], sb_, ALU.mult), reads=[Brs, Btab], writes=[Brt])
            S.op("dve", lambda e: e.tensor_tensor(t2[:, :, r2:2 * r2], rs_[:, :, 0:r2], sb_, ALU.mult), reads=[Brs, Btab], writes=[Brt])
            S.op("dve", lambda e: e.tensor_tensor(dv[:, :, 0:r2], t1[:, :, 0:r2], t2[:, :, 0:r2], ALU.subtract), reads=[Brt], writes=[Bdst])
            S.op("dve", lambda e: e.tensor_tensor(dv[:, :, r2:2 * r2], t1[:, :, r2:2 * r2], t2[:, :, r2:2 * r2], ALU.add), reads=[Brt], writes=[Bdst])

        psbf = [psb[i][:].bitcast(BF16) for i in range(8)]

        def transpose4(srcap, Bsrc, bank=3):
            for h in range(4):
                S.op("pe", lambda e: e.transpose(psbf[bank][:, h * 128:(h + 1) * 128],
                                                 srcap[:, h * 128:(h + 1) * 128], identb),
                     reads=[Bsrc, Bconst], writes=[Bps[bank]], sig=(h == 3))
            return psbf[bank][:, 0:512]

        scaleA = 128.0 ** -0.5
        scaleB = 64.0 ** -0.5
        oset_ctr = [0]
        sb_ctr = [0]

        for G in range(4):
            typ, gi = GROUPS[G]
            U = in_units(G)
            if G == 2:
                S.op("dve", lambda e: e.memset(QT, 0.0), writes=BQT)
            wi_k = pending_units.pop((G, "k"))
            wi_v = pending_units.pop((G, "v"))
            def k_tail(t):
                tmi = t % 2
                pv = transpose4(tm[tmi], Btm[tmi], 3)
                S.op("act", lambda e: e.copy(KT[:, :, t * 128:(t + 1) * 128], pv.rearrange("p (h k) -> p h k", k=128)),
                     reads=[Bps[3]], writes=[BKT[t]])

            for t in range(NT + 1):
                if t < NT:
                    bk = next_pbank()
                    proj(bk, t, wi_k)
                    rope_evac(bk, t, typ, tm[t % 2], Btm[t % 2])
                if t >= 1:
                    k_tail(t - 1)
            if STOP in (11, 14, 15, 16, 17):
                return finish_early()
            wi_q = load_unit(in_unit_ap(U["q"]))
            for t in range(NT):
                bk = next_pbank()
                proj(bk, t, wi_v)
                S.op("act", lambda e: e.copy(Vaug[:, t, :, 0:128], psb[bk][:].rearrange("p (h d) -> p h d", d=128)),
                     reads=[Bps[bk]], writes=[BV[t]])
            if G + 1 < 4:
                pending_units[(G + 1, "k")] = load_unit(in_unit_ap(in_units(G + 1)["k"]))
            def q_tail(t):
                tmi = t % 2
                pvf = transpose4(tm[tmi], Btm[tmi], 3)
                pv = pvf.rearrange("p (h k) -> p h k", k=128)
                if typ == "A":
                    S.op("act", lambda e: e.copy(QT[:, 0, :, t * 128:(t + 1) * 128], pv), reads=[Bps[3]], writes=[BQT[t]])
                else:
                    S.op("act", lambda e: e.copy(QT[0:64, 0, :, t * 128:(t + 1) * 128], pv[0:64]), reads=[Bps[3]], writes=[BQT[t]])
                    S.op("act", lambda e: e.copy(QT[64:128, 1, :, t * 128:(t + 1) * 128], pv[64:128]), reads=[Bps[3]], writes=[BQT[t]])

            for t in range(NQ + 1):
                if t < NQ:
                    bk = next_pbank()
                    proj(bk, t, wi_q)
                    rope_evac(bk, t, typ, tm[t % 2], Btm[t % 2])
                if t >= 1:
                    q_tail(t - 1)
            if G + 1 < 4:
                pending_units[(G + 1, "v")] = load_unit(in_unit_ap(in_units(G + 1)["v"]))
            else:
                pending_units["o0"] = load_unit(w_out_v[:, :, 0:512])
                pending_units["o1"] = load_unit(w_out_v[:, :, 512:1024])

            if STOP == 13:
                return finish_early()
            maps = [0] if typ == "A" else [0, 1]
            its = [(hh, qb, m, kt) for hh in range(4) for qb in range(2) for m in maps for kt in range(NT)]
            scale = scaleA if typ == "A" else scaleB
            state = {}

            def issue_S(i):
                hh, qb, m, kt = its[i]
                sbk = sb_ctr[0] % 4
                sb_ctr[0] += 1
                pi = i % NPT
                S.op("pe", lambda e: e.matmul(psb[sbk][:], KT[:, hh, kt * 128:(kt + 1) * 128],
                                              QT[:, m, hh, qb * 512:(qb + 1) * 512], start=True, stop=True),
                     reads=[BKT[kt]] + BQT[qb * 4:qb * 4 + 4], writes=[Bps[sbk]], sig=True)
                S.op("act", lambda e: e.activation(pt[pi], psb[sbk][:], AF.Exp, scale=scale), reads=[Bps[sbk]], writes=[Bpt[pi]])
                if typ == "A":
                    u = qb * 512 - 128 * kt
                    if kt < 8:
                        msk = mown[:, u + 896:u + 896 + 512]
                    else:
                        msk = moth[:, u + 1920:u + 1920 + 512]
                    S.op("dve", lambda e: e.tensor_tensor(pt[pi], pt[pi], msk, ALU.mult), reads=[Bpt[pi], Btab], writes=[Bpt[pi]])

            def issue_PV(i):
                hh, qb, m, kt = its[i]
                state["i"] = i
                pi = i % NPT
                if kt == 0:
                    state["os"] = oset_ctr[0] % 2
                    oset_ctr[0] += 1
                os_ = state["os"]
                banks = (4 + 2 * os_, 5 + 2 * os_)
                for j in range(4):
                    bk = banks[j // 2]
                    col = (j % 2) * 130
                    S.op("pe", lambda e: e.matmul(psb[bk][:, col:col + 129], pt[pi][:, j * 128:(j + 1) * 128],
                                                  Vaug[:, kt, hh, 0:129], start=(kt == 0 and j % 2 == 0),
                                                  stop=(kt == NT - 1 and j % 2 == 1)),
                         reads=[Bpt[pi], BV[kt]], writes=[Bps[bk]], sig=(j == 3))
                if kt == NT - 1:
                    epilogue(hh, qb, m, banks)

            def epilogue(hh, qb, m, banks):
                head = (0 if typ == "A" else 8) + gi * 4 + hh
                ov = [psb[b][:, 0:260].rearrange("p (j c) -> p j c", c=130) for b in banks]
                for bi in range(2):
                    S.op("dve", lambda e: e.reciprocal(rden[:, 2 * bi:2 * bi + 2].rearrange("p (j o) -> p j o", o=1), ov[bi][:, :, 128:129]),
                         reads=[Bps[banks[bi]]], writes=[Bep])
                yi = oset_ctr[0] % 2
                if typ == "A":
                    for bi in range(2):
                        S.op("dve", lambda e: e.tensor_tensor(ytm[yi][:, 2 * bi:2 * bi + 2, :], ov[bi][:, :, 0:128],
                                                              bcast(rden[:, 2 * bi:2 * bi + 2], 128, 2), ALU.mult),
                             reads=[Bps[banks[bi]], Bep], writes=[Bytm[yi]])
                elif m == 0:
                    for bi in range(2):
                        S.op("dve", lambda e: e.tensor_tensor(o1n[:, 2 * bi:2 * bi + 2, :], ov[bi][:, :, 0:128],
                                                              bcast(rden[:, 2 * bi:2 * bi + 2], 128, 2), ALU.mult),
                             reads=[Bps[banks[bi]], Bep], writes=[Bo1n])
                    return
                else:
                    for bi in range(2):
                        S.op("dve", lambda e: e.tensor_tensor(yb[:, 2 * bi:2 * bi + 2, :], ov[bi][:, :, 0:128],
                                                              bcast(rden[:, 2 * bi:2 * bi + 2], 128, 2), ALU.mult),
                             reads=[Bps[banks[bi]], Bep], writes=[Bep])
                    S.op("dve", lambda e: e.scalar_tensor_tensor(yb, yb, neglam[:, 0:1], o1n, ALU.mult, ALU.add),
                         reads=[Bep, Bo1n, Bconst], writes=[Bep])
                    S.op("dve", lambda e: e.tensor_tensor(o1n, yb, yb, ALU.mult), reads=[Bep, Bo1n], writes=[Bo1n])
                    S.op("dve", lambda e: e.reduce_sum(ssq, o1n, axis=AX.X), reads=[Bo1n], writes=[Bep])

                def part2():
                    if typ == "B":
                        S.op("act", lambda e: e.activation(ssq, ssq, AF.Sqrt, scale=1.0 / 128.0, bias=epsB[:, 0:1]), reads=[Bep, Bconst], writes=[Bep])
                        S.op("dve", lambda e: e.reciprocal(ssq, ssq), reads=[Bep], writes=[Bep])
                        S.op("dve", lambda e: e.tensor_scalar(ssq, ssq, 1.0 - LAM_INIT, None, ALU.mult), reads=[Bep], writes=[Bep])
                        S.op("dve", lambda e: e.tensor_tensor(yb, yb, bcast(ssq, 128, 2), ALU.mult), reads=[Bep], writes=[Bep])
                        S.op("dve", lambda e: e.tensor_tensor(ytm[yi], yb, bcast(subl, 4, 1), ALU.mult), reads=[Bep, Bconst], writes=[Bytm[yi]])
                    tb_ = sb_ctr[0] % 4
                    sb_ctr[0] += 1
                    pvf = transpose4(ytm[yi].rearrange("p j d -> p (j d)"), Bytm[yi], tb_)
                    S.op("dve", lambda e: e.tensor_copy(yT[:, head, qb * 512:(qb + 1) * 512], pvf),
                         reads=[Bps[tb_]], writes=[ByT[qb]])

                deferred.append((state["i"] + DEL, part2))

            DEL = 8
            deferred = []
            LA = 3
            for i in range(len(its) + LA):
                if i < len(its):
                    issue_S(i)
                if i - LA >= 0:
                    issue_PV(i - LA)
                    while deferred and deferred[0][0] <= i - LA:
                        deferred.pop(0)[1]()
            while deferred:
                deferred.pop(0)[1]()
            if STOP == 20 + G:
                return finish_early()
        S.barrier()
        if STOP == 2:
            return finish_early()

        x1 = view(R_A.base, [NQ, D], F32)
        Bx1 = [Buf(f"x1_{t}") for t in range(NQ)]
        for t in range(NQ):
            tkx = S.dma("sp", sl_x1, x1[:, t, :], xb_t[t], writes=[Bx1[t]])
        for t in range(NQ):
            Bx1[t].w = tkx
        obank = [0]
        for n in range(4):
            wi = pending_units.pop(f"o{n}")
            for t in range(NQ):
                bk = obank[0] % 8
                obank[0] += 1
                for c in range(16):
                    S.op("pe", lambda e: e.matmul(psb[bk][:], yT[:, c, t * 128:(t + 1) * 128], wslot[wi][:, c, :],
                                                  start=(c == 0), stop=(c == 15)),
                         reads=[ByT[t // 4], Bw[wi]], writes=[Bps[bk]], sig=(c == 15))
                S.op("dve", lambda e: e.tensor_tensor(x1[:, t, n * 512:(n + 1) * 512], x1[:, t, n * 512:(n + 1) * 512], psb[bk][:], ALU.add),
                     reads=[Bps[bk], Bx1[t]], writes=[Bx1[t]])
            if n + 2 < 4:
                pending_units[f"o{n + 2}"] = load_unit(w_out_v[:, :, (n + 2) * 512:(n + 3) * 512])
            elif n == 2:
                pending_units[("g", 0, 0)] = load_unit(w_gate_v[:, :, 0:512])
            else:
                pending_units[("u", 0, 0)] = load_unit(w_up_v[:, :, 0:512])
        S.barrier()
        if STOP == 3:
            return finish_early()

        R_S.reset()
        h2T = R_Y.base
        h2T = view(R_Y.base, [16, NQ * 128], BF16)
        Bh2T = [Buf(f"h2T{t}") for t in range(NQ)]
        aT = R_S.alloc([NF, 512], BF16)
        sg = [R_S.alloc([512], F32) for _ in range(4)]
        gfull = R_S.alloc([D], F32)
        junk = R_S.alloc([D], BF16)
        xs_b = [R_S.alloc([D], BF16) for _ in range(2)]
        Bg = Buf("gfull2")
        Bjunk = Buf("junk2")
        Bxs = [Buf("xs2_0"), Buf("xs2_1")]
        BaT = [Buf(f"aT{f}") for f in range(NF)]
        Bsg = [Buf(f"sg{i}") for i in range(4)]
        S.dma("sp", sl_g, gfull, g2_d[:, :], writes=[Bg])
        for t in range(NQ):
            norm_to_T(t, x1[:, t, :], Bx1[t], gfull, Bg, ss2, rs2, Bss2[t], xs_b[t % 2], Bxs[t % 2],
                      junk, Bjunk, h2T, Bh2T[t], t % 2)
        S.barrier(engines=("sp",))
        Bg3 = Buf("gfull3")
        S.dma("sp", sl_g, gfull, g3_d[:, :], writes=[Bg3])

        for blk in range(2):
            tb = slice(blk * 512, (blk + 1) * 512)
            for uu in range(11):
                wg = pending_units.pop(("g", blk, uu))
                wu = pending_units.pop(("u", blk, uu))
                for i in range(4):
                    f = uu * 4 + i
                    bk = f % 2
                    for c in range(16):
                        S.op("pe", lambda e: e.matmul(psb[bk][:], wslot[wg][:, c, i * 128:(i + 1) * 128], h2T[:, c, tb],
                                                      start=(c == 0), stop=(c == 15)),
                             reads=[Bw[wg]] + Bh2T[blk * 4:blk * 4 + 4], writes=[Bps[bk]], sig=(c == 15))
                    S.op("act", lambda e: e.activation(sg[i], psb[bk][:], AF.Silu), reads=[Bps[bk]], writes=[Bsg[i]])
                if uu + 1 < 11:
                    pending_units[("g", blk, uu + 1)] = load_unit(w_gate_v[:, :, (uu + 1) * 512:(uu + 2) * 512])
                else:
                    pending_units[("d", blk, 0)] = load_unit(w_down_v[:, 0:16, 0:512])
                for i in range(4):
                    f = uu * 4 + i
                    bk = 2 + f % 2
                    for c in range(16):
                        S.op("pe", lambda e: e.matmul(psb[bk][:], wslot[wu][:, c, i * 128:(i + 1) * 128], h2T[:, c, tb],
                                                      start=(c == 0), stop=(c == 15)),
                             reads=[Bw[wu]] + Bh2T[blk * 4:blk * 4 + 4], writes=[Bps[bk]], sig=(c == 15))
                    S.op("dve", lambda e: e.tensor_tensor(aT[:, f, :], sg[i], psb[bk][:], ALU.mult),
                         reads=[Bps[bk], Bsg[i]], writes=[BaT[f]])
                if uu + 1 < 11:
                    pending_units[("u", blk, uu + 1)] = load_unit(w_up_v[:, :, (uu + 1) * 512:(uu + 2) * 512])
                else:
                    pending_units[("d", blk, 1)] = load_unit(w_down_v[:, 16:32, 0:512])
            dunits = [(n, fu) for n in range(4) for fu in range(3)]
            frange = [(0, 16), (16, 32), (32, 44)]
            for di, (n, fu) in enumerate(dunits):
                wi = pending_units.pop(("d", blk, di))
                f0, f1 = frange[fu]
                for j in range(4):
                    bk = 4 + j
                    for f in range(f0, f1):
                        S.op("pe", lambda e: e.matmul(psb[bk][:], aT[:, f, j * 128:(j + 1) * 128], wslot[wi][:, f - f0, :],
                                                      start=(f == 0), stop=(f == NF - 1)),
                             reads=[BaT[f], Bw[wi]], writes=[Bps[bk]], sig=(f == f1 - 1))
                    if fu == 2:
                        T = blk * 4 + j
                        S.op("dve", lambda e: e.tensor_tensor(x1[:, T, n * 512:(n + 1) * 512], x1[:, T, n * 512:(n + 1) * 512], psb[bk][:], ALU.add),
                             reads=[Bps[bk], Bx1[T]], writes=[Bx1[T]])
                nx = di + 2
                if nx < len(dunits):
                    n2, fu2 = dunits[nx]
                    a0, a1 = frange[fu2]
                    pending_units[("d", blk, nx)] = load_unit(w_down_v[:, a0:a1, n2 * 512:(n2 + 1) * 512], nchunks=a1 - a0)
                elif blk == 0:
                    if nx == len(dunits):
                        pending_units[("g", 1, 0)] = load_unit(w_gate_v[:, :, 0:512])
                    else:
                        pending_units[("u", 1, 0)] = load_unit(w_up_v[:, :, 0:512])
            for j in range(4):
                T = blk * 4 + j
                src = x1[:, T, :]
                S.op("act", lambda e: e.activation(junk, src, AF.Square, accum_out=ss3[:, T:T + 1]),
                     reads=[Bx1[T], Bconst], writes=[Bjunk, Bss3[T]])
                S.op("act", lambda e: e.activation(rs3[:, T:T + 1], ss3[:, T:T + 1], AF.Sqrt, scale=1.0 / D, bias=epsA[:, 0:1]),
                     reads=[Bss3[T], Bconst], writes=[Bss3[T]])
                S.op("dve", lambda e: e.reciprocal(rs3[:, T:T + 1], rs3[:, T:T + 1]), reads=[Bss3[T]], writes=[Bss3[T]])
                S.op("dve", lambda e: e.scalar_tensor_tensor(src, src, rs3[:, T:T + 1], gfull, ALU.mult, ALU.mult),
                     reads=[Bx1[T], Bss3[T], Bg3], writes=[Bx1[T]])
                last = S.dma("sp", sl_o, out_t[T], src, reads=[Bx1[T]])
        S.wait_tok("sp", last)
        assert not pending_units, pending_units
    return nc


def _mult(d):
    d = np.asarray(d)
    m = (np.abs(d) <= 64).astype(np.float32)
    m += ((d % 4 == 0) & (np.abs(d) <= 256)).astype(np.float32)
    m += ((d % 16 == 0) & (np.abs(d) <= 1024)).astype(np.float32)
    return m


def _rope_tab(pos, rot_dim):
    inv = (np.float32(ROPE_THETA) ** (-np.arange(0, rot_dim, 2, dtype=np.float32) / np.float32(rot_dim))).astype(np.float32)
    ang = pos.astype(np.float32)[:, None] * inv[None, :]
    return np.cos(ang).astype(np.float32), np.sin(ang).astype(np.float32)


_NC_CACHE = {}


def kernel(x, norm_attn, w_in, lambda_qk, subln, w_out, norm_ffn, w_gate, w_up, w_down, norm_final):
    x = np.asarray(x, dtype=np.float32)
    f32c = lambda a: np.ascontiguousarray(np.asarray(a, dtype=np.float32))
    rep = lambda v: np.ascontiguousarray(np.broadcast_to(np.asarray(v, dtype=np.float32).reshape(1, -1), (128, np.asarray(v).size)))
    if "nc" not in _NC_CACHE:
        _NC_CACHE["nc"] = build_program()
    nc = _NC_CACHE["nc"]
    shared = dict(
        g1=rep(norm_attn[0]), g2=rep(norm_ffn[0]), g3=rep(norm_final),
        w_in=f32c(w_in[0]), w_out=f32c(w_out[0]), w_gate=f32c(w_gate[0]), w_up=f32c(w_up[0]), w_down=f32c(w_down[0]),
        lamq=rep(np.asarray(lambda_qk[0]).reshape(-1)), subl=rep(subln[0]),
        ident=np.eye(128, dtype=np.float32),
    )
    p = np.arange(128)[:, None]
    c = np.arange(1920)[None, :]
    mown = _mult(p - (c - 896)).astype(ml_dtypes.bfloat16)
    in_maps = []
    for core in range(8):
        b, hf = core // 2, core % 2
        own = np.arange(hf * 1024, (hf + 1) * 1024)
        oth = np.arange((1 - hf) * 1024, (2 - hf) * 1024)
        pos = np.concatenate([own, oth])
        xb = np.ascontiguousarray(x[b][pos])
        ca, sa = _rope_tab(pos, 32)
        cb, sb = _rope_tab(pos, 16)
        lay = lambda a: np.ascontiguousarray(a.reshape(NT, 128, -1).transpose(1, 0, 2))
        moth = _mult(p - (c - 1920) - 2048 * hf).astype(ml_dtypes.bfloat16)
        m = dict(shared)
        m.update(xb=xb, cosA=lay(ca), sinA=lay(sa), cosB=lay(cb), sinB=lay(sb), mown=mown, moth=moth)
        in_maps.append(m)
    if _NC_CACHE.get("prep_only"):
        return nc, in_maps
    res = run_bass_kernel_spmd(nc, in_maps, core_ids=list(range(8)))
    out = np.empty((4, SEQ, D), dtype=np.float32)
    for core in range(8):
        b, hf = core // 2, core % 2
        out[b, hf * 1024:(hf + 1) * 1024] = res.results[core]["out"]
    return out
```

```python
from contextlib import ExitStack
import math
import numpy as np
import ml_dtypes
import concourse.bass as bass
import concourse.mybir as mybir
from concourse.bass_utils import run_bass_kernel_spmd

F32 = mybir.dt.float32
BF16 = mybir.dt.bfloat16
AF = mybir.ActivationFunctionType
ALU = mybir.AluOpType
AX = mybir.AxisListType

D = 2048
SEQ = 2048
NT = 16
NQ = 8
DFF = 5632
NF = 44
LAM_INIT = 0.8 - 0.6 * math.exp(0.0)
ROPE_THETA = 500000.0


class Tok:
    __slots__ = ("sem", "val", "eng")

    def __init__(self, sem, val, eng):
        self.sem = sem
        self.val = val
        self.eng = eng


class Buf:
    __slots__ = ("name", "w", "rs", "excl")

    def __init__(self, name, excl=False):
        self.name = name
        self.w = None
        self.rs = {}
        self.excl = excl


class Sched:
    def __init__(self, nc, stack):
        self.nc = nc
        self.stack = stack
        self.E = {}
        for name, eng in (("pe", nc.tensor), ("act", nc.scalar), ("dve", nc.vector),
                          ("pool", nc.gpsimd), ("sp", nc.sync)):
            sem = stack.enter_context(nc.semaphore("s_" + name))
            self.E[name] = dict(eng=eng, sem=sem, cnt=0, waited={}, name=name)

    def new_sem(self, name):
        return self.stack.enter_context(self.nc.semaphore(name))

    def slot(self, name):
        return dict(sem=self.new_sem(name), cnt=0)

    def _wait(self, E, reads, writes):
        need = {}

        def add(tok, raw):
            if tok is None:
                return
            if tok.eng == E["name"]:
                if E["name"] in ("pe", "sp", "pool"):
                    return
            k = tok.sem.num
            if k not in need or need[k][1] < tok.val:
                need[k] = (tok.sem, tok.val, tok.eng)

        for b in reads:
            add(b.w, True)
            if b.excl:
                for t in b.rs.values():
                    if t.eng != E["name"]:
                        add(t, False)
        for b in writes:
            add(b.w, False)
            for t in b.rs.values():
                add(t, False)
        for k, (sem, val, en) in need.items():
            if E["waited"].get(k, 0) >= val:
                continue
            if en in self.E:
                assert self.E[en]["cnt"] >= val, f"wait on unflagged {en} {val}>{self.E[en]['cnt']}"
            E["eng"].wait_ge(sem, val)
            E["waited"][k] = val

    def op(self, en, fn, reads=(), writes=(), sig=True):
        E = self.E[en]
        self._wait(E, reads, writes)
        ins = fn(E["eng"])
        if en == "pe" and not sig:
            tok = Tok(E["sem"], E["cnt"] + 1, en)
        else:
            ins.then_inc(E["sem"], 1)
            E["cnt"] += 1
            tok = Tok(E["sem"], E["cnt"], en)
        for b in reads:
            b.rs[en] = tok
        for b in writes:
            b.w = tok
            b.rs = {}
        return tok

    def dma(self, en, slot, out, in_, reads=(), writes=()):
        E = self.E[en]
        self._wait(E, reads, writes)
        ins = E["eng"].dma_start(out=out, in_=in_)
        ins.then_inc(slot["sem"], 16)
        slot["cnt"] += 16
        tok = Tok(slot["sem"], slot["cnt"], "dma")
        for b in reads:
            b.rs["dma%d" % slot["sem"].num] = tok
        for b in writes:
            b.w = tok
            b.rs = {}
        return tok

    def wait_tok(self, en, tok):
        E = self.E[en]
        if E["waited"].get(tok.sem.num, 0) < tok.val:
            E["eng"].wait_ge(tok.sem, tok.val)
            E["waited"][tok.sem.num] = tok.val

    def barrier(self, engines=("pe", "act", "dve", "sp")):
        for en in engines:
            for e2 in ("pe", "act", "dve"):
                if e2 == en:
                    continue
                E2 = self.E[e2]
                if E2["cnt"] > 0:
                    self.wait_tok(en, Tok(E2["sem"], E2["cnt"], e2))


def bcast(ap, n, pos):
    l = [list(x) for x in ap.ap]
    l.insert(pos, [0, n])
    return bass.AP(ap.tensor, ap.offset, l)


import os
STOP = int(os.environ.get("K_STOP", "99"))


def build_program():
    nc = bass.Bass("TRN2", target_bir_lowering=False)
    dt_in = lambda name, shape, dt=F32: nc.dram_tensor(name, shape, dt, kind="ExternalInput").ap()
    xb_d = dt_in("xb", [SEQ, D])
    g1_d = dt_in("g1", [128, D])
    g2_d = dt_in("g2", [128, D])
    g3_d = dt_in("g3", [128, D])
    w_in_d = dt_in("w_in", [D, 6144])
    w_out_d = dt_in("w_out", [D, D])
    w_gate_d = dt_in("w_gate", [D, DFF])
    w_up_d = dt_in("w_up", [D, DFF])
    w_down_d = dt_in("w_down", [DFF, D])
    cosA_d = dt_in("cosA", [128, NT, 16])
    sinA_d = dt_in("sinA", [128, NT, 16])
    cosB_d = dt_in("cosB", [128, NT, 8])
    sinB_d = dt_in("sinB", [128, NT, 8])
    mown_d = dt_in("mown", [128, 1920], BF16)
    moth_d = dt_in("moth", [128, 1920], BF16)
    lamq_d = dt_in("lamq", [128, 256])
    subl_d = dt_in("subl", [128, 128])
    ident_d = dt_in("ident", [128, 128])
    out_d = nc.dram_tensor("out", [NQ * 128, D], F32, kind="ExternalOutput").ap()

    xb_t = xb_d.rearrange("(t p) d -> t p d", p=128)
    out_t = out_d.rearrange("(t p) d -> t p d", p=128)
    w_in_v = w_in_d.rearrange("(c p) n -> p c n", p=128)
    w_out_v = w_out_d.rearrange("(c p) n -> p c n", p=128)
    w_gate_v = w_gate_d.rearrange("(c p) n -> p c n", p=128)
    w_up_v = w_up_d.rearrange("(c p) n -> p c n", p=128)
    w_down_v = w_down_d.rearrange("(f p) n -> p f n", p=128)

    with ExitStack() as st:
        S = Sched(nc, st)
        ARENA_BYTES = 206 * 1024
        arena = st.enter_context(nc.sbuf_tensor("arena", [128, ARENA_BYTES // 2], BF16))

        def view(off, shape, dt):
            n = 1
            for s_ in shape:
                n *= s_
            esz = 4 if dt == F32 else 2
            assert off % 4 == 0 and off + n * esz <= ARENA_BYTES, (off, shape)
            ap = arena[:, off // 2:(off + n * esz) // 2]
            if dt == F32:
                ap = ap.bitcast(F32)
            if len(shape) == 2:
                ap = ap.rearrange("p (a b) -> p a b", b=shape[1])
            elif len(shape) == 3:
                ap = ap.rearrange("p (a b c) -> p a b c", b=shape[1], c=shape[2])
            elif len(shape) == 4:
                ap = ap.rearrange("p (a b c d) -> p a b c d", b=shape[1], c=shape[2], d=shape[3])
            return ap

        class Bump:
            def __init__(self, base, size):
                self.base, self.size, self.off = base, size, 0

            def reset(self):
                self.off = 0

            def alloc(self, shape, dt):
                n = 1
                for s_ in shape:
                    n *= s_
                sz = (n * (4 if dt == F32 else 2) + 3) // 4 * 4
                assert self.off + sz <= self.size, ("region overflow", self.off, sz, self.size)
                v = view(self.base + self.off, shape, dt)
                self.off += sz
                return v

        o = 0
        R_P = Bump(o, 4096); o += 4096
        R_A = Bump(o, 65536); o += 65536
        R_W = Bump(o, 32768); o += 32768
        R_Y = Bump(o, 32768); o += 32768
        R_S = Bump(o, ARENA_BYTES - o)
        assert R_S.size >= 73000, R_S.size

        psb = [st.enter_context(nc.psum_tensor(f"psb{i}", [128, 512], F32)) for i in range(8)]
        Bps = [Buf(f"ps{i}", excl=True) for i in range(8)]
        ps3b = psb[3][:].bitcast(BF16)
        Bps3 = [Bps[3], Bps[3]]

        identf = R_P.alloc([128], F32)
        identb = R_P.alloc([128], BF16)
        epsA = R_P.alloc([1], F32)
        epsB = R_P.alloc([1], F32)
        lamq = R_P.alloc([256], F32)
        lprod = R_P.alloc([2, 64], F32)
        ldots = R_P.alloc([2], F32)
        lex = R_P.alloc([2], F32)
        neglam = R_P.alloc([1], F32)
        subl = R_P.alloc([128], F32)
        ss1 = R_P.alloc([NT], F32)
        rs1 = R_P.alloc([NT], F32)
        ss2 = R_P.alloc([NQ], F32)
        rs2 = R_P.alloc([NQ], F32)
        ss3 = R_P.alloc([NQ], F32)
        rs3 = R_P.alloc([NQ], F32)
        Bconst = Buf("const")
        Bss1 = [Buf(f"ss1_{t}") for t in range(NT)]
        Bss2 = [Buf(f"ss2_{t}") for t in range(NQ)]
        Bss3 = [Buf(f"ss3_{t}") for t in range(NQ)]

        wslot = [view(R_W.base + i * 16384, [16, 512], BF16) for i in range(2)]
        Bw = [Buf("w0"), Buf("w1")]
        sl_w = [S.slot("ldw0"), S.slot("ldw1")]
        wctr = [0]

        def load_unit(src_ap, nchunks=16):
            i = wctr[0] % 2
            wctr[0] += 1
            S.dma("pool", sl_w[i], wslot[i][:, 0:nchunks, :], src_ap, writes=[Bw[i]])
            return i

        sl_c = S.slot("ldc")
        sl_g = S.slot("ldg")
        sl_t = S.slot("ldt")
        sl_x1 = S.slot("ldxres")
        sl_x = [S.slot("ldx0"), S.slot("ldx1")]
        sl_o = S.slot("sto")

        def finish_early():
            S.barrier()
            tkk = S.dma("sp", sl_o, out_t[0], view(R_A.base, [D], F32))
            S.wait_tok("sp", tkk)
            return nc

        S.dma("sp", sl_c, identf, ident_d[:, :], writes=[Bconst])
        S.dma("sp", sl_c, lamq, lamq_d[:, :], writes=[Bconst])
        tk = S.dma("sp", sl_c, subl, subl_d[:, :], writes=[Bconst])
        S.op("dve", lambda e: e.tensor_copy(identb, identf), reads=[Bconst], writes=[Bconst])
        S.op("dve", lambda e: e.memset(epsA, 1e-6), writes=[Bconst])
        S.op("dve", lambda e: e.memset(epsB, 1e-5), writes=[Bconst])
        for tl in (ss1, ss2, ss3):
            S.op("dve", lambda e: e.memset(tl, 0.0), writes=[Bconst])
        lqv = lamq.rearrange("p (a b d) -> p a b d", a=2, b=2, d=64)
        S.op("dve", lambda e: e.tensor_tensor(lprod, lqv[:, :, 0, :], lqv[:, :, 1, :], ALU.mult), reads=[Bconst], writes=[Bconst])
        S.op("dve", lambda e: e.reduce_sum(ldots, lprod, axis=AX.X), reads=[Bconst], writes=[Bconst])
        S.op("act", lambda e: e.activation(lex, ldots, AF.Exp), reads=[Bconst], writes=[Bconst])
        S.op("dve", lambda e: e.tensor_tensor(neglam, lex[:, 1:2], lex[:, 0:1], ALU.subtract), reads=[Bconst], writes=[Bconst])
        S.op("dve", lambda e: e.tensor_scalar(neglam, neglam, -LAM_INIT, None, ALU.add), reads=[Bconst], writes=[Bconst])

        def norm_head(t, src, Bsrc, gfull, Bg, ss, rs, Bss, xs, Bxs, junk, Bjunk):
            S.op("act", lambda e: e.activation(junk, src, AF.Square, accum_out=ss[:, t:t + 1]),
                 reads=[Bsrc, Bconst], writes=[Bjunk, Bss])
            S.op("act", lambda e: e.activation(rs[:, t:t + 1], ss[:, t:t + 1], AF.Sqrt, scale=1.0 / D, bias=epsA[:, 0:1]),
                 reads=[Bss, Bconst], writes=[Bss])
            S.op("dve", lambda e: e.reciprocal(rs[:, t:t + 1], rs[:, t:t + 1]), reads=[Bss], writes=[Bss])
            S.op("dve", lambda e: e.scalar_tensor_tensor(xs, src, rs[:, t:t + 1], gfull, ALU.mult, ALU.mult),
                 reads=[Bsrc, Bss, Bg], writes=[Bxs])

        def norm_tail(t, xs, Bxs, dstT, BdstT, par):
            for hb in range(2):
                bk = 2 * par + hb
                pv = psb[bk][:].bitcast(BF16)
                for c8 in range(8):
                    c = hb * 8 + c8
                    S.op("pe", lambda e: e.transpose(pv[:, c8 * 128:(c8 + 1) * 128], xs[:, c * 128:(c + 1) * 128], identb),
                         reads=[Bxs, Bconst], writes=[Bps[bk]], sig=(c8 == 7))
                dst = dstT[:, hb * 8:hb * 8 + 8, t * 128:(t + 1) * 128]
                srcv = pv.rearrange("p (a b) -> p a b", b=128)
                if hb == 0:
                    S.op("act", lambda e: e.copy(dst, srcv), reads=[Bps[bk]], writes=[BdstT])
                else:
                    S.op("dve", lambda e: e.tensor_copy(dst, srcv), reads=[Bps[bk]], writes=[BdstT])

        hT = R_A.alloc([16, SEQ], BF16)
        BhT = [Buf(f"hT{t}") for t in range(NT)]
        R_S.reset()
        xin = [R_S.alloc([D], F32) for _ in range(2)]
        xs_b = [R_S.alloc([D], BF16) for _ in range(2)]
        gfull = R_S.alloc([D], F32)
        junk = R_S.alloc([D], BF16)
        Bxin = [Buf("xin0"), Buf("xin1")]
        Bxs = [Buf("xs0"), Buf("xs1")]
        Bg = Buf("gfull")
        Bjunk = Buf("junk")
        S.dma("sp", sl_g, gfull, g1_d[:, :], writes=[Bg])
        GROUPS = [("A", 0), ("A", 1), ("B", 0), ("B", 1)]

        def in_units(G):
            typ, gi = GROUPS[G]
            base = 0 if typ == "A" else 6
            return dict(q=base + gi, k=base + 2 + gi, v=base + 4 + gi)

        def in_unit_ap(n):
            return w_in_v[:, :, n * 512:(n + 1) * 512]

        pending_units = {}
        pending_units[(0, "k")] = load_unit(in_unit_ap(in_units(0)["k"]))
        pending_units[(0, "v")] = load_unit(in_unit_ap(in_units(0)["v"]))

        for t in range(NT + 1):
            if t < NT:
                S.dma("sp", sl_x[t % 2], xin[t % 2], xb_t[t], writes=[Bxin[t % 2]])
                norm_head(t, xin[t % 2], Bxin[t % 2], gfull, Bg, ss1, rs1, Bss1[t], xs_b[t % 2], Bxs[t % 2], junk, Bjunk)
            if t >= 1:
                norm_tail(t - 1, xs_b[(t - 1) % 2], Bxs[(t - 1) % 2], hT, BhT[t - 1], (t - 1) % 2)
        S.barrier()
        if STOP == 1:
            return finish_early()

        R_S.reset()
        QT = R_S.alloc([2, 4, NQ * 128], BF16)
        KT = R_S.alloc([4, SEQ], BF16)
        Vaug = R_S.alloc([NT, 4, 130], BF16)
        mown = R_S.alloc([1920], BF16)
        moth = R_S.alloc([1920], BF16)
        cosA = R_S.alloc([NT, 16], F32)
        sinA = R_S.alloc([NT, 16], F32)
        cosB = R_S.alloc([NT, 8], F32)
        sinB = R_S.alloc([NT, 8], F32)
        tm = [R_S.alloc([512], BF16) for _ in range(2)]
        NPT = 4
        pt = [R_S.alloc([512], BF16) for _ in range(NPT)]
        rt1 = R_S.alloc([128], F32)
        rt2 = R_S.alloc([128], F32)
        rsrc = R_S.alloc([128], F32)
        rden = R_S.alloc([4], F32)
        o1n = R_S.alloc([4, 128], F32)
        yb = R_S.alloc([4, 128], F32)
        ssq = R_S.alloc([4], F32)
        ytm = [R_S.alloc([4, 128], BF16) for _ in range(2)]
        yT = R_Y.alloc([16, NQ * 128], BF16)
        BQT = [Buf(f"QT{t}") for t in range(NQ)]
        BKT = [Buf(f"KT{t}") for t in range(NT)]
        BV = [Buf(f"V{t}") for t in range(NT)]
        Btab = Buf("tables")
        Btm = [Buf("tm0"), Buf("tm1")]
        Bpt = [Buf(f"pt{i}") for i in range(NPT)]
        Brt = Buf("rt")
        Brs = Buf("rsrc")
        Bep = Buf("ep")
        Bo1n = Buf("o1n")
        Bytm = [Buf("ytm0"), Buf("ytm1")]
        ByT = [Buf(f"yT{q}") for q in range(2)]
        S.dma("sp", sl_t, mown, mown_d[:, :], writes=[Btab])
        S.dma("sp", sl_t, moth, moth_d[:, :], writes=[Btab])
        S.dma("sp", sl_t, cosA, cosA_d[:, :, :], writes=[Btab])
        S.dma("sp", sl_t, sinA, sinA_d[:, :, :], writes=[Btab])
        S.dma("sp", sl_t, cosB, cosB_d[:, :, :], writes=[Btab])
        S.dma("sp", sl_t, sinB, sinB_d[:, :, :], writes=[Btab])
        S.op("dve", lambda e: e.memset(QT, 0.0), writes=BQT)
        S.op("dve", lambda e: e.memset(Vaug[:, :, :, 128:130], 1.0), writes=BV)

        if STOP == 10:
            return finish_early()
        pbank = [0]

        def next_pbank():
            b = (0, 1, 2, 4)[pbank[0] % 4]
            pbank[0] += 1
            return b

        trh = [0]

        def proj(bk, t, wi):
            for c in range(16):
                S.op("pe", lambda e: e.matmul(psb[bk][:], hT[:, c, t * 128:(t + 1) * 128], wslot[wi][:, c, :],
                                              start=(c == 0), stop=(c == 15)),
                     reads=[BhT[t], Bw[wi]], writes=[Bps[bk]], sig=(c == 15))

        def rope_evac(bk, t, typ, dst, Bdst):
            if typ == "A":
                nh, hd, r2, ct, stb = 4, 128, 16, cosA, sinA
            else:
                nh, hd, r2, ct, stb = 8, 64, 8, cosB, sinB
            src = psb[bk][:].rearrange("p (h d) -> p h d", d=hd)
            dv = dst.rearrange("p (h d) -> p h d", d=hd)
            rs_ = rsrc[:, 0:nh * 2 * r2].rearrange("p (h d) -> p h d", d=2 * r2)
            t1 = rt1[:, 0:nh * 2 * r2].rearrange("p (h d) -> p h d", d=2 * r2)
            t2 = rt2[:, 0:nh * 2 * r2].rearrange("p (h d) -> p h d", d=2 * r2)
            cb = bcast(ct[:, t, :], nh, 1)
            sb_ = bcast(stb[:, t, :], nh, 1)
            S.op("act", lambda e: e.copy(dv[:, :, 2 * r2:hd], src[:, :, 2 * r2:hd]), reads=[Bps[bk]], writes=[Bdst])
            S.op("act", lambda e: e.copy(rs_, src[:, :, 0:2 * r2]), reads=[Bps[bk]], writes=[Brs])
            if STOP == 16:
                return
            S.op("dve", lambda e: e.tensor_tensor(t1[:, :, 0:r2], rs_[:, :, 0:r2], cb, ALU.mult), reads=[Brs, Btab], writes=[Brt])
            if STOP == 17:
                return
            S.op("dve", lambda e: e.tensor_tensor(t1[:, :, r2:2 * r2], rs_[:, :, r2:2 * r2], cb, ALU.mult), reads=[Brs, Btab], writes=[Brt])
            S.op("dve", lambda e: e.tensor_tensor(t2[:, :, 0:r2], rs_[:, :, r2:2 * r2], sb_, ALU.mult), reads=[Brs, Btab], writes=[Brt])
            S.op("dve", lambda e: e.tensor_tensor(t2[:, :, r2:2 * r2], rs_[:, :, 0:r2], sb_, ALU.mult), reads=[Brs, Btab], writes=[Brt])
            S.op("dve", lambda e: e.tensor_tensor(dv[:, :, 0:r2], t1[:, :, 0:r2], t2[:, :, 0:r2], ALU.subtract), reads=[Brt], writes=[Bdst])
            S.op("dve", lambda e: e.tensor_tensor(dv[:, :, r2:2 * r2], t1[:, :, r2:2 * r2], t2[:, :, r2:2 * r2], ALU.add), reads=[Brt], writes=[Bdst])

        psbf = [psb[i][:].bitcast(BF16) for i in range(8)]

        def transpose4(srcap, Bsrc, bank=3):
            for h in range(4):
                S.op("pe", lambda e: e.transpose(psbf[bank][:, h * 128:(h + 1) * 128],
                                                 srcap[:, h * 128:(h + 1) * 128], identb),
                     reads=[Bsrc, Bconst], writes=[Bps[bank]], sig=(h == 3))
            return psbf[bank][:, 0:512]

        scaleA = 128.0 ** -0.5
        scaleB = 64.0 ** -0.5
        oset_ctr = [0]
        sb_ctr = [0]

        for G in range(4):
            typ, gi = GROUPS[G]
            U = in_units(G)
            if G == 2:
                S.op("dve", lambda e: e.memset(QT, 0.0), writes=BQT)
            wi_k = pending_units.pop((G, "k"))
            wi_v = pending_units.pop((G, "v"))
            def k_tail(t):
                tmi = t % 2
                pv = transpose4(tm[tmi], Btm[tmi], 3)
                S.op("act", lambda e: e.copy(KT[:, :, t * 128:(t + 1) * 128], pv.rearrange("p (h k) -> p h k", k=128)),
                     reads=[Bps[3]], writes=[BKT[t]])

            for t in range(NT + 1):
                if t < NT:
                    bk = next_pbank()
                    proj(bk, t, wi_k)
                    rope_evac(bk, t, typ, tm[t % 2], Btm[t % 2])
                if t >= 1:
                    k_tail(t - 1)
            if STOP in (11, 14, 15, 16, 17):
                return finish_early()
            wi_q = load_unit(in_unit_ap(U["q"]))
            for t in range(NT):
                bk = next_pbank()
                proj(bk, t, wi_v)
                S.op("act", lambda e: e.copy(Vaug[:, t, :, 0:128], psb[bk][:].rearrange("p (h d) -> p h d", d=128)),
                     reads=[Bps[bk]], writes=[BV[t]])
            if G + 1 < 4:
                pending_units[(G + 1, "k")] = load_unit(in_unit_ap(in_units(G + 1)["k"]))
            def q_tail(t):
                tmi = t % 2
                pvf = transpose4(tm[tmi], Btm[tmi], 3)
                pv = pvf.rearrange("p (h k) -> p h k", k=128)
                if typ == "A":
                    S.op("act", lambda e: e.copy(QT[:, 0, :, t * 128:(t + 1) * 128], pv), reads=[Bps[3]], writes=[BQT[t]])
                else:
                    S.op("act", lambda e: e.copy(QT[0:64, 0, :, t * 128:(t + 1) * 128], pv[0:64]), reads=[Bps[3]], writes=[BQT[t]])
                    S.op("act", lambda e: e.copy(QT[64:128, 1, :, t * 128:(t + 1) * 128], pv[64:128]), reads=[Bps[3]], writes=[BQT[t]])

            for t in range(NQ + 1):
                if t < NQ:
                    bk = next_pbank()
                    proj(bk, t, wi_q)
                    rope_evac(bk, t, typ, tm[t % 2], Btm[t % 2])
                if t >= 1:
                    q_tail(t - 1)
            if G + 1 < 4:
                pending_units[(G + 1, "v")] = load_unit(in_unit_ap(in_units(G + 1)["v"]))
            else:
                pending_units["o0"] = load_unit(w_out_v[:, :, 0:512])
                pending_units["o1"] = load_unit(w_out_v[:, :, 512:1024])

            if STOP == 13:
                return finish_early()
            maps = [0] if typ == "A" else [0, 1]
            its = [(hh, qb, m, kt) for hh in range(4) for qb in range(2) for m in maps for kt in range(NT)]
            scale = scaleA if typ == "A" else scaleB
            state = {}

            def issue_S(i):
                hh, qb, m, kt = its[i]
                sbk = sb_ctr[0] % 4
                sb_ctr[0] += 1
                pi = i % NPT
                S.op("pe", lambda e: e.matmul(psb[sbk][:], KT[:, hh, kt * 128:(kt + 1) * 128],
                                              QT[:, m, hh, qb * 512:(qb + 1) * 512], start=True, stop=True),
                     reads=[BKT[kt]] + BQT[qb * 4:qb * 4 + 4], writes=[Bps[sbk]], sig=True)
                S.op("act", lambda e: e.activation(pt[pi], psb[sbk][:], AF.Exp, scale=scale), reads=[Bps[sbk]], writes=[Bpt[pi]])
                if typ == "A":
                    u = qb * 512 - 128 * kt
                    if kt < 8:
                        msk = mown[:, u + 896:u + 896 + 512]
                    else:
                        msk = moth[:, u + 1920:u + 1920 + 512]
                    S.op("dve", lambda e: e.tensor_tensor(pt[pi], pt[pi], msk, ALU.mult), reads=[Bpt[pi], Btab], writes=[Bpt[pi]])

            def issue_PV(i):
                hh, qb, m, kt = its[i]
                state["i"] = i
                pi = i % NPT
                if kt == 0:
                    state["os"] = oset_ctr[0] % 2
                    oset_ctr[0] += 1
                os_ = state["os"]
                banks = (4 + 2 * os_, 5 + 2 * os_)
                for j in range(4):
                    bk = banks[j // 2]
                    col = (j % 2) * 130
                    S.op("pe", lambda e: e.matmul(psb[bk][:, col:col + 129], pt[pi][:, j * 128:(j + 1) * 128],
                                                  Vaug[:, kt, hh, 0:129], start=(kt == 0 and j % 2 == 0),
                                                  stop=(kt == NT - 1 and j % 2 == 1)),
                         reads=[Bpt[pi], BV[kt]], writes=[Bps[bk]], sig=(j == 3))
                if kt == NT - 1:
                    epilogue(hh, qb, m, banks)

            def epilogue(hh, qb, m, banks):
                head = (0 if typ == "A" else 8) + gi * 4 + hh
                ov = [psb[b][:, 0:260].rearrange("p (j c) -> p j c", c=130) for b in banks]
                for bi in range(2):
                    S.op("dve", lambda e: e.reciprocal(rden[:, 2 * bi:2 * bi + 2].rearrange("p (j o) -> p j o", o=1), ov[bi][:, :, 128:129]),
                         reads=[Bps[banks[bi]]], writes=[Bep])
                yi = oset_ctr[0] % 2
                if typ == "A":
                    for bi in range(2):
                        S.op("dve", lambda e: e.tensor_tensor(ytm[yi][:, 2 * bi:2 * bi + 2, :], ov[bi][:, :, 0:128],
                                                              bcast(rden[:, 2 * bi:2 * bi + 2], 128, 2), ALU.mult),
                             reads=[Bps[banks[bi]], Bep], writes=[Bytm[yi]])
                elif m == 0:
                    for bi in range(2):
                        S.op("dve", lambda e: e.tensor_tensor(o1n[:, 2 * bi:2 * bi + 2, :], ov[bi][:, :, 0:128],
                                                              bcast(rden[:, 2 * bi:2 * bi + 2], 128, 2), ALU.mult),
                             reads=[Bps[banks[bi]], Bep], writes=[Bo1n])
                    return
                else:
                    for bi in range(2):
                        S.op("dve", lambda e: e.tensor_tensor(yb[:, 2 * bi:2 * bi + 2, :], ov[bi][:, :, 0:128],
                                                              bcast(rden[:, 2 * bi:2 * bi + 2], 128, 2), ALU.mult),
                             reads=[Bps[banks[bi]], Bep], writes=[Bep])
                    S.op("dve", lambda e: e.scalar_tensor_tensor(yb, yb, neglam[:, 0:1], o1n, ALU.mult, ALU.add),
                         reads=[Bep, Bo1n, Bconst], writes=[Bep])
                    S.op("dve", lambda e: e.tensor_tensor(o1n, yb, yb, ALU.mult), reads=[Bep, Bo1n], writes=[Bo1n])
                    S.op("dve", lambda e: e.reduce_sum(ssq, o1n, axis=AX.X), reads=[Bo1n], writes=[Bep])

                def part2b():
                    tb_ = sb_ctr[0] % 4
                    sb_ctr[0] += 1
                    pvf = transpose4(ytm[yi].rearrange("p j d -> p (j d)"), Bytm[yi], tb_)
                    S.op("dve", lambda e: e.tensor_copy(yT[:, head, qb * 512:(qb + 1) * 512], pvf),
                         reads=[Bps[tb_]], writes=[ByT[qb]])

                def part2():
                    if typ == "B":
                        S.op("act", lambda e: e.activation(ssq, ssq, AF.Sqrt, scale=1.0 / 128.0, bias=epsB[:, 0:1]), reads=[Bep, Bconst], writes=[Bep])
                        S.op("dve", lambda e: e.reciprocal(ssq, ssq), reads=[Bep], writes=[Bep])
                        S.op("dve", lambda e: e.tensor_scalar(ssq, ssq, 1.0 - LAM_INIT, None, ALU.mult), reads=[Bep], writes=[Bep])
                        S.op("dve", lambda e: e.tensor_tensor(yb, yb, bcast(ssq, 128, 2), ALU.mult), reads=[Bep], writes=[Bep])
                        S.op("dve", lambda e: e.tensor_tensor(ytm[yi], yb, bcast(subl, 4, 1), ALU.mult), reads=[Bep, Bconst], writes=[Bytm[yi]])
                        deferred.append((state["i"] + DEL, part2b))
                    else:
                        part2b()

                deferred.append((state["i"] + DEL, part2))

            DEL = 8
            deferred = []
            LA = 3
            for i in range(len(its) + LA):
                if i < len(its):
                    issue_S(i)
                if i - LA >= 0:
                    issue_PV(i - LA)
                    due = [d for d in deferred if d[0] <= i - LA]
                    for d in due:
                        deferred.remove(d)
                        d[1]()
            while deferred:
                deferred.pop(0)[1]()
            if STOP == 20 + G:
                return finish_early()
        S.barrier()
        if STOP == 2:
            return finish_early()

        x1 = view(R_A.base, [NQ, D], F32)
        Bx1 = [Buf(f"x1_{t}") for t in range(NQ)]
        for t in range(NQ):
            tkx = S.dma("sp", sl_x1, x1[:, t, :], xb_t[t], writes=[Bx1[t]])
        for t in range(NQ):
            Bx1[t].w = tkx
        obank = [0]
        for n in range(4):
            wi = pending_units.pop(f"o{n}")
            for t in range(NQ):
                bk = obank[0] % 8
                obank[0] += 1
                for c in range(16):
                    S.op("pe", lambda e: e.matmul(psb[bk][:], yT[:, c, t * 128:(t + 1) * 128], wslot[wi][:, c, :],
                                                  start=(c == 0), stop=(c == 15)),
                         reads=[ByT[t // 4], Bw[wi]], writes=[Bps[bk]], sig=(c == 15))
                S.op("dve", lambda e: e.tensor_tensor(x1[:, t, n * 512:(n + 1) * 512], x1[:, t, n * 512:(n + 1) * 512], psb[bk][:], ALU.add),
                     reads=[Bps[bk], Bx1[t]], writes=[Bx1[t]])
            if n + 2 < 4:
                pending_units[f"o{n + 2}"] = load_unit(w_out_v[:, :, (n + 2) * 512:(n + 3) * 512])
            elif n == 2:
                pending_units[("g", 0, 0)] = load_unit(w_gate_v[:, :, 0:512])
            else:
                pending_units[("u", 0, 0)] = load_unit(w_up_v[:, :, 0:512])
        S.barrier()
        if STOP == 3:
            return finish_early()

        R_S.reset()
        h2T = R_Y.base
        h2T = view(R_Y.base, [16, NQ * 128], BF16)
        Bh2T = [Buf(f"h2T{t}") for t in range(NQ)]
        aT = R_S.alloc([NF, 512], BF16)
        sg = [R_S.alloc([512], F32) for _ in range(4)]
        gfull = R_S.alloc([D], F32)
        junk = R_S.alloc([D], BF16)
        xs_b = [R_S.alloc([D], BF16) for _ in range(2)]
        Bg = Buf("gfull2")
        Bjunk = Buf("junk2")
        Bxs = [Buf("xs2_0"), Buf("xs2_1")]
        BaT = [Buf(f"aT{f}") for f in range(NF)]
        Bsg = [Buf(f"sg{i}") for i in range(4)]
        S.dma("sp", sl_g, gfull, g2_d[:, :], writes=[Bg])
        for t in range(NQ + 1):
            if t < NQ:
                norm_head(t, x1[:, t, :], Bx1[t], gfull, Bg, ss2, rs2, Bss2[t], xs_b[t % 2], Bxs[t % 2], junk, Bjunk)
            if t >= 1:
                norm_tail(t - 1, xs_b[(t - 1) % 2], Bxs[(t - 1) % 2], h2T, Bh2T[t - 1], (t - 1) % 2)
        S.barrier(engines=("sp",))
        Bg3 = Buf("gfull3")
        S.dma("sp", sl_g, gfull, g3_d[:, :], writes=[Bg3])

        for blk in range(2):
            tb = slice(blk * 512, (blk + 1) * 512)
            for uu in range(11):
                wg = pending_units.pop(("g", blk, uu))
                wu = pending_units.pop(("u", blk, uu))
                for i in range(4):
                    f = uu * 4 + i
                    bk = f % 2
                    for c in range(16):
                        S.op("pe", lambda e: e.matmul(psb[bk][:], wslot[wg][:, c, i * 128:(i + 1) * 128], h2T[:, c, tb],
                                                      start=(c == 0), stop=(c == 15)),
                             reads=[Bw[wg]] + Bh2T[blk * 4:blk * 4 + 4], writes=[Bps[bk]], sig=(c == 15))
                    S.op("act", lambda e: e.activation(sg[i], psb[bk][:], AF.Silu), reads=[Bps[bk]], writes=[Bsg[i]])
                if uu + 1 < 11:
                    pending_units[("g", blk, uu + 1)] = load_unit(w_gate_v[:, :, (uu + 1) * 512:(uu + 2) * 512])
                else:
                    pending_units[("d", blk, 0)] = load_unit(w_down_v[:, 0:16, 0:512])
                for i in range(4):
                    f = uu * 4 + i
                    bk = 2 + f % 2
                    for c in range(16):
                        S.op("pe", lambda e: e.matmul(psb[bk][:], wslot[wu][:, c, i * 128:(i + 1) * 128], h2T[:, c, tb],
                                                      start=(c == 0), stop=(c == 15)),
                             reads=[Bw[wu]] + Bh2T[blk * 4:blk * 4 + 4], writes=[Bps[bk]], sig=(c == 15))
                    S.op("dve", lambda e: e.tensor_tensor(aT[:, f, :], sg[i], psb[bk][:], ALU.mult),
                         reads=[Bps[bk], Bsg[i]], writes=[BaT[f]])
                if uu + 1 < 11:
                    pending_units[("u", blk, uu + 1)] = load_unit(w_up_v[:, :, (uu + 1) * 512:(uu + 2) * 512])
                else:
                    pending_units[("d", blk, 1)] = load_unit(w_down_v[:, 16:32, 0:512])
            dunits = [(n, fu) for n in range(4) for fu in range(3)]
            frange = [(0, 16), (16, 32), (32, 44)]
            for di, (n, fu) in enumerate(dunits):
                wi = pending_units.pop(("d", blk, di))
                f0, f1 = frange[fu]
                for j in range(4):
                    bk = 4 + j
                    for f in range(f0, f1):
                        S.op("pe", lambda e: e.matmul(psb[bk][:], aT[:, f, j * 128:(j + 1) * 128], wslot[wi][:, f - f0, :],
                                                      start=(f == 0), stop=(f == NF - 1)),
                             reads=[BaT[f], Bw[wi]], writes=[Bps[bk]], sig=(f == f1 - 1))
                    if fu == 2:
                        T = blk * 4 + j
                        S.op("dve", lambda e: e.tensor_tensor(x1[:, T, n * 512:(n + 1) * 512], x1[:, T, n * 512:(n + 1) * 512], psb[bk][:], ALU.add),
                             reads=[Bps[bk], Bx1[T]], writes=[Bx1[T]])
                nx = di + 2
                if nx < len(dunits):
                    n2, fu2 = dunits[nx]
                    a0, a1 = frange[fu2]
                    pending_units[("d", blk, nx)] = load_unit(w_down_v[:, a0:a1, n2 * 512:(n2 + 1) * 512], nchunks=a1 - a0)
                elif blk == 0:
                    if nx == len(dunits):
                        pending_units[("g", 1, 0)] = load_unit(w_gate_v[:, :, 0:512])
                    else:
                        pending_units[("u", 1, 0)] = load_unit(w_up_v[:, :, 0:512])
            for j in range(4):
                T = blk * 4 + j
                src = x1[:, T, :]
                S.op("act", lambda e: e.activation(junk, src, AF.Square, accum_out=ss3[:, T:T + 1]),
                     reads=[Bx1[T], Bconst], writes=[Bjunk, Bss3[T]])
                S.op("act", lambda e: e.activation(rs3[:, T:T + 1], ss3[:, T:T + 1], AF.Sqrt, scale=1.0 / D, bias=epsA[:, 0:1]),
                     reads=[Bss3[T], Bconst], writes=[Bss3[T]])
                S.op("dve", lambda e: e.reciprocal(rs3[:, T:T + 1], rs3[:, T:T + 1]), reads=[Bss3[T]], writes=[Bss3[T]])
                S.op("dve", lambda e: e.scalar_tensor_tensor(src, src, rs3[:, T:T + 1], gfull, ALU.mult, ALU.mult),
                     reads=[Bx1[T], Bss3[T], Bg3], writes=[Bx1[T]])
                last = S.dma("sp", sl_o, out_t[T], src, reads=[Bx1[T]])
        S.wait_tok("sp", last)
        assert not pending_units, pending_units
    return nc


def _mult(d):
    d = np.asarray(d)
    m = (np.abs(d) <= 64).astype(np.float32)
    m += ((d % 4 == 0) & (np.abs(d) <= 256)).astype(np.float32)
    m += ((d % 16 == 0) & (np.abs(d) <= 1024)).astype(np.float32)
    return m


def _rope_tab(pos, rot_dim):
    inv = (np.float32(ROPE_THETA) ** (-np.arange(0, rot_dim, 2, dtype=np.float32) / np.float32(rot_dim))).astype(np.float32)
    ang = pos.astype(np.float32)[:, None] * inv[None, :]
    return np.cos(ang).astype(np.float32), np.sin(ang).astype(np.float32)


_NC_CACHE = {}


def kernel(x, norm_attn, w_in, lambda_qk, subln, w_out, norm_ffn, w_gate, w_up, w_down, norm_final):
    x = np.asarray(x, dtype=np.float32)
    f32c = lambda a: np.ascontiguousarray(np.asarray(a, dtype=np.float32))
    rep = lambda v: np.ascontiguousarray(np.broadcast_to(np.asarray(v, dtype=np.float32).reshape(1, -1), (128, np.asarray(v).size)))
    if "nc" not in _NC_CACHE:
        _NC_CACHE["nc"] = build_program()
    nc = _NC_CACHE["nc"]
    shared = dict(
        g1=rep(norm_attn[0]), g2=rep(norm_ffn[0]), g3=rep(norm_final),
        w_in=f32c(w_in[0]), w_out=f32c(w_out[0]), w_gate=f32c(w_gate[0]), w_up=f32c(w_up[0]), w_down=f32c(w_down[0]),
        lamq=rep(np.asarray(lambda_qk[0]).reshape(-1)), subl=rep(subln[0]),
        ident=np.eye(128, dtype=np.float32),
    )
    p = np.arange(128)[:, None]
    c = np.arange(1920)[None, :]
    mown = _mult(p - (c - 896)).astype(ml_dtypes.bfloat16)
    in_maps = []
    for core in range(8):
        b, hf = core // 2, core % 2
        own = np.arange(hf * 1024, (hf + 1) * 1024)
        oth = np.arange((1 - hf) * 1024, (2 - hf) * 1024)
        pos = np.concatenate([own, oth])
        xb = np.ascontiguousarray(x[b][pos])
        ca, sa = _rope_tab(pos, 32)
        cb, sb = _rope_tab(pos, 16)
        lay = lambda a: np.ascontiguousarray(a.reshape(NT, 128, -1).transpose(1, 0, 2))
        moth = _mult(p - (c - 1920) - 2048 * hf).astype(ml_dtypes.bfloat16)
        m = dict(shared)
        m.update(xb=xb, cosA=lay(ca), sinA=lay(sa), cosB=lay(cb), sinB=lay(sb), mown=mown, moth=moth)
        in_maps.append(m)
    if _NC_CACHE.get("prep_only"):
        return nc, in_maps
    res = run_bass_kernel_spmd(nc, in_maps, core_ids=list(range(8)))
    out = np.empty((4, SEQ, D), dtype=np.float32)
    for core in range(8):
        b, hf = core // 2, core % 2
        out[b, hf * 1024:(hf + 1) * 1024] = res.results[core]["out"]
    return out
```

```python
from contextlib import ExitStack
import math
import numpy as np
import ml_dtypes
import concourse.bass as bass
import concourse.mybir as mybir
from concourse.bass_utils import run_bass_kernel_spmd

F32 = mybir.dt.float32
BF16 = mybir.dt.bfloat16
AF = mybir.ActivationFunctionType
ALU = mybir.AluOpType
AX = mybir.AxisListType

D = 2048
SEQ = 2048
NT = 16
NQ = 8
DFF = 5632
NF = 44
LAM_INIT = 0.8 - 0.6 * math.exp(0.0)
ROPE_THETA = 500000.0


class Tok:
    __slots__ = ("sem", "val", "eng")

    def __init__(self, sem, val, eng):
        self.sem = sem
        self.val = val
        self.eng = eng


class Buf:
    __slots__ = ("name", "w", "rs", "excl")

    def __init__(self, name, excl=False):
        self.name = name
        self.w = None
        self.rs = {}
        self.excl = excl


class Sched:
    def __init__(self, nc, stack):
        self.nc = nc
        self.stack = stack
        self.E = {}
        for name, eng in (("pe", nc.tensor), ("act", nc.scalar), ("dve", nc.vector),
                          ("pool", nc.gpsimd), ("sp", nc.sync)):
            sem = stack.enter_context(nc.semaphore("s_" + name))
            self.E[name] = dict(eng=eng, sem=sem, cnt=0, waited={}, name=name)

    def new_sem(self, name):
        return self.stack.enter_context(self.nc.semaphore(name))

    def slot(self, name):
        return dict(sem=self.new_sem(name), cnt=0)

    def _wait(self, E, reads, writes):
        need = {}

        def add(tok, raw):
            if tok is None:
                return
            if tok.eng == E["name"]:
                if E["name"] in ("pe", "sp", "pool"):
                    return
            k = tok.sem.num
            if k not in need or need[k][1] < tok.val:
                need[k] = (tok.sem, tok.val, tok.eng)

        for b in reads:
            add(b.w, True)
            if b.excl:
                for t in b.rs.values():
                    if t.eng != E["name"]:
                        add(t, False)
        for b in writes:
            add(b.w, False)
            for t in b.rs.values():
                add(t, False)
        for k, (sem, val, en) in need.items():
            if E["waited"].get(k, 0) >= val:
                continue
            if en in self.E:
                assert self.E[en]["cnt"] >= val, f"wait on unflagged {en} {val}>{self.E[en]['cnt']}"
            E["eng"].wait_ge(sem, val)
            E["waited"][k] = val

    def op(self, en, fn, reads=(), writes=(), sig=True):
        E = self.E[en]
        self._wait(E, reads, writes)
        ins = fn(E["eng"])
        if en == "pe" and not sig:
            tok = Tok(E["sem"], E["cnt"] + 1, en)
        else:
            ins.then_inc(E["sem"], 1)
            E["cnt"] += 1
            tok = Tok(E["sem"], E["cnt"], en)
        for b in reads:
            b.rs[en] = tok
        for b in writes:
            b.w = tok
            b.rs = {}
        return tok

    def dma(self, en, slot, out, in_, reads=(), writes=()):
        E = self.E[en]
        self._wait(E, reads, writes)
        ins = E["eng"].dma_start(out=out, in_=in_)
        ins.then_inc(slot["sem"], 16)
        slot["cnt"] += 16
        tok = Tok(slot["sem"], slot["cnt"], "dma")
        for b in reads:
            b.rs["dma%d" % slot["sem"].num] = tok
        for b in writes:
            b.w = tok
            b.rs = {}
        return tok

    def wait_tok(self, en, tok):
        E = self.E[en]
        if E["waited"].get(tok.sem.num, 0) < tok.val:
            E["eng"].wait_ge(tok.sem, tok.val)
            E["waited"][tok.sem.num] = tok.val

    def barrier(self, engines=("pe", "act", "dve", "sp")):
        for en in engines:
            for e2 in ("pe", "act", "dve"):
                if e2 == en:
                    continue
                E2 = self.E[e2]
                if E2["cnt"] > 0:
                    self.wait_tok(en, Tok(E2["sem"], E2["cnt"], e2))


def bcast(ap, n, pos):
    l = [list(x) for x in ap.ap]
    l.insert(pos, [0, n])
    return bass.AP(ap.tensor, ap.offset, l)


import os
STOP = int(os.environ.get("K_STOP", "99"))


def build_program():
    nc = bass.Bass("TRN2", target_bir_lowering=False)
    dt_in = lambda name, shape, dt=F32: nc.dram_tensor(name, shape, dt, kind="ExternalInput").ap()
    xb_d = dt_in("xb", [SEQ, D])
    g1_d = dt_in("g1", [128, D])
    g2_d = dt_in("g2", [128, D])
    g3_d = dt_in("g3", [128, D])
    w_in_d = dt_in("w_in", [D, 6144])
    w_out_d = dt_in("w_out", [D, D])
    w_gate_d = dt_in("w_gate", [D, DFF])
    w_up_d = dt_in("w_up", [D, DFF])
    w_down_d = dt_in("w_down", [DFF, D])
    cosA_d = dt_in("cosA", [128, NT, 16])
    sinA_d = dt_in("sinA", [128, NT, 16])
    cosB_d = dt_in("cosB", [128, NT, 8])
    sinB_d = dt_in("sinB", [128, NT, 8])
    mown_d = dt_in("mown", [128, 1920], BF16)
    moth_d = dt_in("moth", [128, 1920], BF16)
    lamq_d = dt_in("lamq", [128, 256])
    subl_d = dt_in("subl", [128, 128])
    ident_d = dt_in("ident", [128, 128])
    out_d = nc.dram_tensor("out", [NQ * 128, D], F32, kind="ExternalOutput").ap()

    xb_t = xb_d.rearrange("(t p) d -> t p d", p=128)
    out_t = out_d.rearrange("(t p) d -> t p d", p=128)
    w_in_v = w_in_d.rearrange("(c p) n -> p c n", p=128)
    w_out_v = w_out_d.rearrange("(c p) n -> p c n", p=128)
    w_gate_v = w_gate_d.rearrange("(c p) n -> p c n", p=128)
    w_up_v = w_up_d.rearrange("(c p) n -> p c n", p=128)
    w_down_v = w_down_d.rearrange("(f p) n -> p f n", p=128)

    with ExitStack() as st:
        S = Sched(nc, st)
        ARENA_BYTES = 206 * 1024
        arena = st.enter_context(nc.sbuf_tensor("arena", [128, ARENA_BYTES // 2], BF16))

        def view(off, shape, dt):
            n = 1
            for s_ in shape:
                n *= s_
            esz = 4 if dt == F32 else 2
            assert off % 4 == 0 and off + n * esz <= ARENA_BYTES, (off, shape)
            ap = arena[:, off // 2:(off + n * esz) // 2]
            if dt == F32:
                ap = ap.bitcast(F32)
            if len(shape) == 2:
                ap = ap.rearrange("p (a b) -> p a b", b=shape[1])
            elif len(shape) == 3:
                ap = ap.rearrange("p (a b c) -> p a b c", b=shape[1], c=shape[2])
            elif len(shape) == 4:
                ap = ap.rearrange("p (a b c d) -> p a b c d", b=shape[1], c=shape[2], d=shape[3])
            return ap

        class Bump:
            def __init__(self, base, size):
                self.base, self.size, self.off = base, size, 0

            def reset(self):
                self.off = 0

            def alloc(self, shape, dt):
                n = 1
                for s_ in shape:
                    n *= s_
                sz = (n * (4 if dt == F32 else 2) + 3) // 4 * 4
                assert self.off + sz <= self.size, ("region overflow", self.off, sz, self.size)
                v = view(self.base + self.off, shape, dt)
                self.off += sz
                return v

        o = 0
        R_P = Bump(o, 4096); o += 4096
        R_A = Bump(o, 65536); o += 65536
        R_W = Bump(o, 32768); o += 32768
        R_Y = Bump(o, 32768); o += 32768
        R_S = Bump(o, ARENA_BYTES - o)
        assert R_S.size >= 73000, R_S.size

        psb = [st.enter_context(nc.psum_tensor(f"psb{i}", [128, 512], F32)) for i in range(8)]
        Bps = [Buf(f"ps{i}", excl=True) for i in range(8)]
        ps3b = psb[3][:].bitcast(BF16)
        Bps3 = [Bps[3], Bps[3]]

        identf = R_P.alloc([128], F32)
        identb = R_P.alloc([128], BF16)
        epsA = R_P.alloc([1], F32)
        epsB = R_P.alloc([1], F32)
        lamq = R_P.alloc([256], F32)
        lprod = R_P.alloc([2, 64], F32)
        ldots = R_P.alloc([2], F32)
        lex = R_P.alloc([2], F32)
        neglam = R_P.alloc([1], F32)
        subl = R_P.alloc([128], F32)
        ss1 = R_P.alloc([NT], F32)
        rs1 = R_P.alloc([NT], F32)
        ss2 = R_P.alloc([NQ], F32)
        rs2 = R_P.alloc([NQ], F32)
        ss3 = R_P.alloc([NQ], F32)
        rs3 = R_P.alloc([NQ], F32)
        Bconst = Buf("const")
        Bss1 = [Buf(f"ss1_{t}") for t in range(NT)]
        Bss2 = [Buf(f"ss2_{t}") for t in range(NQ)]
        Bss3 = [Buf(f"ss3_{t}") for t in range(NQ)]

        wslot = [view(R_W.base + i * 16384, [16, 512], BF16) for i in range(2)]
        Bw = [Buf("w0"), Buf("w1")]
        sl_w = [S.slot("ldw0"), S.slot("ldw1")]
        wctr = [0]

        def load_unit(src_ap, nchunks=16):
            i = wctr[0] % 2
            wctr[0] += 1
            S.dma("pool", sl_w[i], wslot[i][:, 0:nchunks, :], src_ap, writes=[Bw[i]])
            return i

        sl_c = S.slot("ldc")
        sl_g = S.slot("ldg")
        sl_t = S.slot("ldt")
        sl_x1 = S.slot("ldxres")
        sl_x = [S.slot("ldx0"), S.slot("ldx1")]
        sl_o = S.slot("sto")

        def finish_early():
            S.barrier()
            tkk = S.dma("sp", sl_o, out_t[0], view(R_A.base, [D], F32))
            S.wait_tok("sp", tkk)
            return nc

        S.dma("sp", sl_c, identf, ident_d[:, :], writes=[Bconst])
        S.dma("sp", sl_c, lamq, lamq_d[:, :], writes=[Bconst])
        tk = S.dma("sp", sl_c, subl, subl_d[:, :], writes=[Bconst])
        S.op("dve", lambda e: e.tensor_copy(identb, identf), reads=[Bconst], writes=[Bconst])
        S.op("dve", lambda e: e.memset(epsA, 1e-6), writes=[Bconst])
        S.op("dve", lambda e: e.memset(epsB, 1e-5), writes=[Bconst])
        for tl in (ss1, ss2, ss3):
            S.op("dve", lambda e: e.memset(tl, 0.0), writes=[Bconst])
        lqv = lamq.rearrange("p (a b d) -> p a b d", a=2, b=2, d=64)
        S.op("dve", lambda e: e.tensor_tensor(lprod, lqv[:, :, 0, :], lqv[:, :, 1, :], ALU.mult), reads=[Bconst], writes=[Bconst])
        S.op("dve", lambda e: e.reduce_sum(ldots, lprod, axis=AX.X), reads=[Bconst], writes=[Bconst])
        S.op("act", lambda e: e.activation(lex, ldots, AF.Exp), reads=[Bconst], writes=[Bconst])
        S.op("dve", lambda e: e.tensor_tensor(neglam, lex[:, 1:2], lex[:, 0:1], ALU.subtract), reads=[Bconst], writes=[Bconst])
        S.op("dve", lambda e: e.tensor_scalar(neglam, neglam, -LAM_INIT, None, ALU.add), reads=[Bconst], writes=[Bconst])

        def norm_head(t, src, Bsrc, gfull, Bg, ss, rs, Bss, xs, Bxs, junk, Bjunk):
            S.op("act", lambda e: e.activation(junk, src, AF.Square, accum_out=ss[:, t:t + 1]),
                 reads=[Bsrc, Bconst], writes=[Bjunk, Bss])
            S.op("act", lambda e: e.activation(rs[:, t:t + 1], ss[:, t:t + 1], AF.Sqrt, scale=1.0 / D, bias=epsA[:, 0:1]),
                 reads=[Bss, Bconst], writes=[Bss])
            S.op("dve", lambda e: e.reciprocal(rs[:, t:t + 1], rs[:, t:t + 1]), reads=[Bss], writes=[Bss])
            S.op("dve", lambda e: e.scalar_tensor_tensor(xs, src, rs[:, t:t + 1], gfull, ALU.mult, ALU.mult),
                 reads=[Bsrc, Bss, Bg], writes=[Bxs])

        def norm_tail(t, xs, Bxs, dstT, BdstT, par):
            for hb in range(2):
                bk = 2 * par + hb
                pv = psb[bk][:].bitcast(BF16)
                for c8 in range(8):
                    c = hb * 8 + c8
                    S.op("pe", lambda e: e.transpose(pv[:, c8 * 128:(c8 + 1) * 128], xs[:, c * 128:(c + 1) * 128], identb),
                         reads=[Bxs, Bconst], writes=[Bps[bk]], sig=(c8 == 7))
                dst = dstT[:, hb * 8:hb * 8 + 8, t * 128:(t + 1) * 128]
                srcv = pv.rearrange("p (a b) -> p a b", b=128)
                if hb == 0:
                    S.op("act", lambda e: e.copy(dst, srcv), reads=[Bps[bk]], writes=[BdstT])
                else:
                    S.op("dve", lambda e: e.tensor_copy(dst, srcv), reads=[Bps[bk]], writes=[BdstT])

        hT = R_A.alloc([16, SEQ], BF16)
        BhT = [Buf(f"hT{t}") for t in range(NT)]
        R_S.reset()
        NXB = 4
        xin = [R_S.alloc([D], F32) for _ in range(NXB)]
        xs_b = [R_S.alloc([D], BF16) for _ in range(2)]
        gfull = R_S.alloc([D], F32)
        junk = R_S.alloc([D], BF16)
        Bxin = [Buf(f"xin{i}") for i in range(NXB)]
        sl_xp = [S.slot(f"ldxp{i}") for i in range(NXB)]
        Bxs = [Buf("xs0"), Buf("xs1")]
        Bg = Buf("gfull")
        Bjunk = Buf("junk")
        S.dma("sp", sl_g, gfull, g1_d[:, :], writes=[Bg])
        GROUPS = [("A", 0), ("A", 1), ("B", 0), ("B", 1)]

        def in_units(G):
            typ, gi = GROUPS[G]
            base = 0 if typ == "A" else 6
            return dict(q=base + gi, k=base + 2 + gi, v=base + 4 + gi)

        def in_unit_ap(n):
            return w_in_v[:, :, n * 512:(n + 1) * 512]

        pending_units = {}
        pending_units[(0, "k")] = load_unit(in_unit_ap(in_units(0)["k"]))
        pending_units[(0, "v")] = load_unit(in_unit_ap(in_units(0)["v"]))

        for t in range(NT + 1):
            if t < NT:
                S.dma("sp", sl_xp[t % NXB], xin[t % NXB], xb_t[t], writes=[Bxin[t % NXB]])
                norm_head(t, xin[t % NXB], Bxin[t % NXB], gfull, Bg, ss1, rs1, Bss1[t], xs_b[t % 2], Bxs[t % 2], junk, Bjunk)
            if t >= 1:
                norm_tail(t - 1, xs_b[(t - 1) % 2], Bxs[(t - 1) % 2], hT, BhT[t - 1], (t - 1) % 2)
        S.barrier()
        if STOP == 1:
            return finish_early()

        R_S.reset()
        QT = R_S.alloc([2, 4, NQ * 128], BF16)
        KT = R_S.alloc([4, SEQ], BF16)
        Vaug = R_S.alloc([NT, 4, 130], BF16)
        mown = R_S.alloc([1920], BF16)
        moth = R_S.alloc([1920], BF16)
        cosA = R_S.alloc([NT, 16], F32)
        sinA = R_S.alloc([NT, 16], F32)
        cosB = R_S.alloc([NT, 8], F32)
        sinB = R_S.alloc([NT, 8], F32)
        tm = [R_S.alloc([512], BF16) for _ in range(2)]
        NPT = 4
        pt = [R_S.alloc([512], BF16) for _ in range(NPT)]
        rt1 = R_S.alloc([128], F32)
        rt2 = R_S.alloc([128], F32)
        rsrc = R_S.alloc([128], F32)
        rden = R_S.alloc([4], F32)
        o1n = R_S.alloc([4, 128], F32)
        yb = R_S.alloc([4, 128], F32)
        ssq = R_S.alloc([4], F32)
        ytm = [R_S.alloc([4, 128], BF16) for _ in range(2)]
        yT = R_Y.alloc([16, NQ * 128], BF16)
        BQT = [Buf(f"QT{t}") for t in range(NQ)]
        BKT = [Buf(f"KT{t}") for t in range(NT)]
        BV = [Buf(f"V{t}") for t in range(NT)]
        Btab = Buf("tables")
        Btm = [Buf("tm0"), Buf("tm1")]
        Bpt = [Buf(f"pt{i}") for i in range(NPT)]
        Brt = Buf("rt")
        Brs = Buf("rsrc")
        Bep = Buf("ep")
        Bo1n = Buf("o1n")
        Bytm = [Buf("ytm0"), Buf("ytm1")]
        ByT = [Buf(f"yT{q}") for q in range(2)]
        S.dma("sp", sl_t, mown, mown_d[:, :], writes=[Btab])
        S.dma("sp", sl_t, moth, moth_d[:, :], writes=[Btab])
        S.dma("sp", sl_t, cosA, cosA_d[:, :, :], writes=[Btab])
        S.dma("sp", sl_t, sinA, sinA_d[:, :, :], writes=[Btab])
        S.dma("sp", sl_t, cosB, cosB_d[:, :, :], writes=[Btab])
        S.dma("sp", sl_t, sinB, sinB_d[:, :, :], writes=[Btab])
        S.op("dve", lambda e: e.memset(QT, 0.0), writes=BQT)
        S.op("dve", lambda e: e.memset(Vaug[:, :, :, 128:130], 1.0), writes=BV)

        if STOP == 10:
            return finish_early()
        pbank = [0]

        def next_pbank():
            b = (0, 1, 2, 4)[pbank[0] % 4]
            pbank[0] += 1
            return b

        trh = [0]

        def proj(bk, t, wi):
            for c in range(16):
                S.op("pe", lambda e: e.matmul(psb[bk][:], hT[:, c, t * 128:(t + 1) * 128], wslot[wi][:, c, :],
                                              start=(c == 0), stop=(c == 15)),
                     reads=[BhT[t], Bw[wi]], writes=[Bps[bk]], sig=(c == 15))

        def rope_evac(bk, t, typ, dst, Bdst):
            if typ == "A":
                nh, hd, r2, ct, stb = 4, 128, 16, cosA, sinA
            else:
                nh, hd, r2, ct, stb = 8, 64, 8, cosB, sinB
            src = psb[bk][:].rearrange("p (h d) -> p h d", d=hd)
            dv = dst.rearrange("p (h d) -> p h d", d=hd)
            rs_ = rsrc[:, 0:nh * 2 * r2].rearrange("p (h d) -> p h d", d=2 * r2)
            t1 = rt1[:, 0:nh * 2 * r2].rearrange("p (h d) -> p h d", d=2 * r2)
            t2 = rt2[:, 0:nh * 2 * r2].rearrange("p (h d) -> p h d", d=2 * r2)
            cb = bcast(ct[:, t, :], nh, 1)
            sb_ = bcast(stb[:, t, :], nh, 1)
            S.op("act", lambda e: e.copy(dv[:, :, 2 * r2:hd], src[:, :, 2 * r2:hd]), reads=[Bps[bk]], writes=[Bdst])
            S.op("act", lambda e: e.copy(rs_, src[:, :, 0:2 * r2]), reads=[Bps[bk]], writes=[Brs])
            if STOP == 16:
                return
            S.op("dve", lambda e: e.tensor_tensor(t1[:, :, 0:r2], rs_[:, :, 0:r2], cb, ALU.mult), reads=[Brs, Btab], writes=[Brt])
            if STOP == 17:
                return
            S.op("dve", lambda e: e.tensor_tensor(t1[:, :, r2:2 * r2], rs_[:, :, r2:2 * r2], cb, ALU.mult), reads=[Brs, Btab], writes=[Brt])
            S.op("dve", lambda e: e.tensor_tensor(t2[:, :, 0:r2], rs_[:, :, r2:2 * r2], sb_, ALU.mult), reads=[Brs, Btab], writes=[Brt])
            S.op("dve", lambda e: e.tensor_tensor(t2[:, :, r2:2 * r2], rs_[:, :, 0:r2], sb_, ALU.mult), reads=[Brs, Btab], writes=[Brt])
            S.op("dve", lambda e: e.tensor_tensor(dv[:, :, 0:r2], t1[:, :, 0:r2], t2[:, :, 0:r2], ALU.subtract), reads=[Brt], writes=[Bdst])
            S.op("dve", lambda e: e.tensor_tensor(dv[:, :, r2:2 * r2], t1[:, :, r2:2 * r2], t2[:, :, r2:2 * r2], ALU.add), reads=[Brt], writes=[Bdst])

        psbf = [psb[i][:].bitcast(BF16) for i in range(8)]

        def transpose4(srcap, Bsrc, bank=3):
            for h in range(4):
                S.op("pe", lambda e: e.transpose(psbf[bank][:, h * 128:(h + 1) * 128],
                                                 srcap[:, h * 128:(h + 1) * 128], identb),
                     reads=[Bsrc, Bconst], writes=[Bps[bank]], sig=(h == 3))
            return psbf[bank][:, 0:512]

        scaleA = 128.0 ** -0.5
        scaleB = 64.0 ** -0.5
        oset_ctr = [0]
        sb_ctr = [0]

        for G in range(4):
            typ, gi = GROUPS[G]
            U = in_units(G)
            if G == 2:
                S.op("dve", lambda e: e.memset(QT, 0.0), writes=BQT)
            wi_k = pending_units.pop((G, "k"))
            wi_v = pending_units.pop((G, "v"))
            def k_tail(t):
                tmi = t % 2
                pv = transpose4(tm[tmi], Btm[tmi], 3)
                S.op("act", lambda e: e.copy(KT[:, :, t * 128:(t + 1) * 128], pv.rearrange("p (h k) -> p h k", k=128)),
                     reads=[Bps[3]], writes=[BKT[t]])

            for t in range(NT + 1):
                if t < NT:
                    bk = next_pbank()
                    proj(bk, t, wi_k)
                    rope_evac(bk, t, typ, tm[t % 2], Btm[t % 2])
                if t >= 1:
                    k_tail(t - 1)
            if STOP in (11, 14, 15, 16, 17):
                return finish_early()
            wi_q = load_unit(in_unit_ap(U["q"]))
            for t in range(NT):
                bk = next_pbank()
                proj(bk, t, wi_v)
                S.op("act", lambda e: e.copy(Vaug[:, t, :, 0:128], psb[bk][:].rearrange("p (h d) -> p h d", d=128)),
                     reads=[Bps[bk]], writes=[BV[t]])
            if G + 1 < 4:
                pending_units[(G + 1, "k")] = load_unit(in_unit_ap(in_units(G + 1)["k"]))
            def q_tail(t):
                tmi = t % 2
                pvf = transpose4(tm[tmi], Btm[tmi], 3)
                pv = pvf.rearrange("p (h k) -> p h k", k=128)
                if typ == "A":
                    S.op("act", lambda e: e.copy(QT[:, 0, :, t * 128:(t + 1) * 128], pv), reads=[Bps[3]], writes=[BQT[t]])
                else:
                    S.op("act", lambda e: e.copy(QT[0:64, 0, :, t * 128:(t + 1) * 128], pv[0:64]), reads=[Bps[3]], writes=[BQT[t]])
                    S.op("act", lambda e: e.copy(QT[64:128, 1, :, t * 128:(t + 1) * 128], pv[64:128]), reads=[Bps[3]], writes=[BQT[t]])

            for t in range(NQ + 1):
                if t < NQ:
                    bk = next_pbank()
                    proj(bk, t, wi_q)
                    rope_evac(bk, t, typ, tm[t % 2], Btm[t % 2])
                if t >= 1:
                    q_tail(t - 1)
            if G + 1 < 4:
                pending_units[(G + 1, "v")] = load_unit(in_unit_ap(in_units(G + 1)["v"]))
            else:
                pending_units["o0"] = load_unit(w_out_v[:, :, 0:512])
                pending_units["o1"] = load_unit(w_out_v[:, :, 512:1024])

            if STOP == 13:
                return finish_early()
            maps = [0] if typ == "A" else [0, 1]
            its = [(hh, qb, m, kt) for hh in range(4) for qb in range(2) for m in maps for kt in range(NT)]
            scale = scaleA if typ == "A" else scaleB
            state = {}

            def issue_S(i):
                hh, qb, m, kt = its[i]
                sbk = sb_ctr[0] % 4
                sb_ctr[0] += 1
                pi = i % NPT
                S.op("pe", lambda e: e.matmul(psb[sbk][:], KT[:, hh, kt * 128:(kt + 1) * 128],
                                              QT[:, m, hh, qb * 512:(qb + 1) * 512], start=True, stop=True),
                     reads=[BKT[kt]] + BQT[qb * 4:qb * 4 + 4], writes=[Bps[sbk]], sig=True)
                S.op("act", lambda e: e.activation(pt[pi], psb[sbk][:], AF.Exp, scale=scale), reads=[Bps[sbk]], writes=[Bpt[pi]])
                if typ == "A":
                    u = qb * 512 - 128 * kt
                    if kt < 8:
                        msk = mown[:, u + 896:u + 896 + 512]
                    else:
                        msk = moth[:, u + 1920:u + 1920 + 512]
                    S.op("dve", lambda e: e.tensor_tensor(pt[pi], pt[pi], msk, ALU.mult), reads=[Bpt[pi], Btab], writes=[Bpt[pi]])

            def issue_PV(i):
                hh, qb, m, kt = its[i]
                state["i"] = i
                pi = i % NPT
                if kt == 0:
                    state["os"] = oset_ctr[0] % 2
                    oset_ctr[0] += 1
                os_ = state["os"]
                banks = (4 + 2 * os_, 5 + 2 * os_)
                for j in range(4):
                    bk = banks[j // 2]
                    col = (j % 2) * 130
                    S.op("pe", lambda e: e.matmul(psb[bk][:, col:col + 129], pt[pi][:, j * 128:(j + 1) * 128],
                                                  Vaug[:, kt, hh, 0:129], start=(kt == 0 and j % 2 == 0),
                                                  stop=(kt == NT - 1 and j % 2 == 1)),
                         reads=[Bpt[pi], BV[kt]], writes=[Bps[bk]], sig=(j == 3))
                if kt == NT - 1:
                    epilogue(hh, qb, m, banks)

            def epilogue(hh, qb, m, banks):
                head = (0 if typ == "A" else 8) + gi * 4 + hh
                ov = [psb[b][:, 0:260].rearrange("p (j c) -> p j c", c=130) for b in banks]
                for bi in range(2):
                    S.op("dve", lambda e: e.reciprocal(rden[:, 2 * bi:2 * bi + 2].rearrange("p (j o) -> p j o", o=1), ov[bi][:, :, 128:129]),
                         reads=[Bps[banks[bi]]], writes=[Bep])
                yi = oset_ctr[0] % 2
                if typ == "A":
                    for bi in range(2):
                        S.op("dve", lambda e: e.tensor_tensor(ytm[yi][:, 2 * bi:2 * bi + 2, :], ov[bi][:, :, 0:128],
                                                              bcast(rden[:, 2 * bi:2 * bi + 2], 128, 2), ALU.mult),
                             reads=[Bps[banks[bi]], Bep], writes=[Bytm[yi]])
                elif m == 0:
                    for bi in range(2):
                        S.op("dve", lambda e: e.tensor_tensor(o1n[:, 2 * bi:2 * bi + 2, :], ov[bi][:, :, 0:128],
                                                              bcast(rden[:, 2 * bi:2 * bi + 2], 128, 2), ALU.mult),
                             reads=[Bps[banks[bi]], Bep], writes=[Bo1n])
                    return
                else:
                    for bi in range(2):
                        S.op("dve", lambda e: e.tensor_tensor(yb[:, 2 * bi:2 * bi + 2, :], ov[bi][:, :, 0:128],
                                                              bcast(rden[:, 2 * bi:2 * bi + 2], 128, 2), ALU.mult),
                             reads=[Bps[banks[bi]], Bep], writes=[Bep])
                    S.op("dve", lambda e: e.scalar_tensor_tensor(yb, yb, neglam[:, 0:1], o1n, ALU.mult, ALU.add),
                         reads=[Bep, Bo1n, Bconst], writes=[Bep])
                    S.op("dve", lambda e: e.tensor_tensor(o1n, yb, yb, ALU.mult), reads=[Bep, Bo1n], writes=[Bo1n])
                    S.op("dve", lambda e: e.reduce_sum(ssq, o1n, axis=AX.X), reads=[Bo1n], writes=[Bep])

                def part2b():
                    tb_ = sb_ctr[0] % 4
                    sb_ctr[0] += 1
                    pvf = transpose4(ytm[yi].rearrange("p j d -> p (j d)"), Bytm[yi], tb_)
                    S.op("dve", lambda e: e.tensor_copy(yT[:, head, qb * 512:(qb + 1) * 512], pvf),
                         reads=[Bps[tb_]], writes=[ByT[qb]])

                def part2():
                    if typ == "B":
                        S.op("act", lambda e: e.activation(ssq, ssq, AF.Ln, scale=1.0 / 128.0, bias=epsB[:, 0:1]), reads=[Bep, Bconst], writes=[Bep])
                        S.op("act", lambda e: e.activation(ssq, ssq, AF.Exp, scale=-0.5), reads=[Bep], writes=[Bep])
                        S.op("dve", lambda e: e.tensor_scalar(ssq, ssq, 1.0 - LAM_INIT, None, ALU.mult), reads=[Bep], writes=[Bep])
                        S.op("dve", lambda e: e.tensor_tensor(yb, yb, bcast(ssq, 128, 2), ALU.mult), reads=[Bep], writes=[Bep])
                        S.op("dve", lambda e: e.tensor_tensor(ytm[yi], yb, bcast(subl, 4, 1), ALU.mult), reads=[Bep, Bconst], writes=[Bytm[yi]])
                        deferred.append((state["i"] + DEL, part2b))
                    else:
                        part2b()

                deferred.append((state["i"] + DEL, part2))

            DEL = 8
            deferred = []
            LA = 3
            for i in range(len(its) + LA):
                if i < len(its):
                    issue_S(i)
                if i - LA >= 0:
                    issue_PV(i - LA)
                    due = [d for d in deferred if d[0] <= i - LA]
                    for d in due:
                        deferred.remove(d)
                        d[1]()
            while deferred:
                deferred.pop(0)[1]()
            if STOP == 20 + G:
                return finish_early()
        S.barrier()
        if STOP == 2:
            return finish_early()

        x1 = view(R_A.base, [NQ, D], F32)
        Bx1 = [Buf(f"x1_{t}") for t in range(NQ)]
        for t in range(NQ):
            tkx = S.dma("sp", sl_x1, x1[:, t, :], xb_t[t], writes=[Bx1[t]])
        for t in range(NQ):
            Bx1[t].w = tkx
        obank = [0]
        for n in range(4):
            wi = pending_units.pop(f"o{n}")
            for t in range(NQ):
                bk = obank[0] % 8
                obank[0] += 1
                for c in range(16):
                    S.op("pe", lambda e: e.matmul(psb[bk][:], yT[:, c, t * 128:(t + 1) * 128], wslot[wi][:, c, :],
                                                  start=(c == 0), stop=(c == 15)),
                         reads=[ByT[t // 4], Bw[wi]], writes=[Bps[bk]], sig=(c == 15))
                S.op("dve", lambda e: e.tensor_tensor(x1[:, t, n * 512:(n + 1) * 512], x1[:, t, n * 512:(n + 1) * 512], psb[bk][:], ALU.add),
                     reads=[Bps[bk], Bx1[t]], writes=[Bx1[t]])
            if n + 2 < 4:
                pending_units[f"o{n + 2}"] = load_unit(w_out_v[:, :, (n + 2) * 512:(n + 3) * 512])
            elif n == 2:
                pending_units[("g", 0, 0)] = load_unit(w_gate_v[:, :, 0:512])
            else:
                pending_units[("u", 0, 0)] = load_unit(w_up_v[:, :, 0:512])
        S.barrier()
        if STOP == 3:
            return finish_early()

        R_S.reset()
        h2T = R_Y.base
        h2T = view(R_Y.base, [16, NQ * 128], BF16)
        Bh2T = [Buf(f"h2T{t}") for t in range(NQ)]
        aT = R_S.alloc([NF, 512], BF16)
        sg = [R_S.alloc([512], F32) for _ in range(4)]
        gfull = R_S.alloc([D], F32)
        junk = R_S.alloc([D], BF16)
        xs_b = [R_S.alloc([D], BF16) for _ in range(2)]
        Bg = Buf("gfull2")
        Bjunk = Buf("junk2")
        Bxs = [Buf("xs2_0"), Buf("xs2_1")]
        BaT = [Buf(f"aT{f}") for f in range(NF)]
        Bsg = [Buf(f"sg{i}") for i in range(4)]
        S.dma("sp", sl_g, gfull, g2_d[:, :], writes=[Bg])
        for t in range(NQ + 1):
            if t < NQ:
                norm_head(t, x1[:, t, :], Bx1[t], gfull, Bg, ss2, rs2, Bss2[t], xs_b[t % 2], Bxs[t % 2], junk, Bjunk)
            if t >= 1:
                norm_tail(t - 1, xs_b[(t - 1) % 2], Bxs[(t - 1) % 2], h2T, Bh2T[t - 1], (t - 1) % 2)
        S.barrier(engines=("sp",))
        Bg3 = Buf("gfull3")
        S.dma("sp", sl_g, gfull, g3_d[:, :], writes=[Bg3])

        for blk in range(2):
            tb = slice(blk * 512, (blk + 1) * 512)
            for uu in range(11):
                wg = pending_units.pop(("g", blk, uu))
                wu = pending_units.pop(("u", blk, uu))
                for i in range(4):
                    f = uu * 4 + i
                    bk = f % 2
                    for c in range(16):
                        S.op("pe", lambda e: e.matmul(psb[bk][:], wslot[wg][:, c, i * 128:(i + 1) * 128], h2T[:, c, tb],
                                                      start=(c == 0), stop=(c == 15)),
                             reads=[Bw[wg]] + Bh2T[blk * 4:blk * 4 + 4], writes=[Bps[bk]], sig=(c == 15))
                    S.op("act", lambda e: e.activation(sg[i], psb[bk][:], AF.Silu), reads=[Bps[bk]], writes=[Bsg[i]])
                if uu + 1 < 11:
                    pending_units[("g", blk, uu + 1)] = load_unit(w_gate_v[:, :, (uu + 1) * 512:(uu + 2) * 512])
                else:
                    pending_units[("d", blk, 0)] = load_unit(w_down_v[:, 0:16, 0:512])
                for i in range(4):
                    f = uu * 4 + i
                    bk = 2 + f % 2
                    for c in range(16):
                        S.op("pe", lambda e: e.matmul(psb[bk][:], wslot[wu][:, c, i * 128:(i + 1) * 128], h2T[:, c, tb],
                                                      start=(c == 0), stop=(c == 15)),
                             reads=[Bw[wu]] + Bh2T[blk * 4:blk * 4 + 4], writes=[Bps[bk]], sig=(c == 15))
                    S.op("dve", lambda e: e.tensor_tensor(aT[:, f, :], sg[i], psb[bk][:], ALU.mult),
                         reads=[Bps[bk], Bsg[i]], writes=[BaT[f]])
                if uu + 1 < 11:
                    pending_units[("u", blk, uu + 1)] = load_unit(w_up_v[:, :, (uu + 1) * 512:(uu + 2) * 512])
                else:
                    pending_units[("d", blk, 1)] = load_unit(w_down_v[:, 16:32, 0:512])
            dunits = [(n, fu) for n in range(4) for fu in range(3)]
            frange = [(0, 16), (16, 32), (32, 44)]
            for di, (n, fu) in enumerate(dunits):
                wi = pending_units.pop(("d", blk, di))
                f0, f1 = frange[fu]
                for j in range(4):
                    bk = 4 + j
                    for f in range(f0, f1):
                        S.op("pe", lambda e: e.matmul(psb[bk][:], aT[:, f, j * 128:(j + 1) * 128], wslot[wi][:, f - f0, :],
                                                      start=(f == 0), stop=(f == NF - 1)),
                             reads=[BaT[f], Bw[wi]], writes=[Bps[bk]], sig=(f == f1 - 1))
                    if fu == 2:
                        T = blk * 4 + j
                        S.op("dve", lambda e: e.tensor_tensor(x1[:, T, n * 512:(n + 1) * 512], x1[:, T, n * 512:(n + 1) * 512], psb[bk][:], ALU.add),
                             reads=[Bps[bk], Bx1[T]], writes=[Bx1[T]])
                nx = di + 2
                if nx < len(dunits):
                    n2, fu2 = dunits[nx]
                    a0, a1 = frange[fu2]
                    pending_units[("d", blk, nx)] = load_unit(w_down_v[:, a0:a1, n2 * 512:(n2 + 1) * 512], nchunks=a1 - a0)
                elif blk == 0:
                    if nx == len(dunits):
                        pending_units[("g", 1, 0)] = load_unit(w_gate_v[:, :, 0:512])
                    else:
                        pending_units[("u", 1, 0)] = load_unit(w_up_v[:, :, 0:512])
            for j in range(4):
                T = blk * 4 + j
                src = x1[:, T, :]
                S.op("act", lambda e: e.activation(junk, src, AF.Square, accum_out=ss3[:, T:T + 1]),
                     reads=[Bx1[T], Bconst], writes=[Bjunk, Bss3[T]])
                S.op("act", lambda e: e.activation(rs3[:, T:T + 1], ss3[:, T:T + 1], AF.Sqrt, scale=1.0 / D, bias=epsA[:, 0:1]),
                     reads=[Bss3[T], Bconst], writes=[Bss3[T]])
                S.op("dve", lambda e: e.reciprocal(rs3[:, T:T + 1], rs3[:, T:T + 1]), reads=[Bss3[T]], writes=[Bss3[T]])
                S.op("dve", lambda e: e.scalar_tensor_tensor(src, src, rs3[:, T:T + 1], gfull, ALU.mult, ALU.mult),
                     reads=[Bx1[T], Bss3[T], Bg3], writes=[Bx1[T]])
                last = S.dma("sp", sl_o, out_t[T], src, reads=[Bx1[T]])
        S.wait_tok("sp", last)
        assert not pending_units, pending_units
    return nc


def _mult(d):
    d = np.asarray(d)
    m = (np.abs(d) <= 64).astype(np.float32)
    m += ((d % 4 == 0) & (np.abs(d) <= 256)).astype(np.float32)
    m += ((d % 16 == 0) & (np.abs(d) <= 1024)).astype(np.float32)
    return m


def _rope_tab(pos, rot_dim):
    inv = (np.float32(ROPE_THETA) ** (-np.arange(0, rot_dim, 2, dtype=np.float32) / np.float32(rot_dim))).astype(np.float32)
    ang = pos.astype(np.float32)[:, None] * inv[None, :]
    return np.cos(ang).astype(np.float32), np.sin(ang).astype(np.float32)


_NC_CACHE = {}


def kernel(x, norm_attn, w_in, lambda_qk, subln, w_out, norm_ffn, w_gate, w_up, w_down, norm_final):
    x = np.asarray(x, dtype=np.float32)
    f32c = lambda a: np.ascontiguousarray(np.asarray(a, dtype=np.float32))
    rep = lambda v: np.ascontiguousarray(np.broadcast_to(np.asarray(v, dtype=np.float32).reshape(1, -1), (128, np.asarray(v).size)))
    if "nc" not in _NC_CACHE:
        _NC_CACHE["nc"] = build_program()
    nc = _NC_CACHE["nc"]
    shared = dict(
        g1=rep(norm_attn[0]), g2=rep(norm_ffn[0]), g3=rep(norm_final),
        w_in=f32c(w_in[0]), w_out=f32c(w_out[0]), w_gate=f32c(w_gate[0]), w_up=f32c(w_up[0]), w_down=f32c(w_down[0]),
        lamq=rep(np.asarray(lambda_qk[0]).reshape(-1)), subl=rep(subln[0]),
        ident=np.eye(128, dtype=np.float32),
    )
    p = np.arange(128)[:, None]
    c = np.arange(1920)[None, :]
    mown = _mult(p - (c - 896)).astype(ml_dtypes.bfloat16)
    in_maps = []
    for core in range(8):
        b, hf = core // 2, core % 2
        own = np.arange(hf * 1024, (hf + 1) * 1024)
        oth = np.arange((1 - hf) * 1024, (2 - hf) * 1024)
        pos = np.concatenate([own, oth])
        xb = np.ascontiguousarray(x[b][pos])
        ca, sa = _rope_tab(pos, 32)
        cb, sb = _rope_tab(pos, 16)
        lay = lambda a: np.ascontiguousarray(a.reshape(NT, 128, -1).transpose(1, 0, 2))
        moth = _mult(p - (c - 1920) - 2048 * hf).astype(ml_dtypes.bfloat16)
        m = dict(shared)
        m.update(xb=xb, cosA=lay(ca), sinA=lay(sa), cosB=lay(cb), sinB=lay(sb), mown=mown, moth=moth)
        in_maps.append(m)
    if _NC_CACHE.get("prep_only"):
        return nc, in_maps
    res = run_bass_kernel_spmd(nc, in_maps, core_ids=list(range(8)))
    out = np.empty((4, SEQ, D), dtype=np.float32)
    for core in range(8):
        b, hf = core // 2, core % 2
        out[b, hf * 1024:(hf + 1) * 1024] = res.results[core]["out"]
    return out
```

```python
from contextlib import ExitStack
import math
import numpy as np
import ml_dtypes
import concourse.bass as bass
import concourse.mybir as mybir
from concourse.bass_utils import run_bass_kernel_spmd

F32 = mybir.dt.float32
BF16 = mybir.dt.bfloat16
AF = mybir.ActivationFunctionType
ALU = mybir.AluOpType
AX = mybir.AxisListType

D = 2048
SEQ = 2048
NT = 16
NQ = 8
DFF = 5632
NF = 44
LAM_INIT = 0.8 - 0.6 * math.exp(0.0)
ROPE_THETA = 500000.0


class Tok:
    __slots__ = ("sem", "val", "eng")

    def __init__(self, sem, val, eng):
        self.sem = sem
        self.val = val
        self.eng = eng


class Buf:
    __slots__ = ("name", "w", "rs", "excl")

    def __init__(self, name, excl=False):
        self.name = name
        self.w = None
        self.rs = {}
        self.excl = excl


class Sched:
    def __init__(self, nc, stack):
        self.nc = nc
        self.stack = stack
        self.E = {}
        for name, eng in (("pe", nc.tensor), ("act", nc.scalar), ("dve", nc.vector),
                          ("pool", nc.gpsimd), ("sp", nc.sync)):
            sem = stack.enter_context(nc.semaphore("s_" + name))
            self.E[name] = dict(eng=eng, sem=sem, cnt=0, waited={}, name=name)

    def new_sem(self, name):
        return self.stack.enter_context(self.nc.semaphore(name))

    def slot(self, name):
        return dict(sem=self.new_sem(name), cnt=0)

    def _wait(self, E, reads, writes):
        need = {}

        def add(tok, raw):
            if tok is None:
                return
            if tok.eng == E["name"]:
                if E["name"] in ("pe", "sp", "pool"):
                    return
            k = tok.sem.num
            if k not in need or need[k][1] < tok.val:
                need[k] = (tok.sem, tok.val, tok.eng)

        for b in reads:
            add(b.w, True)
            if b.excl:
                for t in b.rs.values():
                    if t.eng != E["name"]:
                        add(t, False)
        for b in writes:
            add(b.w, False)
            for t in b.rs.values():
                add(t, False)
        for k, (sem, val, en) in need.items():
            if E["waited"].get(k, 0) >= val:
                continue
            if en in self.E:
                assert self.E[en]["cnt"] >= val, f"wait on unflagged {en} {val}>{self.E[en]['cnt']}"
            E["eng"].wait_ge(sem, val)
            E["waited"][k] = val

    def op(self, en, fn, reads=(), writes=(), sig=True):
        E = self.E[en]
        self._wait(E, reads, writes)
        ins = fn(E["eng"])
        if en == "pe" and not sig:
            tok = Tok(E["sem"], E["cnt"] + 1, en)
        else:
            ins.then_inc(E["sem"], 1)
            E["cnt"] += 1
            tok = Tok(E["sem"], E["cnt"], en)
        for b in reads:
            b.rs[en] = tok
        for b in writes:
            b.w = tok
            b.rs = {}
        return tok

    def dma(self, en, slot, out, in_, reads=(), writes=()):
        E = self.E[en]
        self._wait(E, reads, writes)
        ins = E["eng"].dma_start(out=out, in_=in_)
        ins.then_inc(slot["sem"], 16)
        slot["cnt"] += 16
        tok = Tok(slot["sem"], slot["cnt"], "dma")
        for b in reads:
            b.rs["dma%d" % slot["sem"].num] = tok
        for b in writes:
            b.w = tok
            b.rs = {}
        return tok

    def wait_tok(self, en, tok):
        E = self.E[en]
        if E["waited"].get(tok.sem.num, 0) < tok.val:
            E["eng"].wait_ge(tok.sem, tok.val)
            E["waited"][tok.sem.num] = tok.val

    def barrier(self, engines=("pe", "act", "dve", "sp")):
        for en in engines:
            for e2 in ("pe", "act", "dve"):
                if e2 == en:
                    continue
                E2 = self.E[e2]
                if E2["cnt"] > 0:
                    self.wait_tok(en, Tok(E2["sem"], E2["cnt"], e2))


def bcast(ap, n, pos):
    l = [list(x) for x in ap.ap]
    l.insert(pos, [0, n])
    return bass.AP(ap.tensor, ap.offset, l)


import os
STOP = int(os.environ.get("K_STOP", "99"))


def build_program():
    nc = bass.Bass("TRN2", target_bir_lowering=False)
    dt_in = lambda name, shape, dt=F32: nc.dram_tensor(name, shape, dt, kind="ExternalInput").ap()
    xb_d = dt_in("xb", [SEQ, D])
    g1_d = dt_in("g1", [128, D])
    g2_d = dt_in("g2", [128, D])
    g3_d = dt_in("g3", [128, D])
    w_in_d = dt_in("w_in", [D, 6144])
    w_out_d = dt_in("w_out", [D, D])
    w_gate_d = dt_in("w_gate", [D, DFF])
    w_up_d = dt_in("w_up", [D, DFF])
    w_down_d = dt_in("w_down", [DFF, D])
    cosA_d = dt_in("cosA", [128, NT, 16])
    sinA_d = dt_in("sinA", [128, NT, 16])
    cosB_d = dt_in("cosB", [128, NT, 8])
    sinB_d = dt_in("sinB", [128, NT, 8])
    mown_d = dt_in("mown", [128, 1920], BF16)
    moth_d = dt_in("moth", [128, 1920], BF16)
    lamq_d = dt_in("lamq", [128, 256])
    subl_d = dt_in("subl", [128, 128])
    ident_d = dt_in("ident", [128, 128])
    out_d = nc.dram_tensor("out", [NQ * 128, D], F32, kind="ExternalOutput").ap()

    xb_t = xb_d.rearrange("(t p) d -> t p d", p=128)
    out_t = out_d.rearrange("(t p) d -> t p d", p=128)
    w_in_v = w_in_d.rearrange("(c p) n -> p c n", p=128)
    w_out_v = w_out_d.rearrange("(c p) n -> p c n", p=128)
    w_gate_v = w_gate_d.rearrange("(c p) n -> p c n", p=128)
    w_up_v = w_up_d.rearrange("(c p) n -> p c n", p=128)
    w_down_v = w_down_d.rearrange("(f p) n -> p f n", p=128)

    with ExitStack() as st:
        S = Sched(nc, st)
        ARENA_BYTES = 206 * 1024
        arena = st.enter_context(nc.sbuf_tensor("arena", [128, ARENA_BYTES // 2], BF16))

        def view(off, shape, dt):
            n = 1
            for s_ in shape:
                n *= s_
            esz = 4 if dt == F32 else 2
            assert off % 4 == 0 and off + n * esz <= ARENA_BYTES, (off, shape)
            ap = arena[:, off // 2:(off + n * esz) // 2]
            if dt == F32:
                ap = ap.bitcast(F32)
            if len(shape) == 2:
                ap = ap.rearrange("p (a b) -> p a b", b=shape[1])
            elif len(shape) == 3:
                ap = ap.rearrange("p (a b c) -> p a b c", b=shape[1], c=shape[2])
            elif len(shape) == 4:
                ap = ap.rearrange("p (a b c d) -> p a b c d", b=shape[1], c=shape[2], d=shape[3])
            return ap

        class Bump:
            def __init__(self, base, size):
                self.base, self.size, self.off = base, size, 0

            def reset(self):
                self.off = 0

            def alloc(self, shape, dt):
                n = 1
                for s_ in shape:
                    n *= s_
                sz = (n * (4 if dt == F32 else 2) + 3) // 4 * 4
                assert self.off + sz <= self.size, ("region overflow", self.off, sz, self.size)
                v = view(self.base + self.off, shape, dt)
                self.off += sz
                return v

        o = 0
        R_P = Bump(o, 4096); o += 4096
        R_A = Bump(o, 65536); o += 65536
        R_W = Bump(o, 32768); o += 32768
        R_Y = Bump(o, 32768); o += 32768
        R_S = Bump(o, ARENA_BYTES - o)
        assert R_S.size >= 73000, R_S.size

        psb = [st.enter_context(nc.psum_tensor(f"psb{i}", [128, 512], F32)) for i in range(8)]
        Bps = [Buf(f"ps{i}", excl=True) for i in range(8)]
        ps3b = psb[3][:].bitcast(BF16)
        Bps3 = [Bps[3], Bps[3]]

        identf = R_P.alloc([128], F32)
        identb = R_P.alloc([128], BF16)
        epsA = R_P.alloc([1], F32)
        epsB = R_P.alloc([1], F32)
        lamq = R_P.alloc([256], F32)
        lprod = R_P.alloc([2, 64], F32)
        ldots = R_P.alloc([2], F32)
        lex = R_P.alloc([2], F32)
        neglam = R_P.alloc([1], F32)
        subl = R_P.alloc([128], F32)
        ss1 = R_P.alloc([NT], F32)
        rs1 = R_P.alloc([NT], F32)
        ss2 = R_P.alloc([NQ], F32)
        rs2 = R_P.alloc([NQ], F32)
        ss3 = R_P.alloc([NQ], F32)
        rs3 = R_P.alloc([NQ], F32)
        Bconst = Buf("const")
        Bss1 = [Buf(f"ss1_{t}") for t in range(NT)]
        Bss2 = [Buf(f"ss2_{t}") for t in range(NQ)]
        Bss3 = [Buf(f"ss3_{t}") for t in range(NQ)]

        wslot = [view(R_W.base + i * 16384, [16, 512], BF16) for i in range(2)]
        Bwh = [[Buf("w0a"), Buf("w0b")], [Buf("w1a"), Buf("w1b")]]
        sl_w = [[S.slot("ldw0a"), S.slot("ldw0b")], [S.slot("ldw1a"), S.slot("ldw1b")]]
        wctr = [0]
        wsplit = [None, None]

        def load_unit(src_ap, nchunks=16, split=None):
            i = wctr[0] % 2
            wctr[0] += 1
            same = (wsplit[i] == (split, nchunks))
            wsplit[i] = (split, nchunks)
            wr = (lambda h: [Bwh[i][h]]) if same else (lambda h: [Bwh[i][0], Bwh[i][1]])
            if split == "col":
                for h in range(2):
                    S.dma("pool", sl_w[i][h], wslot[i][:, 0:nchunks, h * 256:(h + 1) * 256], src_ap[:, :, h * 256:(h + 1) * 256],
                          writes=wr(h))
                if not same:
                    Bwh[i][0].w = Tok(sl_w[i][0]["sem"], sl_w[i][0]["cnt"], "dma")
            elif split == "row":
                hc = nchunks // 2
                S.dma("pool", sl_w[i][0], wslot[i][:, 0:hc, :], src_ap[:, 0:hc, :], writes=wr(0))
                S.dma("pool", sl_w[i][1], wslot[i][:, hc:nchunks, :], src_ap[:, hc:nchunks, :], writes=wr(1))
                if not same:
                    Bwh[i][0].w = Tok(sl_w[i][0]["sem"], sl_w[i][0]["cnt"], "dma")
            else:
                tkw = S.dma("pool", sl_w[i][0], wslot[i][:, 0:nchunks, :], src_ap, writes=[Bwh[i][0], Bwh[i][1]])
            return i

        sl_c = S.slot("ldc")
        sl_g = S.slot("ldg")
        sl_t = S.slot("ldt")
        sl_x1 = S.slot("ldxres")
        sl_x = [S.slot("ldx0"), S.slot("ldx1")]
        sl_o = S.slot("sto")

        def finish_early():
            S.barrier()
            tkk = S.dma("sp", sl_o, out_t[0], view(R_A.base, [D], F32))
            S.wait_tok("sp", tkk)
            return nc

        S.dma("sp", sl_c, identf, ident_d[:, :], writes=[Bconst])
        S.dma("sp", sl_c, lamq, lamq_d[:, :], writes=[Bconst])
        tk = S.dma("sp", sl_c, subl, subl_d[:, :], writes=[Bconst])
        S.op("dve", lambda e: e.tensor_copy(identb, identf), reads=[Bconst], writes=[Bconst])
        S.op("dve", lambda e: e.memset(epsA, 1e-6), writes=[Bconst])
        S.op("dve", lambda e: e.memset(epsB, 1e-5), writes=[Bconst])
        for tl in (ss1, ss2, ss3):
            S.op("dve", lambda e: e.memset(tl, 0.0), writes=[Bconst])
        lqv = lamq.rearrange("p (a b d) -> p a b d", a=2, b=2, d=64)
        S.op("dve", lambda e: e.tensor_tensor(lprod, lqv[:, :, 0, :], lqv[:, :, 1, :], ALU.mult), reads=[Bconst], writes=[Bconst])
        S.op("dve", lambda e: e.reduce_sum(ldots, lprod, axis=AX.X), reads=[Bconst], writes=[Bconst])
        S.op("act", lambda e: e.activation(lex, ldots, AF.Exp), reads=[Bconst], writes=[Bconst])
        S.op("dve", lambda e: e.tensor_tensor(neglam, lex[:, 1:2], lex[:, 0:1], ALU.subtract), reads=[Bconst], writes=[Bconst])
        S.op("dve", lambda e: e.tensor_scalar(neglam, neglam, -LAM_INIT, None, ALU.add), reads=[Bconst], writes=[Bconst])

        def norm_head(t, src, Bsrc, gfull, Bg, ss, rs, Bss, xs, Bxs, junk, Bjunk):
            S.op("act", lambda e: e.activation(junk, src, AF.Square, accum_out=ss[:, t:t + 1]),
                 reads=[Bsrc, Bconst], writes=[Bjunk, Bss])
            S.op("act", lambda e: e.activation(rs[:, t:t + 1], ss[:, t:t + 1], AF.Sqrt, scale=1.0 / D, bias=epsA[:, 0:1]),
                 reads=[Bss, Bconst], writes=[Bss])
            S.op("dve", lambda e: e.reciprocal(rs[:, t:t + 1], rs[:, t:t + 1]), reads=[Bss], writes=[Bss])
            S.op("dve", lambda e: e.scalar_tensor_tensor(xs, src, rs[:, t:t + 1], gfull, ALU.mult, ALU.mult),
                 reads=[Bsrc, Bss, Bg], writes=[Bxs])

        def norm_tail(t, xs, Bxs, dstT, BdstT, par):
            for hb in range(2):
                bk = 2 * par + hb
                pv = psb[bk][:].bitcast(BF16)
                for c8 in range(8):
                    c = hb * 8 + c8
                    S.op("pe", lambda e: e.transpose(pv[:, c8 * 128:(c8 + 1) * 128], xs[:, c * 128:(c + 1) * 128], identb),
                         reads=[Bxs, Bconst], writes=[Bps[bk]], sig=(c8 == 7))
                dst = dstT[:, hb * 8:hb * 8 + 8, t * 128:(t + 1) * 128]
                srcv = pv.rearrange("p (a b) -> p a b", b=128)
                if hb == 0:
                    S.op("act", lambda e: e.copy(dst, srcv), reads=[Bps[bk]], writes=[BdstT])
                else:
                    S.op("dve", lambda e: e.tensor_copy(dst, srcv), reads=[Bps[bk]], writes=[BdstT])

        hT = R_A.alloc([16, SEQ], BF16)
        BhT = [Buf(f"hT{t}") for t in range(NT)]
        R_S.reset()
        NXB = 4
        xin = [R_S.alloc([D], F32) for _ in range(NXB)]
        xs_b = [R_S.alloc([D], BF16) for _ in range(2)]
        gfull = R_S.alloc([D], F32)
        junk = R_S.alloc([D], BF16)
        Bxin = [Buf(f"xin{i}") for i in range(NXB)]
        sl_xp = [S.slot(f"ldxp{i}") for i in range(NXB)]
        Bxs = [Buf("xs0"), Buf("xs1")]
        Bg = Buf("gfull")
        Bjunk = Buf("junk")
        S.dma("sp", sl_g, gfull, g1_d[:, :], writes=[Bg])
        GROUPS = [("A", 0), ("A", 1), ("B", 0), ("B", 1)]

        def in_units(G):
            typ, gi = GROUPS[G]
            base = 0 if typ == "A" else 6
            return dict(q=base + gi, k=base + 2 + gi, v=base + 4 + gi)

        def in_unit_ap(n):
            return w_in_v[:, :, n * 512:(n + 1) * 512]

        pending_units = {}
        pending_units[(0, "k")] = load_unit(in_unit_ap(in_units(0)["k"]))
        pending_units[(0, "v")] = load_unit(in_unit_ap(in_units(0)["v"]))

        for t in range(NT + 1):
            if t < NT:
                S.dma("sp", sl_xp[t % NXB], xin[t % NXB], xb_t[t], writes=[Bxin[t % NXB]])
                norm_head(t, xin[t % NXB], Bxin[t % NXB], gfull, Bg, ss1, rs1, Bss1[t], xs_b[t % 2], Bxs[t % 2], junk, Bjunk)
            if t >= 1:
                norm_tail(t - 1, xs_b[(t - 1) % 2], Bxs[(t - 1) % 2], hT, BhT[t - 1], (t - 1) % 2)
        S.barrier()
        if STOP == 1:
            return finish_early()

        R_S.reset()
        QT = R_S.alloc([2, 4, NQ * 128], BF16)
        KT = R_S.alloc([4, SEQ], BF16)
        Vaug = R_S.alloc([NT, 4, 130], BF16)
        mown = R_S.alloc([1920], BF16)
        moth = R_S.alloc([1920], BF16)
        cosA = R_S.alloc([NT, 16], F32)
        sinA = R_S.alloc([NT, 16], F32)
        cosB = R_S.alloc([NT, 8], F32)
        sinB = R_S.alloc([NT, 8], F32)
        tm = [R_S.alloc([512], BF16) for _ in range(2)]
        NPT = 4
        pt = [R_S.alloc([512], BF16) for _ in range(NPT)]
        rt1 = R_S.alloc([128], F32)
        rt2 = R_S.alloc([128], F32)
        rsrc = R_S.alloc([128], F32)
        rden = R_S.alloc([4], F32)
        o1n = R_S.alloc([4, 128], F32)
        yb = R_S.alloc([4, 128], F32)
        ssq = R_S.alloc([4], F32)
        ytm = [R_S.alloc([4, 128], BF16) for _ in range(2)]
        yT = R_Y.alloc([16, NQ * 128], BF16)
        BQT = [Buf(f"QT{t}") for t in range(NQ)]
        BKT = [Buf(f"KT{t}") for t in range(NT)]
        BV = [Buf(f"V{t}") for t in range(NT)]
        Btab = Buf("tables")
        Btm = [Buf("tm0"), Buf("tm1")]
        Bpt = [Buf(f"pt{i}") for i in range(NPT)]
        Brt = Buf("rt")
        Brs = Buf("rsrc")
        Bep = Buf("ep")
        Bo1n = Buf("o1n")
        Bytm = [Buf("ytm0"), Buf("ytm1")]
        ByT = [Buf(f"yT{q}") for q in range(2)]
        S.dma("sp", sl_t, mown, mown_d[:, :], writes=[Btab])
        S.dma("sp", sl_t, moth, moth_d[:, :], writes=[Btab])
        S.dma("sp", sl_t, cosA, cosA_d[:, :, :], writes=[Btab])
        S.dma("sp", sl_t, sinA, sinA_d[:, :, :], writes=[Btab])
        S.dma("sp", sl_t, cosB, cosB_d[:, :, :], writes=[Btab])
        S.dma("sp", sl_t, sinB, sinB_d[:, :, :], writes=[Btab])
        S.op("dve", lambda e: e.memset(QT, 0.0), writes=BQT)
        S.op("dve", lambda e: e.memset(Vaug[:, :, :, 128:130], 1.0), writes=BV)

        if STOP == 10:
            return finish_early()
        pbank = [0]

        def next_pbank():
            b = (0, 1, 2, 4)[pbank[0] % 4]
            pbank[0] += 1
            return b

        trh = [0]

        def proj(bk, t, wi):
            for c in range(16):
                S.op("pe", lambda e: e.matmul(psb[bk][:], hT[:, c, t * 128:(t + 1) * 128], wslot[wi][:, c, :],
                                              start=(c == 0), stop=(c == 15)),
                     reads=[BhT[t]] + Bwh[wi], writes=[Bps[bk]], sig=(c == 15))

        def rope_evac(bk, t, typ, dst, Bdst):
            if typ == "A":
                nh, hd, r2, ct, stb = 4, 128, 16, cosA, sinA
            else:
                nh, hd, r2, ct, stb = 8, 64, 8, cosB, sinB
            src = psb[bk][:].rearrange("p (h d) -> p h d", d=hd)
            dv = dst.rearrange("p (h d) -> p h d", d=hd)
            rs_ = rsrc[:, 0:nh * 2 * r2].rearrange("p (h d) -> p h d", d=2 * r2)
            t1 = rt1[:, 0:nh * 2 * r2].rearrange("p (h d) -> p h d", d=2 * r2)
            t2 = rt2[:, 0:nh * 2 * r2].rearrange("p (h d) -> p h d", d=2 * r2)
            cb = bcast(ct[:, t, :], nh, 1)
            sb_ = bcast(stb[:, t, :], nh, 1)
            S.op("act", lambda e: e.copy(dv[:, :, 2 * r2:hd], src[:, :, 2 * r2:hd]), reads=[Bps[bk]], writes=[Bdst])
            S.op("act", lambda e: e.copy(rs_, src[:, :, 0:2 * r2]), reads=[Bps[bk]], writes=[Brs])
            if STOP == 16:
                return
            S.op("dve", lambda e: e.tensor_tensor(t1[:, :, 0:r2], rs_[:, :, 0:r2], cb, ALU.mult), reads=[Brs, Btab], writes=[Brt])
            if STOP == 17:
                return
            S.op("dve", lambda e: e.tensor_tensor(t1[:, :, r2:2 * r2], rs_[:, :, r2:2 * r2], cb, ALU.mult), reads=[Brs, Btab], writes=[Brt])
            S.op("dve", lambda e: e.tensor_tensor(t2[:, :, 0:r2], rs_[:, :, r2:2 * r2], sb_, ALU.mult), reads=[Brs, Btab], writes=[Brt])
            S.op("dve", lambda e: e.tensor_tensor(t2[:, :, r2:2 * r2], rs_[:, :, 0:r2], sb_, ALU.mult), reads=[Brs, Btab], writes=[Brt])
            S.op("dve", lambda e: e.tensor_tensor(dv[:, :, 0:r2], t1[:, :, 0:r2], t2[:, :, 0:r2], ALU.subtract), reads=[Brt], writes=[Bdst])
            S.op("dve", lambda e: e.tensor_tensor(dv[:, :, r2:2 * r2], t1[:, :, r2:2 * r2], t2[:, :, r2:2 * r2], ALU.add), reads=[Brt], writes=[Bdst])

        psbf = [psb[i][:].bitcast(BF16) for i in range(8)]

        def transpose4(srcap, Bsrc, bank=3):
            for h in range(4):
                S.op("pe", lambda e: e.transpose(psbf[bank][:, h * 128:(h + 1) * 128],
                                                 srcap[:, h * 128:(h + 1) * 128], identb),
                     reads=[Bsrc, Bconst], writes=[Bps[bank]], sig=(h == 3))
            return psbf[bank][:, 0:512]

        scaleA = 128.0 ** -0.5
        scaleB = 64.0 ** -0.5
        oset_ctr = [0]
        sb_ctr = [0]

        for G in range(4):
            typ, gi = GROUPS[G]
            U = in_units(G)
            if G == 2:
                S.op("dve", lambda e: e.memset(QT, 0.0), writes=BQT)
            wi_k = pending_units.pop((G, "k"))
            wi_v = pending_units.pop((G, "v"))
            def k_tail(t):
                tmi = t % 2
                pv = transpose4(tm[tmi], Btm[tmi], 3)
                S.op("act", lambda e: e.copy(KT[:, :, t * 128:(t + 1) * 128], pv.rearrange("p (h k) -> p h k", k=128)),
                     reads=[Bps[3]], writes=[BKT[t]])

            for t in range(NT + 1):
                if t < NT:
                    bk = next_pbank()
                    proj(bk, t, wi_k)
                    rope_evac(bk, t, typ, tm[t % 2], Btm[t % 2])
                if t >= 1:
                    k_tail(t - 1)
            if STOP in (11, 14, 15, 16, 17):
                return finish_early()
            wi_q = load_unit(in_unit_ap(U["q"]))
            for t in range(NT):
                bk = next_pbank()
                proj(bk, t, wi_v)
                S.op("act", lambda e: e.copy(Vaug[:, t, :, 0:128], psb[bk][:].rearrange("p (h d) -> p h d", d=128)),
                     reads=[Bps[bk]], writes=[BV[t]])
            if G + 1 < 4:
                pending_units[(G + 1, "k")] = load_unit(in_unit_ap(in_units(G + 1)["k"]))
            def q_tail(t):
                tmi = t % 2
                pvf = transpose4(tm[tmi], Btm[tmi], 3)
                pv = pvf.rearrange("p (h k) -> p h k", k=128)
                if typ == "A":
                    S.op("act", lambda e: e.copy(QT[:, 0, :, t * 128:(t + 1) * 128], pv), reads=[Bps[3]], writes=[BQT[t]])
                else:
                    S.op("act", lambda e: e.copy(QT[0:64, 0, :, t * 128:(t + 1) * 128], pv[0:64]), reads=[Bps[3]], writes=[BQT[t]])
                    S.op("act", lambda e: e.copy(QT[64:128, 1, :, t * 128:(t + 1) * 128], pv[64:128]), reads=[Bps[3]], writes=[BQT[t]])

            for t in range(NQ + 1):
                if t < NQ:
                    bk = next_pbank()
                    proj(bk, t, wi_q)
                    rope_evac(bk, t, typ, tm[t % 2], Btm[t % 2])
                if t >= 1:
                    q_tail(t - 1)
            if G + 1 < 4:
                pending_units[(G + 1, "v")] = load_unit(in_unit_ap(in_units(G + 1)["v"]))
            else:
                pending_units["o0"] = load_unit(w_out_v[:, :, 0:512])
                pending_units["o1"] = load_unit(w_out_v[:, :, 512:1024])

            if STOP == 13:
                return finish_early()
            maps = [0] if typ == "A" else [0, 1]
            its = [(hh, qb, m, kt) for hh in range(4) for qb in range(2) for m in maps for kt in range(NT)]
            scale = scaleA if typ == "A" else scaleB
            state = {}

            def issue_S(i):
                hh, qb, m, kt = its[i]
                sbk = sb_ctr[0] % 4
                sb_ctr[0] += 1
                pi = i % NPT
                S.op("pe", lambda e: e.matmul(psb[sbk][:], KT[:, hh, kt * 128:(kt + 1) * 128],
                                              QT[:, m, hh, qb * 512:(qb + 1) * 512], start=True, stop=True),
                     reads=[BKT[kt]] + BQT[qb * 4:qb * 4 + 4], writes=[Bps[sbk]], sig=True)
                S.op("act", lambda e: e.activation(pt[pi], psb[sbk][:], AF.Exp, scale=scale), reads=[Bps[sbk]], writes=[Bpt[pi]])
                if typ == "A":
                    u = qb * 512 - 128 * kt
                    if kt < 8:
                        msk = mown[:, u + 896:u + 896 + 512]
                    else:
                        msk = moth[:, u + 1920:u + 1920 + 512]
                    S.op("dve", lambda e: e.tensor_tensor(pt[pi], pt[pi], msk, ALU.mult), reads=[Bpt[pi], Btab], writes=[Bpt[pi]])

            def issue_PV(i):
                hh, qb, m, kt = its[i]
                state["i"] = i
                pi = i % NPT
                if kt == 0:
                    state["os"] = oset_ctr[0] % 2
                    oset_ctr[0] += 1
                os_ = state["os"]
                banks = (4 + 2 * os_, 5 + 2 * os_)
                for j in range(4):
                    bk = banks[j // 2]
                    col = (j % 2) * 130
                    S.op("pe", lambda e: e.matmul(psb[bk][:, col:col + 129], pt[pi][:, j * 128:(j + 1) * 128],
                                                  Vaug[:, kt, hh, 0:129], start=(kt == 0 and j % 2 == 0),
                                                  stop=(kt == NT - 1 and j % 2 == 1)),
                         reads=[Bpt[pi], BV[kt]], writes=[Bps[bk]], sig=(j == 3))
                if kt == NT - 1:
                    epilogue(hh, qb, m, banks)

            def epilogue(hh, qb, m, banks):
                head = (0 if typ == "A" else 8) + gi * 4 + hh
                ov = [psb[b][:, 0:260].rearrange("p (j c) -> p j c", c=130) for b in banks]
                for bi in range(2):
                    S.op("dve", lambda e: e.reciprocal(rden[:, 2 * bi:2 * bi + 2].rearrange("p (j o) -> p j o", o=1), ov[bi][:, :, 128:129]),
                         reads=[Bps[banks[bi]]], writes=[Bep])
                yi = oset_ctr[0] % 2
                if typ == "A":
                    for bi in range(2):
                        S.op("dve", lambda e: e.tensor_tensor(ytm[yi][:, 2 * bi:2 * bi + 2, :], ov[bi][:, :, 0:128],
                                                              bcast(rden[:, 2 * bi:2 * bi + 2], 128, 2), ALU.mult),
                             reads=[Bps[banks[bi]], Bep], writes=[Bytm[yi]])
                elif m == 0:
                    for bi in range(2):
                        S.op("dve", lambda e: e.tensor_tensor(o1n[:, 2 * bi:2 * bi + 2, :], ov[bi][:, :, 0:128],
                                                              bcast(rden[:, 2 * bi:2 * bi + 2], 128, 2), ALU.mult),
                             reads=[Bps[banks[bi]], Bep], writes=[Bo1n])
                    return
                else:
                    for bi in range(2):
                        S.op("dve", lambda e: e.tensor_tensor(yb[:, 2 * bi:2 * bi + 2, :], ov[bi][:, :, 0:128],
                                                              bcast(rden[:, 2 * bi:2 * bi + 2], 128, 2), ALU.mult),
                             reads=[Bps[banks[bi]], Bep], writes=[Bep])
                    S.op("dve", lambda e: e.scalar_tensor_tensor(yb, yb, neglam[:, 0:1], o1n, ALU.mult, ALU.add),
                         reads=[Bep, Bo1n, Bconst], writes=[Bep])
                    S.op("dve", lambda e: e.tensor_tensor(o1n, yb, yb, ALU.mult), reads=[Bep, Bo1n], writes=[Bo1n])
                    S.op("dve", lambda e: e.reduce_sum(ssq, o1n, axis=AX.X), reads=[Bo1n], writes=[Bep])

                def part2b():
                    tb_ = sb_ctr[0] % 4
                    sb_ctr[0] += 1
                    pvf = transpose4(ytm[yi].rearrange("p j d -> p (j d)"), Bytm[yi], tb_)
                    S.op("dve", lambda e: e.tensor_copy(yT[:, head, qb * 512:(qb + 1) * 512], pvf),
                         reads=[Bps[tb_]], writes=[ByT[qb]])

                def part2():
                    if typ == "B":
                        S.op("act", lambda e: e.activation(ssq, ssq, AF.Ln, scale=1.0 / 128.0, bias=epsB[:, 0:1]), reads=[Bep, Bconst], writes=[Bep])
                        S.op("act", lambda e: e.activation(ssq, ssq, AF.Exp, scale=-0.5), reads=[Bep], writes=[Bep])
                        S.op("dve", lambda e: e.tensor_scalar(ssq, ssq, 1.0 - LAM_INIT, None, ALU.mult), reads=[Bep], writes=[Bep])
                        S.op("dve", lambda e: e.tensor_tensor(yb, yb, bcast(ssq, 128, 2), ALU.mult), reads=[Bep], writes=[Bep])
                        S.op("dve", lambda e: e.tensor_tensor(ytm[yi], yb, bcast(subl, 4, 1), ALU.mult), reads=[Bep, Bconst], writes=[Bytm[yi]])
                        deferred.append((state["i"] + DEL, part2b))
                    else:
                        part2b()

                deferred.append((state["i"] + DEL, part2))

            DEL = 8
            deferred = []
            LA = 3
            for i in range(len(its) + LA):
                if i < len(its):
                    issue_S(i)
                if i - LA >= 0:
                    issue_PV(i - LA)
                    due = [d for d in deferred if d[0] <= i - LA]
                    for d in due:
                        deferred.remove(d)
                        d[1]()
            while deferred:
                deferred.pop(0)[1]()
            if STOP == 20 + G:
                return finish_early()
        S.barrier()
        if STOP == 2:
            return finish_early()

        x1 = view(R_A.base, [NQ, D], F32)
        Bx1 = [Buf(f"x1_{t}") for t in range(NQ)]
        for t in range(NQ):
            tkx = S.dma("sp", sl_x1, x1[:, t, :], xb_t[t], writes=[Bx1[t]])
        for t in range(NQ):
            Bx1[t].w = tkx
        obank = [0]
        for n in range(4):
            wi = pending_units.pop(f"o{n}")
            for t in range(NQ):
                bk = obank[0] % 8
                obank[0] += 1
                for c in range(16):
                    S.op("pe", lambda e: e.matmul(psb[bk][:], yT[:, c, t * 128:(t + 1) * 128], wslot[wi][:, c, :],
                                                  start=(c == 0), stop=(c == 15)),
                         reads=[ByT[t // 4]] + Bwh[wi], writes=[Bps[bk]], sig=(c == 15))
                S.op("dve", lambda e: e.tensor_tensor(x1[:, t, n * 512:(n + 1) * 512], x1[:, t, n * 512:(n + 1) * 512], psb[bk][:], ALU.add),
                     reads=[Bps[bk], Bx1[t]], writes=[Bx1[t]])
            if n + 2 < 4:
                pending_units[f"o{n + 2}"] = load_unit(w_out_v[:, :, (n + 2) * 512:(n + 3) * 512])
            elif n == 2:
                pending_units[("g", 0, 0)] = load_unit(w_gate_v[:, :, 0:512], split="col")
            else:
                pending_units[("u", 0, 0)] = load_unit(w_up_v[:, :, 0:512], split="col")
        S.barrier()
        if STOP == 3:
            return finish_early()

        R_S.reset()
        h2T = R_Y.base
        h2T = view(R_Y.base, [16, NQ * 128], BF16)
        Bh2T = [Buf(f"h2T{t}") for t in range(NQ)]
        aT = R_S.alloc([NF, 512], BF16)
        sg = [R_S.alloc([512], F32) for _ in range(4)]
        gfull = R_S.alloc([D], F32)
        junk = R_S.alloc([D], BF16)
        xs_b = [R_S.alloc([D], BF16) for _ in range(2)]
        Bg = Buf("gfull2")
        Bjunk = Buf("junk2")
        Bxs = [Buf("xs2_0"), Buf("xs2_1")]
        BaT = [Buf(f"aT{f}") for f in range(NF)]
        Bsg = [Buf(f"sg{i}") for i in range(4)]
        S.dma("sp", sl_g, gfull, g2_d[:, :], writes=[Bg])
        for t in range(NQ + 1):
            if t < NQ:
                norm_head(t, x1[:, t, :], Bx1[t], gfull, Bg, ss2, rs2, Bss2[t], xs_b[t % 2], Bxs[t % 2], junk, Bjunk)
            if t >= 1:
                norm_tail(t - 1, xs_b[(t - 1) % 2], Bxs[(t - 1) % 2], h2T, Bh2T[t - 1], (t - 1) % 2)
        S.barrier(engines=("sp",))
        Bg3 = Buf("gfull3")
        S.dma("sp", sl_g, gfull, g3_d[:, :], writes=[Bg3])

        for blk in range(2):
            tb = slice(blk * 512, (blk + 1) * 512)
            for uu in range(11):
                wg = pending_units.pop(("g", blk, uu))
                wu = pending_units.pop(("u", blk, uu))
                for i in range(4):
                    f = uu * 4 + i
                    bk = f % 2
                    for c in range(16):
                        S.op("pe", lambda e: e.matmul(psb[bk][:], wslot[wg][:, c, i * 128:(i + 1) * 128], h2T[:, c, tb],
                                                      start=(c == 0), stop=(c == 15)),
                             reads=[Bwh[wg][i // 2]] + Bh2T[blk * 4:blk * 4 + 4], writes=[Bps[bk]], sig=(c == 15))
                    S.op("act", lambda e: e.activation(sg[i], psb[bk][:], AF.Silu), reads=[Bps[bk]], writes=[Bsg[i]])
                if uu + 1 < 11:
                    pending_units[("g", blk, uu + 1)] = load_unit(w_gate_v[:, :, (uu + 1) * 512:(uu + 2) * 512], split="col")
                else:
                    pending_units[("d", blk, 0)] = load_unit(w_down_v[:, 0:16, 0:512], split="row")
                for i in range(4):
                    f = uu * 4 + i
                    bk = 2 + f % 2
                    for c in range(16):
                        S.op("pe", lambda e: e.matmul(psb[bk][:], wslot[wu][:, c, i * 128:(i + 1) * 128], h2T[:, c, tb],
                                                      start=(c == 0), stop=(c == 15)),
                             reads=[Bwh[wu][i // 2]] + Bh2T[blk * 4:blk * 4 + 4], writes=[Bps[bk]], sig=(c == 15))
                    S.op("dve", lambda e: e.tensor_tensor(aT[:, f, :], sg[i], psb[bk][:], ALU.mult),
                         reads=[Bps[bk], Bsg[i]], writes=[BaT[f]])
                if uu + 1 < 11:
                    pending_units[("u", blk, uu + 1)] = load_unit(w_up_v[:, :, (uu + 1) * 512:(uu + 2) * 512], split="col")
                else:
                    pending_units[("d", blk, 1)] = load_unit(w_down_v[:, 16:32, 0:512], split="row")
            dunits = [(n, fu) for n in range(4) for fu in range(3)]
            frange = [(0, 16), (16, 32), (32, 44)]
            for di, (n, fu) in enumerate(dunits):
                wi = pending_units.pop(("d", blk, di))
                f0, f1 = frange[fu]
                hc = (f1 - f0) // 2
                halves = [(f0, f0 + hc), (f0 + hc, f1)]
                for hsel, (h0, h1) in enumerate(halves):
                    for j in range(4):
                        bk = 4 + j
                        for f in range(h0, h1):
                            S.op("pe", lambda e: e.matmul(psb[bk][:], aT[:, f, j * 128:(j + 1) * 128], wslot[wi][:, f - f0, :],
                                                          start=(f == 0), stop=(f == NF - 1)),
                                 reads=[BaT[f], Bwh[wi][hsel]], writes=[Bps[bk]], sig=(f == h1 - 1))
                        if fu == 2 and hsel == 1:
                            T = blk * 4 + j
                            S.op("dve", lambda e: e.tensor_tensor(x1[:, T, n * 512:(n + 1) * 512], x1[:, T, n * 512:(n + 1) * 512], psb[bk][:], ALU.add),
                                 reads=[Bps[bk], Bx1[T]], writes=[Bx1[T]])
                nx = di + 2
                if nx < len(dunits):
                    n2, fu2 = dunits[nx]
                    a0, a1 = frange[fu2]
                    pending_units[("d", blk, nx)] = load_unit(w_down_v[:, a0:a1, n2 * 512:(n2 + 1) * 512], nchunks=a1 - a0, split="row")
                elif blk == 0:
                    if nx == len(dunits):
                        pending_units[("g", 1, 0)] = load_unit(w_gate_v[:, :, 0:512], split="col")
                    else:
                        pending_units[("u", 1, 0)] = load_unit(w_up_v[:, :, 0:512], split="col")
            for j in range(4):
                T = blk * 4 + j
                src = x1[:, T, :]
                S.op("act", lambda e: e.activation(junk, src, AF.Square, accum_out=ss3[:, T:T + 1]),
                     reads=[Bx1[T], Bconst], writes=[Bjunk, Bss3[T]])
                S.op("act", lambda e: e.activation(rs3[:, T:T + 1], ss3[:, T:T + 1], AF.Sqrt, scale=1.0 / D, bias=epsA[:, 0:1]),
                     reads=[Bss3[T], Bconst], writes=[Bss3[T]])
                S.op("dve", lambda e: e.reciprocal(rs3[:, T:T + 1], rs3[:, T:T + 1]), reads=[Bss3[T]], writes=[Bss3[T]])
                S.op("dve", lambda e: e.scalar_tensor_tensor(src, src, rs3[:, T:T + 1], gfull, ALU.mult, ALU.mult),
                     reads=[Bx1[T], Bss3[T], Bg3], writes=[Bx1[T]])
                last = S.dma("sp", sl_o, out_t[T], src, reads=[Bx1[T]])
        S.wait_tok("sp", last)
        assert not pending_units, pending_units
    return nc


def _mult(d):
    d = np.asarray(d)
    m = (np.abs(d) <= 64).astype(np.float32)
    m += ((d % 4 == 0) & (np.abs(d) <= 256)).astype(np.float32)
    m += ((d % 16 == 0) & (np.abs(d) <= 1024)).astype(np.float32)
    return m


def _rope_tab(pos, rot_dim):
    inv = (np.float32(ROPE_THETA) ** (-np.arange(0, rot_dim, 2, dtype=np.float32) / np.float32(rot_dim))).astype(np.float32)
    ang = pos.astype(np.float32)[:, None] * inv[None, :]
    return np.cos(ang).astype(np.float32), np.sin(ang).astype(np.float32)


_NC_CACHE = {}


def kernel(x, norm_attn, w_in, lambda_qk, subln, w_out, norm_ffn, w_gate, w_up, w_down, norm_final):
    x = np.asarray(x, dtype=np.float32)
    f32c = lambda a: np.ascontiguousarray(np.asarray(a, dtype=np.float32))
    rep = lambda v: np.ascontiguousarray(np.broadcast_to(np.asarray(v, dtype=np.float32).reshape(1, -1), (128, np.asarray(v).size)))
    if "nc" not in _NC_CACHE:
        _NC_CACHE["nc"] = build_program()
    nc = _NC_CACHE["nc"]
    shared = dict(
        g1=rep(norm_attn[0]), g2=rep(norm_ffn[0]), g3=rep(norm_final),
        w_in=f32c(w_in[0]), w_out=f32c(w_out[0]), w_gate=f32c(w_gate[0]), w_up=f32c(w_up[0]), w_down=f32c(w_down[0]),
        lamq=rep(np.asarray(lambda_qk[0]).reshape(-1)), subl=rep(subln[0]),
        ident=np.eye(128, dtype=np.float32),
    )
    p = np.arange(128)[:, None]
    c = np.arange(1920)[None, :]
    mown = _mult(p - (c - 896)).astype(ml_dtypes.bfloat16)
    in_maps = []
    for core in range(8):
        b, hf = core // 2, core % 2
        own = np.arange(hf * 1024, (hf + 1) * 1024)
        oth = np.arange((1 - hf) * 1024, (2 - hf) * 1024)
        pos = np.concatenate([own, oth])
        xb = np.ascontiguousarray(x[b][pos])
        ca, sa = _rope_tab(pos, 32)
        cb, sb = _rope_tab(pos, 16)
        lay = lambda a: np.ascontiguousarray(a.reshape(NT, 128, -1).transpose(1, 0, 2))
        moth = _mult(p - (c - 1920) - 2048 * hf).astype(ml_dtypes.bfloat16)
        m = dict(shared)
        m.update(xb=xb, cosA=lay(ca), sinA=lay(sa), cosB=lay(cb), sinB=lay(sb), mown=mown, moth=moth)
        in_maps.append(m)
    if _NC_CACHE.get("prep_only"):
        return nc, in_maps
    res = run_bass_kernel_spmd(nc, in_maps, core_ids=list(range(8)))
    out = np.empty((4, SEQ, D), dtype=np.float32)
    for core in range(8):
        b, hf = core // 2, core % 2
        out[b, hf * 1024:(hf + 1) * 1024] = res.results[core]["out"]
    return out
```

```python
from contextlib import ExitStack
import math
import numpy as np
import ml_dtypes
import concourse.bass as bass
import concourse.mybir as mybir
from concourse.bass_utils import run_bass_kernel_spmd

F32 = mybir.dt.float32
BF16 = mybir.dt.bfloat16
AF = mybir.ActivationFunctionType
ALU = mybir.AluOpType
AX = mybir.AxisListType

D = 2048
SEQ = 2048
NT = 16
NQ = 8
DFF = 5632
NF = 44
LAM_INIT = 0.8 - 0.6 * math.exp(0.0)
ROPE_THETA = 500000.0


class Tok:
    __slots__ = ("sem", "val", "eng")

    def __init__(self, sem, val, eng):
        self.sem = sem
        self.val = val
        self.eng = eng


class Buf:
    __slots__ = ("name", "w", "rs", "excl")

    def __init__(self, name, excl=False):
        self.name = name
        self.w = None
        self.rs = {}
        self.excl = excl


class Sched:
    def __init__(self, nc, stack):
        self.nc = nc
        self.stack = stack
        self.E = {}
        for name, eng in (("pe", nc.tensor), ("act", nc.scalar), ("dve", nc.vector),
                          ("pool", nc.gpsimd), ("sp", nc.sync)):
            sem = stack.enter_context(nc.semaphore("s_" + name))
            self.E[name] = dict(eng=eng, sem=sem, cnt=0, waited={}, name=name)

    def new_sem(self, name):
        return self.stack.enter_context(self.nc.semaphore(name))

    def slot(self, name):
        return dict(sem=self.new_sem(name), cnt=0)

    def _wait(self, E, reads, writes):
        need = {}

        def add(tok, raw):
            if tok is None:
                return
            if tok.eng == E["name"]:
                if E["name"] in ("pe", "sp", "pool"):
                    return
            k = tok.sem.num
            if k not in need or need[k][1] < tok.val:
                need[k] = (tok.sem, tok.val, tok.eng)

        for b in reads:
            add(b.w, True)
            if b.excl:
                for t in b.rs.values():
                    if t.eng != E["name"]:
                        add(t, False)
        for b in writes:
            add(b.w, False)
            for t in b.rs.values():
                add(t, False)
        for k, (sem, val, en) in need.items():
            if E["waited"].get(k, 0) >= val:
                continue
            if en in self.E:
                assert self.E[en]["cnt"] >= val, f"wait on unflagged {en} {val}>{self.E[en]['cnt']}"
            E["eng"].wait_ge(sem, val)
            E["waited"][k] = val

    def op(self, en, fn, reads=(), writes=(), sig=True):
        E = self.E[en]
        self._wait(E, reads, writes)
        ins = fn(E["eng"])
        if en == "pe" and not sig:
            tok = Tok(E["sem"], E["cnt"] + 1, en)
        else:
            ins.then_inc(E["sem"], 1)
            E["cnt"] += 1
            tok = Tok(E["sem"], E["cnt"], en)
        for b in reads:
            b.rs[en] = tok
        for b in writes:
            b.w = tok
            b.rs = {}
        return tok

    def dma(self, en, slot, out, in_, reads=(), writes=()):
        E = self.E[en]
        self._wait(E, reads, writes)
        ins = E["eng"].dma_start(out=out, in_=in_)
        ins.then_inc(slot["sem"], 16)
        slot["cnt"] += 16
        tok = Tok(slot["sem"], slot["cnt"], "dma")
        for b in reads:
            b.rs["dma%d" % slot["sem"].num] = tok
        for b in writes:
            b.w = tok
            b.rs = {}
        return tok

    def wait_tok(self, en, tok):
        E = self.E[en]
        if E["waited"].get(tok.sem.num, 0) < tok.val:
            E["eng"].wait_ge(tok.sem, tok.val)
            E["waited"][tok.sem.num] = tok.val

    def barrier(self, engines=("pe", "act", "dve", "sp")):
        for en in engines:
            for e2 in ("pe", "act", "dve"):
                if e2 == en:
                    continue
                E2 = self.E[e2]
                if E2["cnt"] > 0:
                    self.wait_tok(en, Tok(E2["sem"], E2["cnt"], e2))


def bcast(ap, n, pos):
    l = [list(x) for x in ap.ap]
    l.insert(pos, [0, n])
    return bass.AP(ap.tensor, ap.offset, l)


import os
STOP = int(os.environ.get("K_STOP", "99"))


def build_program():
    nc = bass.Bass("TRN2", target_bir_lowering=False)
    dt_in = lambda name, shape, dt=F32: nc.dram_tensor(name, shape, dt, kind="ExternalInput").ap()
    xb_d = dt_in("xb", [SEQ, D])
    g1_d = dt_in("g1", [128, D])
    g2_d = dt_in("g2", [128, D])
    g3_d = dt_in("g3", [128, D])
    w_in_d = dt_in("w_in", [D, 6144])
    w_out_d = dt_in("w_out", [D, D])
    w_gate_d = dt_in("w_gate", [D, DFF])
    w_up_d = dt_in("w_up", [D, DFF])
    w_down_d = dt_in("w_down", [DFF, D])
    cosA_d = dt_in("cosA", [128, NT, 16])
    sinA_d = dt_in("sinA", [128, NT, 16])
    cosB_d = dt_in("cosB", [128, NT, 8])
    sinB_d = dt_in("sinB", [128, NT, 8])
    mown_d = dt_in("mown", [128, 1920], BF16)
    moth_d = dt_in("moth", [128, 1920], BF16)
    lamq_d = dt_in("lamq", [128, 256])
    subl_d = dt_in("subl", [128, 128])
    ident_d = dt_in("ident", [128, 128])
    out_d = nc.dram_tensor("out", [NQ * 128, D], F32, kind="ExternalOutput").ap()

    xb_t = xb_d.rearrange("(t p) d -> t p d", p=128)
    out_t = out_d.rearrange("(t p) d -> t p d", p=128)
    w_in_v = w_in_d.rearrange("(c p) n -> p c n", p=128)
    w_out_v = w_out_d.rearrange("(c p) n -> p c n", p=128)
    w_gate_v = w_gate_d.rearrange("(c p) n -> p c n", p=128)
    w_up_v = w_up_d.rearrange("(c p) n -> p c n", p=128)
    w_down_v = w_down_d.rearrange("(f p) n -> p f n", p=128)

    with ExitStack() as st:
        S = Sched(nc, st)
        ARENA_BYTES = 206 * 1024
        arena = st.enter_context(nc.sbuf_tensor("arena", [128, ARENA_BYTES // 2], BF16))

        def view(off, shape, dt):
            n = 1
            for s_ in shape:
                n *= s_
            esz = 4 if dt == F32 else 2
            assert off % 4 == 0 and off + n * esz <= ARENA_BYTES, (off, shape)
            ap = arena[:, off // 2:(off + n * esz) // 2]
            if dt == F32:
                ap = ap.bitcast(F32)
            if len(shape) == 2:
                ap = ap.rearrange("p (a b) -> p a b", b=shape[1])
            elif len(shape) == 3:
                ap = ap.rearrange("p (a b c) -> p a b c", b=shape[1], c=shape[2])
            elif len(shape) == 4:
                ap = ap.rearrange("p (a b c d) -> p a b c d", b=shape[1], c=shape[2], d=shape[3])
            return ap

        class Bump:
            def __init__(self, base, size):
                self.base, self.size, self.off = base, size, 0

            def reset(self):
                self.off = 0

            def alloc(self, shape, dt):
                n = 1
                for s_ in shape:
                    n *= s_
                sz = (n * (4 if dt == F32 else 2) + 3) // 4 * 4
                assert self.off + sz <= self.size, ("region overflow", self.off, sz, self.size)
                v = view(self.base + self.off, shape, dt)
                self.off += sz
                return v

        o = 0
        R_P = Bump(o, 4096); o += 4096
        R_A = Bump(o, 65536); o += 65536
        R_W = Bump(o, 32768); o += 32768
        R_Y = Bump(o, 32768); o += 32768
        R_S = Bump(o, ARENA_BYTES - o)
        assert R_S.size >= 73000, R_S.size

        psb = [st.enter_context(nc.psum_tensor(f"psb{i}", [128, 512], F32)) for i in range(8)]
        Bps = [Buf(f"ps{i}", excl=True) for i in range(8)]
        ps3b = psb[3][:].bitcast(BF16)
        Bps3 = [Bps[3], Bps[3]]

        identf = R_P.alloc([128], F32)
        identb = R_P.alloc([128], BF16)
        epsA = R_P.alloc([1], F32)
        epsB = R_P.alloc([1], F32)
        lamq = R_P.alloc([256], F32)
        lprod = R_P.alloc([2, 64], F32)
        ldots = R_P.alloc([2], F32)
        lex = R_P.alloc([2], F32)
        neglam = R_P.alloc([1], F32)
        subl = R_P.alloc([128], F32)
        ss1 = R_P.alloc([NT], F32)
        rs1 = R_P.alloc([NT], F32)
        ss2 = R_P.alloc([NQ], F32)
        rs2 = R_P.alloc([NQ], F32)
        ss3 = R_P.alloc([NQ], F32)
        rs3 = R_P.alloc([NQ], F32)
        Bconst = Buf("const")
        Bss1 = [Buf(f"ss1_{t}") for t in range(NT)]
        Bss2 = [Buf(f"ss2_{t}") for t in range(NQ)]
        Bss3 = [Buf(f"ss3_{t}") for t in range(NQ)]

        wslot = [view(R_W.base + i * 16384, [16, 512], BF16) for i in range(2)]
        Bwh = [[Buf("w0a"), Buf("w0b")], [Buf("w1a"), Buf("w1b")]]
        sl_w = [[S.slot("ldw0a"), S.slot("ldw0b")], [S.slot("ldw1a"), S.slot("ldw1b")]]
        wctr = [0]
        wsplit = [None, None]

        def load_unit(src_ap, nchunks=16, split=None):
            i = wctr[0] % 2
            wctr[0] += 1
            same = (wsplit[i] == (split, nchunks))
            wsplit[i] = (split, nchunks)
            wr = (lambda h: [Bwh[i][h]]) if same else (lambda h: [Bwh[i][0], Bwh[i][1]])
            if split == "col":
                for h in range(2):
                    S.dma("pool", sl_w[i][h], wslot[i][:, 0:nchunks, h * 256:(h + 1) * 256], src_ap[:, :, h * 256:(h + 1) * 256],
                          writes=wr(h))
                if not same:
                    Bwh[i][0].w = Tok(sl_w[i][0]["sem"], sl_w[i][0]["cnt"], "dma")
            elif split == "row":
                hc = nchunks // 2
                S.dma("pool", sl_w[i][0], wslot[i][:, 0:hc, :], src_ap[:, 0:hc, :], writes=wr(0))
                S.dma("pool", sl_w[i][1], wslot[i][:, hc:nchunks, :], src_ap[:, hc:nchunks, :], writes=wr(1))
                if not same:
                    Bwh[i][0].w = Tok(sl_w[i][0]["sem"], sl_w[i][0]["cnt"], "dma")
            else:
                tkw = S.dma("pool", sl_w[i][0], wslot[i][:, 0:nchunks, :], src_ap, writes=[Bwh[i][0], Bwh[i][1]])
            return i

        sl_c = S.slot("ldc")
        sl_g = S.slot("ldg")
        sl_t = S.slot("ldt")
        sl_x1 = S.slot("ldxres")
        sl_x = [S.slot("ldx0"), S.slot("ldx1")]
        sl_o = S.slot("sto")

        def finish_early():
            S.barrier()
            tkk = S.dma("sp", sl_o, out_t[0], view(R_A.base, [D], F32))
            S.wait_tok("sp", tkk)
            return nc

        S.dma("sp", sl_c, identf, ident_d[:, :], writes=[Bconst])
        S.dma("sp", sl_c, lamq, lamq_d[:, :], writes=[Bconst])
        tk = S.dma("sp", sl_c, subl, subl_d[:, :], writes=[Bconst])
        S.op("dve", lambda e: e.tensor_copy(identb, identf), reads=[Bconst], writes=[Bconst])
        S.op("dve", lambda e: e.memset(epsA, 1e-6), writes=[Bconst])
        S.op("dve", lambda e: e.memset(epsB, 1e-5), writes=[Bconst])
        for tl in (ss1, ss2, ss3):
            S.op("dve", lambda e: e.memset(tl, 0.0), writes=[Bconst])
        lqv = lamq.rearrange("p (a b d) -> p a b d", a=2, b=2, d=64)
        S.op("dve", lambda e: e.tensor_tensor(lprod, lqv[:, :, 0, :], lqv[:, :, 1, :], ALU.mult), reads=[Bconst], writes=[Bconst])
        S.op("dve", lambda e: e.reduce_sum(ldots, lprod, axis=AX.X), reads=[Bconst], writes=[Bconst])
        S.op("act", lambda e: e.activation(lex, ldots, AF.Exp), reads=[Bconst], writes=[Bconst])
        S.op("dve", lambda e: e.tensor_tensor(neglam, lex[:, 1:2], lex[:, 0:1], ALU.subtract), reads=[Bconst], writes=[Bconst])
        S.op("dve", lambda e: e.tensor_scalar(neglam, neglam, -LAM_INIT, None, ALU.add), reads=[Bconst], writes=[Bconst])

        def norm_head(t, src, Bsrc, gfull, Bg, ss, rs, Bss, xs, Bxs, junk, Bjunk):
            S.op("act", lambda e: e.activation(junk, src, AF.Square, accum_out=ss[:, t:t + 1]),
                 reads=[Bsrc, Bconst], writes=[Bjunk, Bss])
            S.op("act", lambda e: e.activation(rs[:, t:t + 1], ss[:, t:t + 1], AF.Sqrt, scale=1.0 / D, bias=epsA[:, 0:1]),
                 reads=[Bss, Bconst], writes=[Bss])
            S.op("dve", lambda e: e.reciprocal(rs[:, t:t + 1], rs[:, t:t + 1]), reads=[Bss], writes=[Bss])
            S.op("dve", lambda e: e.scalar_tensor_tensor(xs, src, rs[:, t:t + 1], gfull, ALU.mult, ALU.mult),
                 reads=[Bsrc, Bss, Bg], writes=[Bxs])

        def norm_tail(t, xs, Bxs, dstT, BdstT, par):
            for hb in range(2):
                bk = 2 * par + hb
                pv = psb[bk][:].bitcast(BF16)
                for c8 in range(8):
                    c = hb * 8 + c8
                    S.op("pe", lambda e: e.transpose(pv[:, c8 * 128:(c8 + 1) * 128], xs[:, c * 128:(c + 1) * 128], identb),
                         reads=[Bxs, Bconst], writes=[Bps[bk]], sig=(c8 == 7))
                dst = dstT[:, hb * 8:hb * 8 + 8, t * 128:(t + 1) * 128]
                srcv = pv.rearrange("p (a b) -> p a b", b=128)
                if hb == 0:
                    S.op("act", lambda e: e.copy(dst, srcv), reads=[Bps[bk]], writes=[BdstT])
                else:
                    S.op("dve", lambda e: e.tensor_copy(dst, srcv), reads=[Bps[bk]], writes=[BdstT])

        hT = R_A.alloc([16, SEQ], BF16)
        BhT = [Buf(f"hT{t}") for t in range(NT)]
        R_S.reset()
        NXB = 4
        xin = [R_S.alloc([D], F32) for _ in range(NXB)]
        xs_b = [R_S.alloc([D], BF16) for _ in range(2)]
        gfull = R_S.alloc([D], F32)
        junk = R_S.alloc([D], BF16)
        Bxin = [Buf(f"xin{i}") for i in range(NXB)]
        sl_xp = [S.slot(f"ldxp{i}") for i in range(NXB)]
        Bxs = [Buf("xs0"), Buf("xs1")]
        Bg = Buf("gfull")
        Bjunk = Buf("junk")
        S.dma("sp", sl_g, gfull, g1_d[:, :], writes=[Bg])
        GROUPS = [("A", 0), ("A", 1), ("B", 0), ("B", 1)]

        def in_units(G):
            typ, gi = GROUPS[G]
            base = 0 if typ == "A" else 6
            return dict(q=base + gi, k=base + 2 + gi, v=base + 4 + gi)

        def in_unit_ap(n):
            return w_in_v[:, :, n * 512:(n + 1) * 512]

        pending_units = {}
        pending_units[(0, "k")] = load_unit(in_unit_ap(in_units(0)["k"]))
        pending_units[(0, "v")] = load_unit(in_unit_ap(in_units(0)["v"]))

        for t in range(NT + 1):
            if t < NT:
                S.dma("sp", sl_xp[t % NXB], xin[t % NXB], xb_t[t], writes=[Bxin[t % NXB]])
                norm_head(t, xin[t % NXB], Bxin[t % NXB], gfull, Bg, ss1, rs1, Bss1[t], xs_b[t % 2], Bxs[t % 2], junk, Bjunk)
            if t >= 1:
                norm_tail(t - 1, xs_b[(t - 1) % 2], Bxs[(t - 1) % 2], hT, BhT[t - 1], (t - 1) % 2)
        S.barrier()
        if STOP == 1:
            return finish_early()

        R_S.reset()
        QT = R_S.alloc([2, 4, NQ * 128], BF16)
        KT = R_S.alloc([4, SEQ], BF16)
        Vaug = R_S.alloc([NT, 4, 130], BF16)
        mown = R_S.alloc([1920], BF16)
        moth = R_S.alloc([1920], BF16)
        cosA = R_S.alloc([NT, 16], F32)
        sinA = R_S.alloc([NT, 16], F32)
        cosB = R_S.alloc([NT, 8], F32)
        sinB = R_S.alloc([NT, 8], F32)
        tm = [R_S.alloc([512], BF16) for _ in range(2)]
        NPT = 4
        pt = [R_S.alloc([512], BF16) for _ in range(NPT)]
        rt1 = R_S.alloc([128], F32)
        rt2 = R_S.alloc([128], F32)
        rsrc = R_S.alloc([128], F32)
        rden = R_S.alloc([4], F32)
        o1n = R_S.alloc([4, 128], F32)
        yb = R_S.alloc([4, 128], F32)
        ssq = R_S.alloc([4], F32)
        ytm = [R_S.alloc([4, 128], BF16) for _ in range(2)]
        yT = R_Y.alloc([16, NQ * 128], BF16)
        BQT = [Buf(f"QT{t}") for t in range(NQ)]
        BKT = [Buf(f"KT{t}") for t in range(NT)]
        BV = [Buf(f"V{t}") for t in range(NT)]
        Btab = Buf("tables")
        Btm = [Buf("tm0"), Buf("tm1")]
        Bpt = [Buf(f"pt{i}") for i in range(NPT)]
        Brt = Buf("rt")
        Brs = Buf("rsrc")
        Bep = Buf("ep")
        Bo1n = Buf("o1n")
        Bytm = [Buf("ytm0"), Buf("ytm1")]
        ByT = [Buf(f"yT{q}") for q in range(2)]
        S.dma("sp", sl_t, mown, mown_d[:, :], writes=[Btab])
        S.dma("sp", sl_t, moth, moth_d[:, :], writes=[Btab])
        S.dma("sp", sl_t, cosA, cosA_d[:, :, :], writes=[Btab])
        S.dma("sp", sl_t, sinA, sinA_d[:, :, :], writes=[Btab])
        S.dma("sp", sl_t, cosB, cosB_d[:, :, :], writes=[Btab])
        S.dma("sp", sl_t, sinB, sinB_d[:, :, :], writes=[Btab])
        S.op("dve", lambda e: e.memset(QT, 0.0), writes=BQT)
        S.op("dve", lambda e: e.memset(Vaug[:, :, :, 128:130], 1.0), writes=BV)

        if STOP == 10:
            return finish_early()
        pbank = [0]

        def next_pbank():
            b = (0, 1, 2, 4)[pbank[0] % 4]
            pbank[0] += 1
            return b

        trh = [0]

        def proj(bk, t, wi):
            for c in range(16):
                S.op("pe", lambda e: e.matmul(psb[bk][:], hT[:, c, t * 128:(t + 1) * 128], wslot[wi][:, c, :],
                                              start=(c == 0), stop=(c == 15)),
                     reads=[BhT[t]] + Bwh[wi], writes=[Bps[bk]], sig=(c == 15))

        def rope_evac(bk, t, typ, dst, Bdst):
            if typ == "A":
                nh, hd, r2, ct, stb = 4, 128, 16, cosA, sinA
            else:
                nh, hd, r2, ct, stb = 8, 64, 8, cosB, sinB
            src = psb[bk][:].rearrange("p (h d) -> p h d", d=hd)
            dv = dst.rearrange("p (h d) -> p h d", d=hd)
            rs_ = rsrc[:, 0:nh * 2 * r2].rearrange("p (h d) -> p h d", d=2 * r2)
            t1 = rt1[:, 0:nh * 2 * r2].rearrange("p (h d) -> p h d", d=2 * r2)
            t2 = rt2[:, 0:nh * 2 * r2].rearrange("p (h d) -> p h d", d=2 * r2)
            cb = bcast(ct[:, t, :], nh, 1)
            sb_ = bcast(stb[:, t, :], nh, 1)
            S.op("act", lambda e: e.copy(dv[:, :, 2 * r2:hd], src[:, :, 2 * r2:hd]), reads=[Bps[bk]], writes=[Bdst])
            S.op("act", lambda e: e.copy(rs_, src[:, :, 0:2 * r2]), reads=[Bps[bk]], writes=[Brs])
            if STOP == 16:
                return
            S.op("dve", lambda e: e.tensor_tensor(t1[:, :, 0:r2], rs_[:, :, 0:r2], cb, ALU.mult), reads=[Brs, Btab], writes=[Brt])
            if STOP == 17:
                return
            S.op("dve", lambda e: e.tensor_tensor(t1[:, :, r2:2 * r2], rs_[:, :, r2:2 * r2], cb, ALU.mult), reads=[Brs, Btab], writes=[Brt])
            S.op("dve", lambda e: e.tensor_tensor(t2[:, :, 0:r2], rs_[:, :, r2:2 * r2], sb_, ALU.mult), reads=[Brs, Btab], writes=[Brt])
            S.op("dve", lambda e: e.tensor_tensor(t2[:, :, r2:2 * r2], rs_[:, :, 0:r2], sb_, ALU.mult), reads=[Brs, Btab], writes=[Brt])
            S.op("dve", lambda e: e.tensor_tensor(dv[:, :, 0:r2], t1[:, :, 0:r2], t2[:, :, 0:r2], ALU.subtract), reads=[Brt], writes=[Bdst])
            S.op("dve", lambda e: e.tensor_tensor(dv[:, :, r2:2 * r2], t1[:, :, r2:2 * r2], t2[:, :, r2:2 * r2], ALU.add), reads=[Brt], writes=[Bdst])

        psbf = [psb[i][:].bitcast(BF16) for i in range(8)]

        def transpose4(srcap, Bsrc, bank=3):
            for h in range(4):
                S.op("pe", lambda e: e.transpose(psbf[bank][:, h * 128:(h + 1) * 128],
                                                 srcap[:, h * 128:(h + 1) * 128], identb),
                     reads=[Bsrc, Bconst], writes=[Bps[bank]], sig=(h == 3))
            return psbf[bank][:, 0:512]

        scaleA = 128.0 ** -0.5
        scaleB = 64.0 ** -0.5
        oset_ctr = [0]
        sb_ctr = [0]

        for G in range(4):
            typ, gi = GROUPS[G]
            U = in_units(G)
            if G == 2:
                S.op("dve", lambda e: e.memset(QT, 0.0), writes=BQT)
            wi_k = pending_units.pop((G, "k"))
            wi_v = pending_units.pop((G, "v"))
            def k_tail(t):
                tmi = t % 2
                pv = transpose4(tm[tmi], Btm[tmi], 3)
                S.op("act", lambda e: e.copy(KT[:, :, t * 128:(t + 1) * 128], pv.rearrange("p (h k) -> p h k", k=128)),
                     reads=[Bps[3]], writes=[BKT[t]])

            for t in range(NT + 1):
                if t < NT:
                    bk = next_pbank()
                    proj(bk, t, wi_k)
                    rope_evac(bk, t, typ, tm[t % 2], Btm[t % 2])
                if t >= 1:
                    k_tail(t - 1)
            if STOP in (11, 14, 15, 16, 17):
                return finish_early()
            wi_q = load_unit(in_unit_ap(U["q"]))
            for t in range(NT):
                bk = next_pbank()
                proj(bk, t, wi_v)
                S.op("act", lambda e: e.copy(Vaug[:, t, :, 0:128], psb[bk][:].rearrange("p (h d) -> p h d", d=128)),
                     reads=[Bps[bk]], writes=[BV[t]])
            if G + 1 < 4:
                pending_units[(G + 1, "k")] = load_unit(in_unit_ap(in_units(G + 1)["k"]))
            def q_tail(t):
                tmi = t % 2
                pvf = transpose4(tm[tmi], Btm[tmi], 3)
                pv = pvf.rearrange("p (h k) -> p h k", k=128)
                if typ == "A":
                    S.op("act", lambda e: e.copy(QT[:, 0, :, t * 128:(t + 1) * 128], pv), reads=[Bps[3]], writes=[BQT[t]])
                else:
                    S.op("act", lambda e: e.copy(QT[0:64, 0, :, t * 128:(t + 1) * 128], pv[0:64]), reads=[Bps[3]], writes=[BQT[t]])
                    S.op("act", lambda e: e.copy(QT[64:128, 1, :, t * 128:(t + 1) * 128], pv[64:128]), reads=[Bps[3]], writes=[BQT[t]])

            for t in range(NQ + 1):
                if t < NQ:
                    bk = next_pbank()
                    proj(bk, t, wi_q)
                    rope_evac(bk, t, typ, tm[t % 2], Btm[t % 2])
                if t >= 1:
                    q_tail(t - 1)
            if G + 1 < 4:
                pending_units[(G + 1, "v")] = load_unit(in_unit_ap(in_units(G + 1)["v"]))
            else:
                pending_units["o0"] = load_unit(w_out_v[:, :, 0:512])
                pending_units["o1"] = load_unit(w_out_v[:, :, 512:1024])

            if STOP == 13:
                return finish_early()
            maps = [0] if typ == "A" else [0, 1]
            its = [(hh, qb, m, kt) for hh in range(4) for qb in range(2) for m in maps for kt in range(NT)]
            scale = scaleA if typ == "A" else scaleB
            state = {}

            def issue_S(i):
                hh, qb, m, kt = its[i]
                sbk = sb_ctr[0] % 4
                sb_ctr[0] += 1
                pi = i % NPT
                S.op("pe", lambda e: e.matmul(psb[sbk][:], KT[:, hh, kt * 128:(kt + 1) * 128],
                                              QT[:, m, hh, qb * 512:(qb + 1) * 512], start=True, stop=True),
                     reads=[BKT[kt]] + BQT[qb * 4:qb * 4 + 4], writes=[Bps[sbk]], sig=True)
                S.op("act", lambda e: e.activation(pt[pi], psb[sbk][:], AF.Exp, scale=scale), reads=[Bps[sbk]], writes=[Bpt[pi]])
                if typ == "A":
                    u = qb * 512 - 128 * kt
                    if kt < 8:
                        msk = mown[:, u + 896:u + 896 + 512]
                    else:
                        msk = moth[:, u + 1920:u + 1920 + 512]
                    S.op("dve", lambda e: e.tensor_tensor(pt[pi], pt[pi], msk, ALU.mult), reads=[Bpt[pi], Btab], writes=[Bpt[pi]])

            def issue_PV(i):
                hh, qb, m, kt = its[i]
                state["i"] = i
                pi = i % NPT
                if kt == 0:
                    state["os"] = oset_ctr[0] % 2
                    oset_ctr[0] += 1
                os_ = state["os"]
                banks = (4 + 2 * os_, 5 + 2 * os_)
                for j in range(4):
                    bk = banks[j // 2]
                    col = (j % 2) * 130
                    S.op("pe", lambda e: e.matmul(psb[bk][:, col:col + 129], pt[pi][:, j * 128:(j + 1) * 128],
                                                  Vaug[:, kt, hh, 0:129], start=(kt == 0 and j % 2 == 0),
                                                  stop=(kt == NT - 1 and j % 2 == 1)),
                         reads=[Bpt[pi], BV[kt]], writes=[Bps[bk]], sig=(j == 3))
                if kt == NT - 1:
                    epilogue(hh, qb, m, banks)

            def epilogue(hh, qb, m, banks):
                head = (0 if typ == "A" else 8) + gi * 4 + hh
                ov = [psb[b][:, 0:260].rearrange("p (j c) -> p j c", c=130) for b in banks]
                for bi in range(2):
                    S.op("dve", lambda e: e.reciprocal(rden[:, 2 * bi:2 * bi + 2].rearrange("p (j o) -> p j o", o=1), ov[bi][:, :, 128:129]),
                         reads=[Bps[banks[bi]]], writes=[Bep])
                yi = oset_ctr[0] % 2
                if typ == "A":
                    for bi in range(2):
                        S.op("dve", lambda e: e.tensor_tensor(ytm[yi][:, 2 * bi:2 * bi + 2, :], ov[bi][:, :, 0:128],
                                                              bcast(rden[:, 2 * bi:2 * bi + 2], 128, 2), ALU.mult),
                             reads=[Bps[banks[bi]], Bep], writes=[Bytm[yi]])
                elif m == 0:
                    for bi in range(2):
                        S.op("dve", lambda e: e.tensor_tensor(o1n[:, 2 * bi:2 * bi + 2, :], ov[bi][:, :, 0:128],
                                                              bcast(rden[:, 2 * bi:2 * bi + 2], 128, 2), ALU.mult),
                             reads=[Bps[banks[bi]], Bep], writes=[Bo1n])
                    return
                else:
                    for bi in range(2):
                        S.op("dve", lambda e: e.tensor_tensor(yb[:, 2 * bi:2 * bi + 2, :], ov[bi][:, :, 0:128],
                                                              bcast(rden[:, 2 * bi:2 * bi + 2], 128, 2), ALU.mult),
                             reads=[Bps[banks[bi]], Bep], writes=[Bep])
                    S.op("dve", lambda e: e.scalar_tensor_tensor(yb, yb, neglam[:, 0:1], o1n, ALU.mult, ALU.add),
                         reads=[Bep, Bo1n, Bconst], writes=[Bep])
                    S.op("dve", lambda e: e.tensor_tensor(o1n, yb, yb, ALU.mult), reads=[Bep, Bo1n], writes=[Bo1n])
                    S.op("dve", lambda e: e.reduce_sum(ssq, o1n, axis=AX.X), reads=[Bo1n], writes=[Bep])

                def part2b():
                    tb_ = sb_ctr[0] % 4
                    sb_ctr[0] += 1
                    pvf = transpose4(ytm[yi].rearrange("p j d -> p (j d)"), Bytm[yi], tb_)
                    S.op("dve", lambda e: e.tensor_copy(yT[:, head, qb * 512:(qb + 1) * 512], pvf),
                         reads=[Bps[tb_]], writes=[ByT[qb]])

                def part2():
                    if typ == "B":
                        S.op("act", lambda e: e.activation(ssq, ssq, AF.Ln, scale=1.0 / 128.0, bias=epsB[:, 0:1]), reads=[Bep, Bconst], writes=[Bep])
                        S.op("act", lambda e: e.activation(ssq, ssq, AF.Exp, scale=-0.5), reads=[Bep], writes=[Bep])
                        S.op("dve", lambda e: e.tensor_scalar(ssq, ssq, 1.0 - LAM_INIT, None, ALU.mult), reads=[Bep], writes=[Bep])
                        S.op("dve", lambda e: e.tensor_tensor(yb, yb, bcast(ssq, 128, 2), ALU.mult), reads=[Bep], writes=[Bep])
                        S.op("dve", lambda e: e.tensor_tensor(ytm[yi], yb, bcast(subl, 4, 1), ALU.mult), reads=[Bep, Bconst], writes=[Bytm[yi]])
                        deferred.append((state["i"] + DEL, part2b))
                    else:
                        part2b()

                deferred.append((state["i"] + DEL, part2))

            DEL = 8
            deferred = []
            LA = 3
            for i in range(len(its) + LA):
                if i < len(its):
                    issue_S(i)
                if i - LA >= 0:
                    issue_PV(i - LA)
                    due = [d for d in deferred if d[0] <= i - LA]
                    for d in due:
                        deferred.remove(d)
                        d[1]()
            while deferred:
                deferred.pop(0)[1]()
            if STOP == 20 + G:
                return finish_early()
        S.barrier()
        if STOP == 2:
            return finish_early()

        x1 = view(R_A.base, [NQ, D], F32)
        Bx1 = [Buf(f"x1_{t}") for t in range(NQ)]
        for t in range(NQ):
            tkx = S.dma("sp", sl_x1, x1[:, t, :], xb_t[t], writes=[Bx1[t]])
        for t in range(NQ):
            Bx1[t].w = tkx
        obank = [0]
        for n in range(4):
            wi = pending_units.pop(f"o{n}")
            for t in range(NQ):
                bk = obank[0] % 8
                obank[0] += 1
                for c in range(16):
                    S.op("pe", lambda e: e.matmul(psb[bk][:], yT[:, c, t * 128:(t + 1) * 128], wslot[wi][:, c, :],
                                                  start=(c == 0), stop=(c == 15)),
                         reads=[ByT[t // 4]] + Bwh[wi], writes=[Bps[bk]], sig=(c == 15))
                S.op("dve", lambda e: e.tensor_tensor(x1[:, t, n * 512:(n + 1) * 512], x1[:, t, n * 512:(n + 1) * 512], psb[bk][:], ALU.add),
                     reads=[Bps[bk], Bx1[t]], writes=[Bx1[t]])
            if n + 2 < 4:
                pending_units[f"o{n + 2}"] = load_unit(w_out_v[:, :, (n + 2) * 512:(n + 3) * 512])
            elif n == 2:
                pending_units[("g", 0, 0)] = load_unit(w_gate_v[:, :, 0:512], split="col")
            else:
                pending_units[("u", 0, 0)] = load_unit(w_up_v[:, :, 0:512], split="col")
        S.barrier()
        if STOP == 3:
            return finish_early()

        R_S.reset()
        h2T = R_Y.base
        h2T = view(R_Y.base, [16, NQ * 128], BF16)
        Bh2T = [Buf(f"h2T{t}") for t in range(NQ)]
        aT = R_S.alloc([NF, 512], BF16)
        sg = [R_S.alloc([512], F32) for _ in range(4)]
        gfull = R_S.alloc([D], F32)
        junk = R_S.alloc([D], BF16)
        xs_b = [R_S.alloc([D], BF16) for _ in range(2)]
        Bg = Buf("gfull2")
        Bjunk = Buf("junk2")
        Bxs = [Buf("xs2_0"), Buf("xs2_1")]
        BaT = [Buf(f"aT{f}") for f in range(NF)]
        Bsg = [Buf(f"sg{i}") for i in range(4)]
        S.dma("sp", sl_g, gfull, g2_d[:, :], writes=[Bg])
        for t in range(NQ + 1):
            if t < NQ:
                norm_head(t, x1[:, t, :], Bx1[t], gfull, Bg, ss2, rs2, Bss2[t], xs_b[t % 2], Bxs[t % 2], junk, Bjunk)
            if t >= 1:
                norm_tail(t - 1, xs_b[(t - 1) % 2], Bxs[(t - 1) % 2], h2T, Bh2T[t - 1], (t - 1) % 2)
        S.barrier(engines=("sp",))
        Bg3 = Buf("gfull3")
        S.dma("sp", sl_g, gfull, g3_d[:, :], writes=[Bg3])

        for blk in range(2):
            tb = slice(blk * 512, (blk + 1) * 512)
            for uu in range(11):
                wg = pending_units.pop(("g", blk, uu))
                wu = pending_units.pop(("u", blk, uu))
                for i in range(4):
                    f = uu * 4 + i
                    bk = f % 2
                    for c in range(16):
                        S.op("pe", lambda e: e.matmul(psb[bk][:], wslot[wg][:, c, i * 128:(i + 1) * 128], h2T[:, c, tb],
                                                      start=(c == 0), stop=(c == 15)),
                             reads=[Bwh[wg][i // 2]] + Bh2T[blk * 4:blk * 4 + 4], writes=[Bps[bk]], sig=(c == 15))
                    S.op("act", lambda e: e.activation(sg[i], psb[bk][:], AF.Silu), reads=[Bps[bk]], writes=[Bsg[i]])
                if uu + 1 < 11:
                    pending_units[("g", blk, uu + 1)] = load_unit(w_gate_v[:, :, (uu + 1) * 512:(uu + 2) * 512], split="col")
                else:
                    pending_units[("d", blk, 0)] = load_unit(w_down_v[:, 0:16, 0:512], split="row")
                for i in range(4):
                    f = uu * 4 + i
                    bk = 2 + f % 2
                    for c in range(16):
                        S.op("pe", lambda e: e.matmul(psb[bk][:], wslot[wu][:, c, i * 128:(i + 1) * 128], h2T[:, c, tb],
                                                      start=(c == 0), stop=(c == 15)),
                             reads=[Bwh[wu][i // 2]] + Bh2T[blk * 4:blk * 4 + 4], writes=[Bps[bk]], sig=(c == 15))
                    S.op("dve", lambda e: e.tensor_tensor(aT[:, f, :], sg[i], psb[bk][:], ALU.mult),
                         reads=[Bps[bk], Bsg[i]], writes=[BaT[f]])
                if uu + 1 < 11:
                    pending_units[("u", blk, uu + 1)] = load_unit(w_up_v[:, :, (uu + 1) * 512:(uu + 2) * 512], split="col")
                else:
                    pending_units[("d", blk, 1)] = load_unit(w_down_v[:, 16:32, 0:512], split="row")
            dunits = [(n, fu) for n in range(4) for fu in range(3)]
            frange = [(0, 16), (16, 32), (32, 44)]
            for di, (n, fu) in enumerate(dunits):
                wi = pending_units.pop(("d", blk, di))
                f0, f1 = frange[fu]
                hc = (f1 - f0) // 2
                halves = [(f0, f0 + hc), (f0 + hc, f1)]
                boff = 4 if n % 2 == 0 else 0
                last_unit = (di == len(dunits) - 1)

                def down_mm(j, hsel, h0, h1):
                    bk = boff + j
                    for f in range(h0, h1):
                        S.op("pe", lambda e: e.matmul(psb[bk][:], aT[:, f, j * 128:(j + 1) * 128], wslot[wi][:, f - f0, :],
                                                      start=(f == 0), stop=(f == NF - 1)),
                             reads=[BaT[f], Bwh[wi][hsel]], writes=[Bps[bk]], sig=(f == h1 - 1))
                    if fu == 2 and hsel == 1:
                        T = blk * 4 + j
                        S.op("dve", lambda e: e.tensor_tensor(x1[:, T, n * 512:(n + 1) * 512], x1[:, T, n * 512:(n + 1) * 512], psb[bk][:], ALU.add),
                             reads=[Bps[bk], Bx1[T]], writes=[Bx1[T]])

                if last_unit:
                    for j in range(4):
                        for hsel, (h0, h1) in enumerate(halves):
                            down_mm(j, hsel, h0, h1)
                else:
                    for hsel, (h0, h1) in enumerate(halves):
                        for j in range(4):
                            down_mm(j, hsel, h0, h1)
                nx = di + 2
                if nx < len(dunits):
                    n2, fu2 = dunits[nx]
                    a0, a1 = frange[fu2]
                    pending_units[("d", blk, nx)] = load_unit(w_down_v[:, a0:a1, n2 * 512:(n2 + 1) * 512], nchunks=a1 - a0, split="row")
                elif blk == 0:
                    if nx == len(dunits):
                        pending_units[("g", 1, 0)] = load_unit(w_gate_v[:, :, 0:512], split="col")
                    else:
                        pending_units[("u", 1, 0)] = load_unit(w_up_v[:, :, 0:512], split="col")
            for j in range(4):
                T = blk * 4 + j
                src = x1[:, T, :]
                S.op("act", lambda e: e.activation(junk, src, AF.Square, accum_out=ss3[:, T:T + 1]),
                     reads=[Bx1[T], Bconst], writes=[Bjunk, Bss3[T]])
                S.op("act", lambda e: e.activation(rs3[:, T:T + 1], ss3[:, T:T + 1], AF.Sqrt, scale=1.0 / D, bias=epsA[:, 0:1]),
                     reads=[Bss3[T], Bconst], writes=[Bss3[T]])
                S.op("dve", lambda e: e.reciprocal(rs3[:, T:T + 1], rs3[:, T:T + 1]), reads=[Bss3[T]], writes=[Bss3[T]])
                S.op("dve", lambda e: e.scalar_tensor_tensor(src, src, rs3[:, T:T + 1], gfull, ALU.mult, ALU.mult),
                     reads=[Bx1[T], Bss3[T], Bg3], writes=[Bx1[T]])
                last = S.dma("sp", sl_o, out_t[T], src, reads=[Bx1[T]])
        S.wait_tok("sp", last)
        assert not pending_units, pending_units
    return nc


def _mult(d):
    d = np.asarray(d)
    m = (np.abs(d) <= 64).astype(np.float32)
    m += ((d % 4 == 0) & (np.abs(d) <= 256)).astype(np.float32)
    m += ((d % 16 == 0) & (np.abs(d) <= 1024)).astype(np.float32)
    return m


def _rope_tab(pos, rot_dim):
    inv = (np.float32(ROPE_THETA) ** (-np.arange(0, rot_dim, 2, dtype=np.float32) / np.float32(rot_dim))).astype(np.float32)
    ang = pos.astype(np.float32)[:, None] * inv[None, :]
    return np.cos(ang).astype(np.float32), np.sin(ang).astype(np.float32)


_NC_CACHE = {}


def kernel(x, norm_attn, w_in, lambda_qk, subln, w_out, norm_ffn, w_gate, w_up, w_down, norm_final):
    x = np.asarray(x, dtype=np.float32)
    f32c = lambda a: np.ascontiguousarray(np.asarray(a, dtype=np.float32))
    rep = lambda v: np.ascontiguousarray(np.broadcast_to(np.asarray(v, dtype=np.float32).reshape(1, -1), (128, np.asarray(v).size)))
    if "nc" not in _NC_CACHE:
        _NC_CACHE["nc"] = build_program()
    nc = _NC_CACHE["nc"]
    shared = dict(
        g1=rep(norm_attn[0]), g2=rep(norm_ffn[0]), g3=rep(norm_final),
        w_in=f32c(w_in[0]), w_out=f32c(w_out[0]), w_gate=f32c(w_gate[0]), w_up=f32c(w_up[0]), w_down=f32c(w_down[0]),
        lamq=rep(np.asarray(lambda_qk[0]).reshape(-1)), subl=rep(subln[0]),
        ident=np.eye(128, dtype=np.float32),
    )
    p = np.arange(128)[:, None]
    c = np.arange(1920)[None, :]
    mown = _mult(p - (c - 896)).astype(ml_dtypes.bfloat16)
    in_maps = []
    for core in range(8):
        b, hf = core // 2, core % 2
        own = np.arange(hf * 1024, (hf + 1) * 1024)
        oth = np.arange((1 - hf) * 1024, (2 - hf) * 1024)
        pos = np.concatenate([own, oth])
        xb = np.ascontiguousarray(x[b][pos])
        ca, sa = _rope_tab(pos, 32)
        cb, sb = _rope_tab(pos, 16)
        lay = lambda a: np.ascontiguousarray(a.reshape(NT, 128, -1).transpose(1, 0, 2))
        moth = _mult(p - (c - 1920) - 2048 * hf).astype(ml_dtypes.bfloat16)
        m = dict(shared)
        m.update(xb=xb, cosA=lay(ca), sinA=lay(sa), cosB=lay(cb), sinB=lay(sb), mown=mown, moth=moth)
        in_maps.append(m)
    if _NC_CACHE.get("prep_only"):
        return nc, in_maps
    res = run_bass_kernel_spmd(nc, in_maps, core_ids=list(range(8)))
    out = np.empty((4, SEQ, D), dtype=np.float32)
    for core in range(8):
        b, hf = core // 2, core % 2
        out[b, hf * 1024:(hf + 1) * 1024] = res.results[core]["out"]
    return out
```
